# Optimizing a Trainium2 kernel written in Bass

```python
import math
import jax, jax.numpy as jnp
from jax import lax
import numpy as np

D_MODEL = 1024
BATCH = 16
SEQ = 2048
DEPTH = 1

MEM_LEN = 256
A_HEADS = 8
A_HEAD_DIM = 64
A_WIDTH = A_HEADS * A_HEAD_DIM
MOBA_BLOCK = 256
MOBA_TOPK = 3
Q_BLOCK = 128
G_HEADS = 4
G_WIDTH = D_MODEL - A_WIDTH
G_HEAD_V = G_WIDTH // G_HEADS
G_KEY_WIDTH = G_WIDTH // 2
G_HEAD_K = G_KEY_WIDTH // G_HEADS
G_GATE_RANK = 16
G_GATE_NORM = 16.0
G_CHUNK = 64
MIX_WIDTH = A_WIDTH + G_WIDTH
IN_SPLITS = (A_WIDTH, A_WIDTH, A_WIDTH, G_KEY_WIDTH, G_KEY_WIDTH, G_WIDTH, G_GATE_RANK, G_WIDTH)
IN_COLS = 3 * A_WIDTH + 2 * G_KEY_WIDTH + G_WIDTH + G_GATE_RANK + G_WIDTH
X_HEADS = 4
X_HEAD_DIM = D_MODEL // X_HEADS
D_FF = 4 * D_MODEL
RP_BUCKETS = 32
RP_MAX_DIST = 128
EPS = 1e-6

kernel_name = "hybrid_moba_gla_xattn_block"


def rms_norm(x, g):
    xf = x.astype(jnp.float32)
    y = xf * lax.rsqrt(jnp.mean(xf * xf, axis=-1, keepdims=True) + EPS)
    return (y * g.astype(jnp.float32)).astype(x.dtype)


def rel_bucket(dist):
    max_exact = RP_BUCKETS // 2
    d = jnp.maximum(dist, 0)
    large = max_exact + (jnp.log(jnp.maximum(d, 1).astype(jnp.float32) / max_exact)
                         / math.log(RP_MAX_DIST / max_exact) * (RP_BUCKETS - max_exact)).astype(jnp.int32)
    large = jnp.minimum(large, RP_BUCKETS - 1)
    return jnp.where(d < max_exact, d, large)


def moba_attention(q, k, v, rp_table):
    B, T, H, dh = q.shape
    nb = -(-T // MOBA_BLOCK)
    pad = nb * MOBA_BLOCK - T
    kp = jnp.pad(k, ((0, 0), (0, pad), (0, 0), (0, 0)))
    vp = jnp.pad(v, ((0, 0), (0, pad), (0, 0), (0, 0)))
    kb = kp.reshape(B, nb, MOBA_BLOCK, H, dh).transpose(0, 3, 1, 2, 4)
    vb = vp.reshape(B, nb, MOBA_BLOCK, H, dh).transpose(0, 3, 1, 2, 4)
    nq = T // Q_BLOCK
    qc = q.reshape(B, nq, Q_BLOCK, H, dh).transpose(0, 1, 3, 2, 4).reshape(B * nq, H, Q_BLOCK, dh)
    scale = dh ** -0.5
    n_sel = min(MOBA_TOPK, nb - 1)
    table_t = rp_table.T.astype(jnp.float32)
    key_off = jnp.arange(MOBA_BLOCK, dtype=jnp.int32)
    hidx = jnp.arange(H)[:, None, None]
    b_ids = jnp.repeat(jnp.arange(B, dtype=jnp.int32), nq)
    qb_ids = jnp.tile(jnp.arange(nq, dtype=jnp.int32), B)
    xs = (qc, b_ids, qb_ids)
    if n_sel > 0:
        pos = jnp.arange(T, dtype=jnp.int32)
        q_blk = pos // MOBA_BLOCK
        kmean = jnp.mean(kb.astype(jnp.float32), axis=3)
        gate = jnp.einsum('bthd,bhnd->bhtn', q.astype(jnp.float32), kmean)
        past = jnp.arange(nb)[None, :] < q_blk[:, None]
        gate = jnp.where(past[None, None], gate, -jnp.inf)
        _, sel = lax.top_k(gate, n_sel)
        valid = sel < q_blk[None, None, :, None]
        sel_c = sel.reshape(B, H, nq, Q_BLOCK, n_sel).transpose(0, 2, 1, 3, 4).reshape(B * nq, H, Q_BLOCK, n_sel)
        valid_c = valid.reshape(B, H, nq, Q_BLOCK, n_sel).transpose(0, 2, 1, 3, 4).reshape(B * nq, H, Q_BLOCK, n_sel)
        xs = xs + (sel_c, valid_c)

    def step(args):
        qi, b, qb = args[0], args[1], args[2]
        kb_b = kb[b]
        vb_b = vb[b]
        q_pos = qb * Q_BLOCK + jnp.arange(Q_BLOCK, dtype=jnp.int32)
        own = (qb * Q_BLOCK) // MOBA_BLOCK
        k_own = lax.dynamic_index_in_dim(kb_b, own, axis=1, keepdims=False)
        v_own = lax.dynamic_index_in_dim(vb_b, own, axis=1, keepdims=False)
        d_own = q_pos[:, None] - (own * MOBA_BLOCK + key_off)[None, :]
        s_own = jnp.einsum('hqd,hld->hql', qi, k_own).astype(jnp.float32) * scale + table_t[:, rel_bucket(d_own)]
        s_own = jnp.where(d_own[None] >= 0, s_own, -jnp.inf)
        if n_sel > 0:
            sel_i, valid_i = args[3], args[4]
            k_sel = kb_b[hidx, sel_i]
            v_sel = vb_b[hidx, sel_i]
            d_sel = q_pos[None, :, None, None] - (sel_i[..., None] * MOBA_BLOCK + key_off)
            s_sel = (jnp.einsum('hqd,hqcld->hqcl', qi, k_sel).astype(jnp.float32) * scale
                     + table_t[hidx[..., None], rel_bucket(d_sel)])
            s_sel = jnp.where(valid_i[..., None], s_sel, -jnp.inf)
            s = jnp.concatenate([s_sel.reshape(H, Q_BLOCK, n_sel * MOBA_BLOCK), s_own], axis=-1)
            p = jax.nn.softmax(s, axis=-1).astype(v.dtype)
            p_sel = p[..., :n_sel * MOBA_BLOCK].reshape(H, Q_BLOCK, n_sel, MOBA_BLOCK)
            p_own = p[..., n_sel * MOBA_BLOCK:]
            return jnp.einsum('hqcl,hqcld->hqd', p_sel, v_sel) + jnp.einsum('hql,hld->hqd', p_own, v_own)
        p_own = jax.nn.softmax(s_own, axis=-1).astype(v.dtype)
        return jnp.einsum('hql,hld->hqd', p_own, v_own)

    o = lax.map(step, xs)
    return o.reshape(B, nq, H, Q_BLOCK, dh).transpose(0, 1, 3, 2, 4).reshape(B, T, H * dh)


def gla_chunked(q, k, v, log_a):
    B, T, H, dk = q.shape
    dv = v.shape[-1]
    nc = T // G_CHUNK
    scale = dk ** -0.5

    def to_chunks(t):
        return t.reshape(B, nc, G_CHUNK, H, t.shape[-1]).transpose(1, 0, 3, 2, 4)

    causal = jnp.tril(jnp.ones((G_CHUNK, G_CHUNK), dtype=bool))

    def body(S, inp):
        qi, ki, vi, gi = inp
        qf = qi.astype(jnp.float32) * scale
        kf = ki.astype(jnp.float32)
        vf = vi.astype(jnp.float32)
        b = jnp.cumsum(gi.astype(jnp.float32), axis=2)
        o_inter = jnp.einsum('bhcd,bhde->bhce', qf * jnp.exp(b), S)
        diff = b[:, :, :, None, :] - b[:, :, None, :, :]
        decay = jnp.exp(jnp.where(causal[None, None, :, :, None], diff, -jnp.inf))
        A = jnp.einsum('bhid,bhjd,bhijd->bhij', qf, kf, decay)
        o_intra = jnp.einsum('bhij,bhje->bhie', A, vf)
        b_last = b[:, :, -1:, :]
        S_new = jnp.exp(b_last[:, :, 0, :])[..., None] * S + jnp.einsum('bhcd,bhce->bhde', kf * jnp.exp(b_last - b), vf)
        return S_new, o_inter + o_intra

    S0 = jnp.zeros((B, H, dk, dv), jnp.float32)
    _, o = lax.scan(body, S0, (to_chunks(q), to_chunks(k), to_chunks(v), to_chunks(log_a)))
    return o.transpose(1, 0, 3, 2, 4).reshape(B, T, H, dv).astype(v.dtype)


def setup_inputs(seed: int = 0) -> dict:
    key = jax.random.key(seed)
    ks = jax.random.split(key, 20)

    def nrm(k, shape, scale):
        return jax.random.normal(k, shape, jnp.float32) * scale

    def gain(k, shape):
        return 1.0 + 0.05 * jax.random.normal(k, shape, jnp.float32)

    L = DEPTH
    return {
        "x": nrm(ks[0], (BATCH, SEQ, D_MODEL), 1.0),
        "mem": nrm(ks[1], (BATCH, MEM_LEN, D_MODEL), 1.0),
        "rp_table": nrm(ks[2], (RP_BUCKETS, A_HEADS), 0.5),
        "norm_mix": gain(ks[3], (L, D_MODEL)),
        "w_in": nrm(ks[4], (L, D_MODEL, IN_COLS), D_MODEL ** -0.5),
        "w_gate_up": nrm(ks[5], (L, G_GATE_RANK, G_KEY_WIDTH), G_GATE_RANK ** -0.5),
        "b_gate": nrm(ks[6], (L, G_KEY_WIDTH), 0.1),
        "g_norm": gain(ks[7], (L, G_HEAD_V)),
        "w_out": nrm(ks[8], (L, MIX_WIDTH, D_MODEL), MIX_WIDTH ** -0.5),
        "norm_xattn": gain(ks[9], (L, D_MODEL)),
        "norm_mem": gain(ks[10], (L, D_MODEL)),
        "w_xq": nrm(ks[11], (L, D_MODEL, D_MODEL), D_MODEL ** -0.5),
        "w_xkv": nrm(ks[12], (L, D_MODEL, 2 * D_MODEL), D_MODEL ** -0.5),
        "w_xo": nrm(ks[13], (L, D_MODEL, D_MODEL), D_MODEL ** -0.5),
        "norm_mlp": gain(ks[14], (L, D_MODEL)),
        "w_up": nrm(ks[15], (L, D_MODEL, D_FF), D_MODEL ** -0.5),
        "w_down": nrm(ks[16], (L, D_FF, D_MODEL), D_FF ** -0.5),
        "norm_final": gain(ks[17], (D_MODEL,)),
    }


def reference(x, mem, rp_table, norm_mix, w_in, w_gate_up, b_gate, g_norm, w_out,
              norm_xattn, norm_mem, w_xq, w_xkv, w_xo, norm_mlp, w_up, w_down, norm_final):
    B, T, _ = x.shape
    M = mem.shape[1]
    offsets = [int(o) for o in np.cumsum(IN_SPLITS)[:-1]]
    for l in range(DEPTH):
        h = rms_norm(x, norm_mix[l])
        proj = h @ w_in[l]
        qa, ka, va, qg, kg, vg, glr, rg = jnp.split(proj, offsets, axis=-1)
        o_a = moba_attention(qa.reshape(B, T, A_HEADS, A_HEAD_DIM),
                             ka.reshape(B, T, A_HEADS, A_HEAD_DIM),
                             va.reshape(B, T, A_HEADS, A_HEAD_DIM), rp_table)
        log_a = jax.nn.log_sigmoid((glr @ w_gate_up[l] + b_gate[l]).astype(jnp.float32)) / G_GATE_NORM
        o_g = gla_chunked(qg.reshape(B, T, G_HEADS, G_HEAD_K),
                          kg.reshape(B, T, G_HEADS, G_HEAD_K),
                          vg.reshape(B, T, G_HEADS, G_HEAD_V),
                          log_a.reshape(B, T, G_HEADS, G_HEAD_K))
        o_g = rms_norm(o_g, g_norm[l]).reshape(B, T, G_WIDTH) * jax.nn.silu(rg)
        x = x + jnp.concatenate([o_a, o_g], axis=-1) @ w_out[l]
        h = rms_norm(x, norm_xattn[l])
        m = rms_norm(mem, norm_mem[l])
        qx = (h @ w_xq[l]).reshape(B, T, X_HEADS, X_HEAD_DIM)
        kx, vx = jnp.split(m @ w_xkv[l], 2, axis=-1)
        kx = kx.reshape(B, M, X_HEADS, X_HEAD_DIM)
        vx = vx.reshape(B, M, X_HEADS, X_HEAD_DIM)
        s = jnp.einsum('bthd,bmhd->bhtm', qx, kx).astype(jnp.float32) * (X_HEAD_DIM ** -0.5)
        p = jax.nn.softmax(s, axis=-1).astype(vx.dtype)
        ox = jnp.einsum('bhtm,bmhd->bthd', p, vx).reshape(B, T, D_MODEL)
        x = x + ox @ w_xo[l]
        h = rms_norm(x, norm_mlp[l])
        x = x + jnp.square(jax.nn.relu(h @ w_up[l])) @ w_down[l]
    return rms_norm(x, norm_final)
```

```python
import math
from contextlib import ExitStack

import numpy as np
import concourse.bass as bass
import concourse.mybir as mybir
from concourse.bass_utils import run_bass_kernel_spmd

F32 = mybir.dt.float32
BF16 = mybir.dt.bfloat16
AF = mybir.ActivationFunctionType
ALU = mybir.AluOpType
AXL = mybir.AxisListType

D = 1024
MEM = 256
IN_COLS = 3088
DFF = 4096
EPS = 1e-6
NCORES = 8

ENGS = ['pe', 'act', 'dve', 'pool', 'sp']
PSUM_KEYS = set('b%d' % i for i in range(8))
SAME_ENG_SYNC = {'pe': False, 'act': True, 'dve': True, 'pool': True, 'sp': False}


class Sched:
    def __init__(self, nc, st):
        self.nc = nc
        self.st = st
        self.semobj = {}
        for e in ENGS:
            self.semobj[e] = st.enter_context(nc.semaphore('q_' + e))
        self.cnt = {e: 0 for e in ENGS}
        self.known = {e: {} for e in ENGS}
        self.prog = {e: [] for e in ENGS}
        self.lastw = {}
        self.readers = {}
        self.dmacnt = {}
        self.stopped = False
        self.stop_at = None

    def checkpoint(self, label):
        if self.stop_at is not None and label == self.stop_at:
            self.stopped = True

    def _sem(self, key):
        if key not in self.semobj:
            self.semobj[key] = self.st.enter_context(self.nc.semaphore('d_' + key))
            self.dmacnt[key] = 0
        return self.semobj[key]

    def _deps(self, reads, writes, eng=None):
        deps = {}

        def add(s, v):
            if deps.get(s, 0) < v:
                deps[s] = v
        for r in reads:
            w = self.lastw.get(r)
            if w is not None:
                add(*w)
            if r in PSUM_KEYS:
                for s_, v in self.readers.get(r, {}).items():
                    if s_ != eng:
                        add(s_, v)
        for w_ in writes:
            w = self.lastw.get(w_)
            if w is not None:
                add(*w)
            for s, v in self.readers.get(w_, {}).items():
                add(s, v)
        return deps

    def _waits(self, eng, deps):
        waits = []
        for s, v in deps.items():
            if s == eng and not SAME_ENG_SYNC[eng]:
                continue
            if self.known[eng].get(s, 0) < v:
                waits.append((s, v))
                self.known[eng][s] = v
        return waits

    def _mark(self, reads, writes, s, v):
        for r in reads:
            d = self.readers.setdefault(r, {})
            if d.get(s, 0) < v:
                d[s] = v
        for w in writes:
            self.lastw[w] = (s, v)
            self.readers[w] = {}

    def op(self, eng, fn, reads=(), writes=()):
        if self.stopped:
            return
        deps = self._deps(reads, writes, eng)
        waits = self._waits(eng, deps)
        self.cnt[eng] += 1
        v = self.cnt[eng]
        self.prog[eng].append((waits, fn, eng, 1))
        self._mark(reads, writes, eng, v)

    def dma(self, q, semkey, out, in_, reads=(), writes=()):
        if self.stopped:
            return
        self._sem(semkey)
        deps = self._deps(reads, writes)
        waits = self._waits(q, deps)
        self.dmacnt[semkey] += 16
        v = self.dmacnt[semkey]
        fn = (lambda e, o=out, i=in_: e.dma_start(out=o, in_=i))
        self.prog[q].append((waits, fn, semkey, 16))
        self._mark(reads, writes, semkey, v)

    def wait_all(self, eng, keys):
        deps = self._deps(keys, ())
        waits = self._waits(eng, deps)
        if waits:
            self.prog[eng].append((waits, None, None, 0))

    def barrier(self):
        for e in ENGS:
            deps = {}
            for e2 in ENGS:
                if e2 != e and e2 != 'sp' and self.cnt[e2] > 0:
                    deps[e2] = self.cnt[e2]
            for k, v in self.dmacnt.items():
                if v > 0:
                    deps[k] = v
            waits = self._waits(e, deps)
            if waits:
                self.prog[e].append((waits, None, None, 0))

    def replay(self, name, e):
        for waits, fn, semname, inc in self.prog[name]:
            for s, v in waits:
                e.wait_ge(self.semobj[s], v)
            if fn is None:
                continue
            inst = fn(e)
            inst.then_inc(self.semobj[semname], inc)

    def emit(self):
        nc = self.nc
        with nc.Block() as block:
            @block.tensor
            def _(e):
                self.replay('pe', e)

            @block.scalar
            def _(e):
                self.replay('act', e)

            @block.vector
            def _(e):
                self.replay('dve', e)

            @block.gpsimd
            def _(e):
                self.replay('pool', e)

            @block.sync
            def _(e):
                self.replay('sp', e)

    def matmul(self, out, lhsT, rhs, start, stop, reads, writes):
        self.op('pe', lambda e, a=(out, lhsT, rhs, start, stop): e.matmul(a[0], a[1], a[2], start=a[3], stop=a[4]),
                reads, writes)

    def transpose(self, out, in_, ident, reads, writes):
        self.op('pe', lambda e, a=(out, in_, ident): e.transpose(a[0], a[1], a[2]), reads, writes)

    def act(self, out, in_, func, reads, writes, bias=None, scale=None, accum_out=None):
        kw = {}
        if bias is not None:
            kw['bias'] = bias
        if scale is not None:
            kw['scale'] = scale
        if accum_out is not None:
            kw['accum_out'] = accum_out
        self.op('act', lambda e, a=(out, in_, func), kw=kw: e.activation(a[0], a[1], a[2], **kw), reads, writes)

    def tt(self, eng, out, in0, in1, op, reads, writes):
        self.op(eng, lambda e, a=(out, in0, in1, op): e.tensor_tensor(a[0], a[1], a[2], a[3]), reads, writes)

    def ts(self, eng, out, in0, s1, op0, reads, writes, s2=None, op1=None):
        kw = {}
        if op1 is not None:
            kw['op1'] = op1
        self.op(eng, lambda e, a=(out, in0, s1, s2, op0), kw=kw: e.tensor_scalar(a[0], a[1], a[2], a[3], a[4], **kw),
                reads, writes)

    def stt(self, out, in0, scalar, in1, op0, op1, reads, writes):
        self.op('dve', lambda e, a=(out, in0, scalar, in1, op0, op1):
                e.scalar_tensor_tensor(a[0], a[1], a[2], a[3], a[4], a[5]), reads, writes)

    def copy(self, eng, out, in_, reads, writes):
        if eng == 'act':
            self.op('act', lambda e, a=(out, in_): e.copy(a[0], a[1]), reads, writes)
        else:
            self.op(eng, lambda e, a=(out, in_): e.tensor_copy(a[0], a[1]), reads, writes)

    def memset(self, eng, ap, val, writes):
        self.op(eng, lambda e, a=(ap, val): e.memset(a[0], a[1]), (), writes)


class Arena:
    def __init__(self, tile, ncols):
        self.t = tile
        self.n = ncols
        self.off = 0

    def reset(self):
        self.off = 0

    def f32(self, ncols):
        assert self.off + ncols <= self.n, (self.off, ncols, self.n)
        ap = self.t[:, self.off:self.off + ncols]
        self.off += ncols
        return ap

    def bf16(self, ncols):
        n32 = (ncols + 1) // 2
        assert self.off + n32 <= self.n, (self.off, n32, self.n)
        ap = self.t[:, self.off:self.off + n32].bitcast(BF16)
        self.off += n32
        return ap


def build(nseq=2, T=2048, GC=1024, debug=False, stop_at=None):
    NT = T // 128
    NG = T // 512
    GC = min(GC, T)
    NCG = T // GC
    TC = GC // 128
    nc = bass.Bass("TRN2", target_bir_lowering=False)

    def dt(name, shape, dtype=F32, kind="ExternalInput"):
        return nc.dram_tensor(name, shape, dtype, kind=kind).ap()

    x_d = dt("x", [nseq, T, D])
    mem_d = dt("mem", [nseq, MEM, D])
    w_in_d = dt("w_in", [D, IN_COLS])
    w_gu_d = dt("w_gate_up", [16, 256])
    b_gate_d = dt("b_gate", [1, 256])
    w_out_d = dt("w_out", [D, D])
    w_xq_d = dt("w_xq", [D, D])
    w_xkv_d = dt("w_xkv", [D, 2 * D])
    w_xo_d = dt("w_xo", [D, D])
    w_up_d = dt("w_up", [D, DFF])
    w_down_d = dt("w_down", [DFF, D])
    cst_d = dt("cst", [128, 1024])
    blk_d = dt("blkind", [9, T])
    vecs_d = dt("vecs", [128, 48])
    t5_d = dt("t5b", [128, 2048])
    gfin_d = dt("gfin", [128, D])
    out_d = dt("out", [nseq, T, D], kind="ExternalOutput")
    if debug:
        dbg_d = dt("dbg", [128, 8 * T], BF16, kind="ExternalOutput")

    def wv(w, c0, n):
        return w.rearrange("(k p) c -> p k c", p=128)[:, :, c0:c0 + n]

    with ExitStack() as st:
        S = Sched(nc, st)
        S.stop_at = stop_at

        def sb(name, shape, dtype):
            return st.enter_context(nc.sbuf_tensor("sb_" + name, shape, dtype))

        def psb(name, shape, dtype):
            return st.enter_context(nc.psum_tensor(name, shape, dtype))

        cst = sb("cst", [128, 1024], F32)
        ident_f = cst[:, 0:128]
        triU = cst[:, 128:256]
        strictL = cst[:, 256:384]
        pbias = cst[:, 512:1024]
        ident_b = sb("ident_b", [128, 128], BF16)
        causal_b = sb("causal_b", [128, 128], BF16)
        ones_b = sb("ones_b", [128, 128], BF16)
        ones_f = sb("ones_f", [128, 64], F32)
        KTx = sb("KTx", [128, 8, T], BF16)
        t5b = sb("t5b", [128, 8, 256], F32)
        vecs = sb("vecs", [128, 48], F32)
        gmix, gxat, gmlp, gmem = vecs[:, 0:8], vecs[:, 8:16], vecs[:, 16:24], vecs[:, 24:32]
        gnorm = vecs[:, 32:33]
        c31 = vecs[:, 33:41]
        w_aug = sb("w_aug", [32, 256], BF16)
        wglr = sb("wglr", [128, 8, 16], BF16)
        stat = sb("stat", [128, 48], F32)
        xt = sb("xt", [128, 2, D], F32)
        junk = sb("junk", [128, D], BF16)
        xn = sb("xn", [128, 2, D], BF16)
        hT = sb("hT", [128, 8, 1024], BF16)
        wsl = sb("wsl", [128, 3, 8, 512], BF16)
        mixT = sb("mixT", [128, 8, T], BF16)
        ARENA_COLS = 18496
        arena_t = sb("arena", [128, ARENA_COLS], F32)
        AR = Arena(arena_t, ARENA_COLS)

        pb = [psb("pbank0", [128, 1024], BF16)] + [psb("pbank%d" % i, [128, 512], F32) for i in range(1, 8)]

        S.dma('sp', 'cst', cst[:], cst_d, (), ['cst'])
        S.dma('sp', 'cst', vecs[:], vecs_d, (), ['cst'])
        S.dma('sp', 'cst', t5b[:].rearrange("p a b -> p (a b)"), t5_d, (), ['cst'])
        S.dma('pool', 'cstb', ident_b[:], cst_d[:, 0:128], (), ['cstb'])
        S.dma('pool', 'cstb', causal_b[:], cst_d[:, 384:512], (), ['cstb'])
        S.memset('pool', KTx[64:128, :, :], 0.0, ['KTx'])
        for h in range(8):
            S.dma('pool', 'cstb', KTx[64:73, h, :], blk_d, ['KTx'], ['cstb', 'KTx'])
        S.dma('pool', 'cstb', w_aug[0:16, :], w_gu_d, (), ['cstb'])
        S.dma('pool', 'cstb', w_aug[16:17, :], b_gate_d, (), ['cstb'])
        S.dma('pool', 'cstb', wglr[:], wv(w_in_d, 2560, 16), (), ['cstb'])
        S.memset('dve', ones_b[:], 1.0, ['ones_b'])
        S.memset('dve', ones_f[:], 1.0, ['ones_f'])
        for h in range(8):
            S.ts('dve', t5b[:, h, :], t5b[:, h, :], c31[:, h:h + 1], ALU.subtract, ['cst'], ['cst2'])
        S.checkpoint('const')

        wplan = []
        for s in range(nseq):
            for g in range(NG):
                for c0 in (0, 512, 1024, 1536, 2048, 2576):
                    wplan.append(wv(w_in_d, c0, 512))
            for c in range(4):
                wplan.append(wv(w_xkv_d, c * 512, 512))
            for cg in range(NCG):
                for c in range(2):
                    wplan.append(wv(w_out_d, c * 512, 512))
                for c in range(2):
                    wplan.append(wv(w_xq_d, c * 512, 512))
                for c in range(2):
                    wplan.append(wv(w_xo_d, c * 512, 512))
                for q in range(4):
                    for c in range(2):
                        wplan.append(wv(w_up_d, q * 1024 + c * 512, 512))
                    for c in range(2):
                        wplan.append(w_down_d.rearrange("(q k p) n -> p q k n", k=8, p=128)[:, q, :, c * 512:(c + 1) * 512])
        wstate = {'i': 0, 'issued': 0}

        def wget(ahead=2):
            i = wstate['i']
            while wstate['issued'] <= min(i + ahead, len(wplan) - 1):
                n = wstate['issued']
                S.dma('pool', 'w%d' % (n % 3), wsl[:, n % 3], wplan[n], (), ['w%d' % (n % 3)])
                wstate['issued'] += 1
            wstate['i'] += 1
            return wsl[:, i % 3], 'w%d' % (i % 3)

        MG = [0, 0, 0, 1, 1, 2, 1, 3]
        MA = [0, 64, 32, 64, 0, 64, 32, 64]
        cnt = {'gst': 0, 'stat': 0, 'xs': 0, 'acc': 0, 'sc': 0, 'pt': 0, 'ts': 0, 'ev': 0}

        def rot(name, n):
            v = cnt[name] % n
            cnt[name] += 1
            return v

        def evac_copy(out, in_, reads, writes):
            if rot('ev', 2) == 0:
                S.copy('act', out, in_, reads, writes)
            else:
                S.copy('dve', out, in_, reads, writes)

        def recip2(out, in_, scratch, reads, okey, skey):
            S.act(scratch, in_, AF.Ln, reads, [skey])
            S.act(out, scratch, AF.Exp, [skey], [okey], scale=-1.0)

        def rstd_of(src, skey, nfeat):
            sl = rot('stat', 16)
            k = 'st%d' % sl
            ssq, lnv, rs = stat[:, 3 * sl:3 * sl + 1], stat[:, 3 * sl + 1:3 * sl + 2], stat[:, 3 * sl + 2:3 * sl + 3]
            S.act(junk[:, 0:nfeat], src, AF.Square, [skey], [k], accum_out=ssq)
            S.act(lnv, ssq, AF.Ln, [k], [k], bias=EPS, scale=1.0 / nfeat)
            S.act(rs, lnv, AF.Exp, [k], [k], scale=-0.5)
            return rs, k

        def norm_part1(src, skey):
            rs, k = rstd_of(src, skey, D)
            xs = rot('xs', 2)
            xk = 'xn%d' % xs
            S.ts('dve', xn[:, xs, :], src, rs, ALU.mult, [skey, k], [xk])
            return xs, xk

        def norm_part2(r, gcols, col0, hkey):
            xs, xk = r
            for kk in range(8):
                S.transpose(pb[0][:, kk * 128:(kk + 1) * 128], xn[:, xs, kk * 128:(kk + 1) * 128], ident_b[:],
                            [xk, 'cstb'], ['b0'])
            S.tt('dve', hT[:, :, col0:col0 + 128], pb[0][:, :].rearrange("p (a b) -> p a b", b=128),
                 gcols.unsqueeze(2).to_broadcast([128, 8, 128]), ALU.mult, ['b0', 'cst'], [hkey])

        def norm_to_hT(src, skey, gcols, col0, hkey):
            norm_part2(norm_part1(src, skey), gcols, col0, hkey)

        for s in range(nseq):
            AR.reset()
            VA = AR.bf16(NT * 4 * 192).rearrange("p (a h d) -> p a h d", h=4, d=192)
            QTx = AR.bf16(8 * 512).rearrange("p (a b) -> p a b", b=512)
            QGT = AR.bf16(2 * 512).rearrange("p (a b) -> p a b", b=512)
            KGT = AR.bf16(2 * 512).rearrange("p (a b) -> p a b", b=512)
            KGtok = AR.bf16(4 * 256).rearrange("p (a b) -> p a b", b=256)
            VGtok = AR.bf16(4 * 512).rearrange("p (a b) -> p a b", b=512)
            silu_rg = AR.bf16(4 * 512).rearrange("p (a b) -> p a b", b=512)
            glrT = AR.bf16(512)
            PT = AR.bf16(4 * 512).rearrange("p (a b) -> p a b", b=512)
            allowed = AR.bf16(8 * 80).rearrange("p (a b) -> p a b", b=80)
            qtl2 = AR.bf16(4 * 128).rearrange("p (j h b) -> p j h b", h=2, b=128)
            ktl2 = AR.bf16(4 * 128).rearrange("p (j h b) -> p j h b", h=2, b=128)
            khat = AR.bf16(2 * 128).rearrange("p (a b) -> p a b", b=128)
            ATm = AR.bf16(4 * 128).rearrange("p (a b) -> p a b", b=128)
            S_b = AR.bf16(2 * 128).rearrange("p (a b) -> p a b", b=128)
            ksum_xf = AR.bf16(64)
            ksum_x3 = ksum_xf.rearrange("p (h b) -> p h b", b=8)
            ksum_x4 = ksum_xf.rearrange("p (j e b) -> p j e b", e=2, b=8)
            sp_tok = AR.f32(4 * 256).rearrange("p (a b) -> p a b", b=256)
            etmp = AR.f32(1 * 256).rearrange("p (a b) -> p a b", b=256)
            tmpS = AR.f32(2 * 256).rearrange("p (a b) -> p a b", b=256)
            gm = AR.f32(64)
            top8 = AR.f32(64)
            rl_sb = AR.f32(1 * 512).rearrange("p (a b) -> p a b", b=512)
            eb = AR.f32(2 * 128).rearrange("p (a b) -> p a b", b=128)
            enb = AR.f32(2 * 128).rearrange("p (a b) -> p a b", b=128)
            erev = AR.f32(2 * 128).rearrange("p (a b) -> p a b", b=128)
            o_r = AR.f32(4 * 128).rearrange("p (a b) -> p a b", b=128)
            S_f = AR.f32(2 * 128).rearrange("p (a b) -> p a b", b=128)
            ksum_f = AR.f32(32).rearrange("p (a b) -> p a b", b=8)
            gstat = AR.f32(48).rearrange("p (a b) -> p a b", b=6)

            def norm_stages(g):
                out_ = []
                hs0 = (g % 2) * 512
                for t in range(4):
                    tt = g * 4 + t
                    xs = tt % 2
                    box = {}

                    def n1(tt=tt, xs=xs, box=box):
                        S.dma('sp', 'xt%d' % xs, xt[:, xs, :], x_d[s, tt * 128:(tt + 1) * 128, :], (), ['xt%d' % xs])
                        box['r'] = norm_part1(xt[:, xs, :], 'xt%d' % xs)

                    def n2(t=t, box=box, g=g, hs0=hs0):
                        norm_part2(box['r'], gmix, hs0 + t * 128, 'hTa%d_%d' % (g % 2, t))
                    out_ += [n1, n2]
                return out_

            def mask_stages(g):
                out_ = []
                for t in range(4):
                    B = (4 * g + t) // 2
                    tsl_ = slice(t * 128, (t + 1) * 128)

                    def m1(B=B, tsl_=tsl_):
                        for h in range(8):
                            S.matmul(pb[6][:, h * 8:(h + 1) * 8], QTx[:, h, tsl_], ksum_x3[:, h, :], True, True,
                                     ['QTq', 'QTm', 'ksum_x'], ['b6'])
                        S.tt('dve', gm, pb[6][:, 0:64], pbias[:, B * 64:(B + 1) * 64], ALU.add, ['b6', 'cst'], ['gm'])
                        for h in range(8):
                            S.op('dve', lambda e, h=h: e.max(top8[:, h * 8:(h + 1) * 8], gm[:, h * 8:(h + 1) * 8]),
                                 ['gm'], ['top8'])
                        for h in range(8):
                            S.ts('dve', allowed[:, h, 64:72], gm[:, h * 8:(h + 1) * 8], top8[:, h * 8 + 3:h * 8 + 4],
                                 ALU.is_ge, ['gm', 'top8'], ['allowed'], s2=64.0, op1=ALU.mult)

                    def m2(tsl_=tsl_):
                        for hg in range(2):
                            for h4 in range(4):
                                S.matmul(pb[7][0:73, h4 * 128:(h4 + 1) * 128], allowed[:, hg * 4 + h4, 0:73], ident_b[:],
                                         True, True, ['allowed', 'cstb'], ['b7'])
                            S.copy('act', QTx[64:73, hg * 4:(hg + 1) * 4, tsl_],
                                   pb[7][64:73, :].rearrange("p (a b) -> p a b", b=128), ['b7'], ['QTm'])
                    out_ += [m1, m2]
                return out_

            def gla_stages(g):
                lists = []
                for t in range(4):
                    n = 4 * g + t
                    tsl_ = slice(t * 128, (t + 1) * 128)
                    pair_lists = []
                    for j in range(2):
                        Bk = pb[1 + j]
                        bk = 'b%d' % (1 + j)
                        js = slice(j * 128, (j + 1) * 128)
                        box = {}

                        def g1(t=t, j=j, Bk=Bk, bk=bk, js=js):
                            S.matmul(Bk[:, 0:128], sp_tok[:, t, js], triU, True, True, ['sp_tok', 'cst'], [bk])
                            S.matmul(Bk[:, 128:256], strictL, sp_tok[:, t, js], True, True, ['sp_tok', 'cst'], [bk])

                        def g2(j=j, Bk=Bk, bk=bk):
                            S.act(eb[:, j, :], Bk[:, 0:128], AF.Exp, [bk], ['eb%d' % j])
                            S.act(enb[:, j, :], Bk[:, 0:128], AF.Exp, [bk], ['enb%d' % j], scale=-1.0)
                            S.act(erev[:, j, :], Bk[:, 128:256], AF.Exp, [bk], ['erev%d' % j])

                        def g3(t=t, j=j, js=js, tsl_=tsl_):
                            for hh in range(2):
                                rr = slice(hh * 64, (hh + 1) * 64)
                                S.tt('dve', qtl2[rr, j, hh, :], QGT[rr, j, tsl_], eb[rr, j, :], ALU.mult,
                                     ['QGT', 'eb%d' % j], ['qtl%d' % j])
                                S.tt('dve', ktl2[rr, j, hh, :], KGT[rr, j, tsl_], enb[rr, j, :], ALU.mult,
                                     ['KGT', 'enb%d' % j], ['ktl%d' % j])
                            S.tt('dve', khat[:, j, :], KGtok[:, t, js], erev[:, j, :], ALU.mult,
                                 ['KGtok', 'erev%d' % j], ['khat%d' % j])

                        def g4(j=j, Bk=Bk, bk=bk):
                            for hh in range(2):
                                S.matmul(Bk[:, 256 + hh * 128:384 + hh * 128], ktl2[:, j, hh, :], qtl2[:, j, hh, :],
                                         True, True, ['ktl%d' % j, 'qtl%d' % j], [bk])

                        def g5(j=j, Bk=Bk, bk=bk):
                            for hh in range(2):
                                S.tt('dve', ATm[:, 2 * j + hh, :], Bk[:, 256 + hh * 128:384 + hh * 128], causal_b[:],
                                     ALU.mult, [bk, 'cstb'], ['ATm%d' % (2 * j + hh)])

                        def g6(t=t, j=j, Bk=Bk, bk=bk):
                            for hh in range(2):
                                head = 2 * j + hh
                                S.matmul(Bk[:, hh * 128:(hh + 1) * 128], qtl2[:, j, hh, :], S_b[:, j, :], True, False,
                                         ['qtl%d' % j, 'S_b%d' % j], [bk])
                                S.matmul(Bk[:, hh * 128:(hh + 1) * 128], ATm[:, head, :],
                                         VGtok[:, t, head * 128:(head + 1) * 128], False, True,
                                         ['ATm%d' % head, 'VGtok'], [bk])
                            for hh in range(2):
                                head = 2 * j + hh
                                S.matmul(Bk[:, 256 + hh * 128:384 + hh * 128], khat[:, j, :],
                                         VGtok[:, t, head * 128:(head + 1) * 128], True, True,
                                         ['khat%d' % j, 'VGtok'], [bk])

                        def g7(j=j, Bk=Bk, bk=bk, box=box):
                            sl = rot('gst', 8)
                            k = 'gst%d' % sl
                            box['sl'] = sl
                            for hh in range(2):
                                S.act(junk[:, 0:128], Bk[:, hh * 128:(hh + 1) * 128], AF.Square, [bk], [k],
                                      accum_out=gstat[:, sl, hh:hh + 1])
                            S.act(gstat[:, sl, 2:4], gstat[:, sl, 0:2], AF.Ln, [k], [k], bias=EPS, scale=1.0 / 128)
                            S.act(gstat[:, sl, 4:6], gstat[:, sl, 2:4], AF.Exp, [k], [k], scale=-0.5)
                            for hh in range(2):
                                rr = slice(hh * 64, (hh + 1) * 64)
                                S.stt(S_f[rr, j, :], S_f[rr, j, :], eb[rr, j, 127:128],
                                      Bk[rr, 256 + hh * 128:384 + hh * 128], ALU.mult, ALU.add,
                                      ['S_f%d' % j, 'eb%d' % j, bk], ['S_f%d' % j])
                            S.copy('act', S_b[:, j, :], S_f[:, j, :], ['S_f%d' % j], ['S_b%d' % j])

                        def g8(j=j, Bk=Bk, bk=bk, box=box):
                            sl = box['sl']
                            for hh in range(2):
                                S.ts('dve', o_r[:, 2 * j + hh, :], Bk[:, hh * 128:(hh + 1) * 128],
                                     gstat[:, sl, 4 + hh:5 + hh], ALU.mult, [bk, 'gst%d' % sl], ['o_r%d' % (2 * j + hh)])

                        def g9(j=j, Bk=Bk, bk=bk):
                            for hh in range(2):
                                S.transpose(Bk[:, 256 + hh * 128:384 + hh * 128], o_r[:, 2 * j + hh, :], ident_f,
                                            ['o_r%d' % (2 * j + hh), 'cst'], [bk])

                        def g10(j=j, Bk=Bk, bk=bk, n=n, tsl_=tsl_):
                            for hh in range(2):
                                head = 2 * j + hh
                                S.stt(mixT[:, 4 + head, n * 128:(n + 1) * 128], Bk[:, 256 + hh * 128:384 + hh * 128],
                                      gnorm, silu_rg[:, head, tsl_], ALU.mult, ALU.mult, [bk, 'cst', 'silu_rg'],
                                      ['mix%d' % (4 + head)])
                        pair_lists.append([g1, g2, g3, g4, g5, g6, g7, g8, g9, g10])
                    for a_, b_ in zip(pair_lists[0], pair_lists[1]):
                        lists.append(lambda a_=a_, b_=b_: (a_(), b_()))
                return lists

            def drain(lst, n):
                for _ in range(min(n, len(lst))):
                    lst.pop(0)()

            def proj_fm(w, wk, hkeys, hs, cb, bi):
                for k in range(8):
                    S.matmul(pb[bi][:, :], w[:, k, cb * 128:(cb + 1) * 128], hT[:, k, hs], k == 0, k == 7,
                             hkeys + [wk], ['b%d' % bi])

            def proj_tm(w, wk, hkey, hcol, bi, c0, c1):
                for k in range(8):
                    S.matmul(pb[bi][:, 0:c1 - c0], hT[:, k, hcol:hcol + 128], w[:, k, c0:c1], k == 0, k == 7,
                             [hkey, wk], ['b%d' % bi])

            first_norm = norm_stages(0)
            first_norm[0]()
            first_norm[2]()
            for j in range(4):
                S.memset('dve', VA[:, :, j, 64:128], 1.0, ['VA'])
            S.memset('dve', QTx[:, :, :], 0.0, ['QTq', 'QTm'])
            S.memset('dve', glrT[0:32, :], 1.0, ['glrT'])
            S.memset('dve', allowed[:, :, :], 0.0, ['allowed'])
            S.ts('dve', allowed[:, :, 72:73], c31.unsqueeze(2), -64.0, ALU.add, ['cst', 'allowed'], ['allowed'])
            S.memset('dve', qtl2[:, :, :, :], 0.0, ['qtl0', 'qtl1'])
            S.memset('dve', ktl2[:, :, :, :], 0.0, ['ktl0', 'ktl1'])
            S.memset('dve', S_f[:, :, :], 0.0, ['S_f0', 'S_f1'])
            S.memset('dve', S_b[:, :, :], 0.0, ['S_b0', 'S_b1'])
            S.memset('dve', ksum_f[:, :, :], 0.0, ['ksum_f'])
            S.memset('dve', ksum_xf, 0.0, ['ksum_x'])

            for i_ in (1, 4, 3, 6, 5, 7):
                first_norm[i_]()
            for g in range(NG):
                hs0 = (g % 2) * 512
                hs = slice(hs0, hs0 + 512)
                hkeys = ['hTa%d_%d' % (g % 2, t) for t in range(4)]
                gs = slice(g * 512, (g + 1) * 512)
                S.checkpoint('A_norm')
                ns = norm_stages(g + 1) if g + 1 < NG else []
                for k in range(8):
                    S.matmul(pb[5][0:16, :], wglr[:, k, :], hT[:, k, hs], k == 0, k == 7, hkeys + ['cstb'], ['b5'])
                S.copy('act', glrT[0:16, :], pb[5][0:16, :], ['b5'], ['glrT'])
                w, wk = wget()
                for cb in range(4):
                    bi = 1 + cb % 2
                    proj_fm(w, wk, hkeys, hs, cb, bi)
                    S.act(QTx[0:64, 2 * cb, :], pb[bi][0:64, :], AF.Copy, ['b%d' % bi], ['QTq'], scale=0.125)
                    S.ts('dve', QTx[0:64, 2 * cb + 1, :], pb[bi][64:128, :], 0.125, ALU.mult, ['b%d' % bi], ['QTq'])
                for t in range(4):
                    bi = 5 + t % 2
                    S.matmul(pb[bi][:, 0:256], glrT[0:17, t * 128:(t + 1) * 128], w_aug[0:17, :], True, True,
                             ['glrT', 'cstb'], ['b%d' % bi])
                    S.act(etmp[:, 0, :], pb[bi][:, 0:256], AF.Exp, ['b%d' % bi], ['etmp0'], scale=-1.0)
                    S.act(sp_tok[:, t, :], etmp[:, 0, :], AF.Ln, ['etmp0'], ['sp_tok'], bias=1.0)
                w, wk = wget()
                for cb in range(4):
                    bi = 1 + cb % 2
                    proj_fm(w, wk, hkeys, hs, cb, bi)
                    S.op('dve', lambda e, a=(ksum_f[:, cb, 2 * g:2 * g + 2],
                                             pb[bi][:, :].rearrange("p (b k) -> p b k", k=256)):
                         e.reduce_sum(a[0], a[1], AXL.X), ['b%d' % bi], ['ksum_f'])
                    S.copy('act', KTx[0:64, 2 * cb, gs], pb[bi][0:64, :], ['b%d' % bi], ['KTx'])
                    S.copy('dve', KTx[0:64, 2 * cb + 1, gs], pb[bi][64:128, :], ['b%d' % bi], ['KTx'])
                S.ts('dve', ksum_x4[0:64, :, 0, :], ksum_f[0:64, :, :], 1.0 / 256, ALU.mult, ['ksum_f'], ['ksum_x'])
                S.ts('dve', ksum_x4[0:64, :, 1, :], ksum_f[64:128, :, :], 1.0 / 256, ALU.mult, ['ksum_f'], ['ksum_x'])
                ms = mask_stages(g)
                drain(ms, 1)
                drain(ns, 1)
                w, wk = wget()
                for t in range(4):
                    bi = 1 + t % 2
                    proj_tm(w, wk, hkeys[t], hs0 + t * 128, bi, 0, 512)
                    pv4 = pb[bi][:, :].rearrange("p (j e d) -> p j e d", e=2, d=64)
                    S.copy('act', VA[:, g * 4 + t, :, 0:64], pv4[:, :, 0, :], ['b%d' % bi], ['VA'])
                    S.copy('dve', VA[:, g * 4 + t, :, 128:192], pv4[:, :, 1, :], ['b%d' % bi], ['VA'])
                drain(ms, 2)
                drain(ns, 2)
                w, wk = wget()
                for cb in range(4):
                    bi = 1 + cb % 2
                    proj_fm(w, wk, hkeys, hs, cb, bi)
                    if cb < 2:
                        S.act(QGT[:, cb, :], pb[bi][:, :], AF.Copy, ['b%d' % bi], ['QGT'], scale=0.125)
                    else:
                        S.copy('dve', KGT[:, cb - 2, :], pb[bi][:, :], ['b%d' % bi], ['KGT'])
                for t in range(4):
                    bi = 1 + t % 2
                    proj_tm(w, wk, hkeys[t], hs0 + t * 128, bi, 256, 512)
                    evac_copy(KGtok[:, t, :], pb[bi][:, 0:256], ['b%d' % bi], ['KGtok'])
                drain(ms, 2)
                drain(ns, 2)
                w, wk = wget()
                for t in range(4):
                    bi = 1 + t % 2
                    proj_tm(w, wk, hkeys[t], hs0 + t * 128, bi, 0, 512)
                    evac_copy(VGtok[:, t, :], pb[bi][:, :], ['b%d' % bi], ['VGtok'])
                drain(ms, 2)
                drain(ns, 2)
                w, wk = wget()
                for cb in range(4):
                    bi = 1 + cb % 2
                    proj_fm(w, wk, hkeys, hs, cb, bi)
                    S.act(silu_rg[:, cb, :], pb[bi][:, :], AF.Silu, ['b%d' % bi], ['silu_rg'])
                drain(ms, len(ms))
                drain(ns, len(ns))
                S.checkpoint('A_proj')
                S.checkpoint('B_mask')
                deferred = gla_stages(g)
                nkt = 4 * g + 4
                iters = [(h, kt) for h in range(8) for kt in range(nkt)]
                SB = [3, 4, 7]
                OB = [5, 6]

                def emit_qk(idx):
                    h, kt = iters[idx]
                    qlo = max(0, kt - 4 * g) * 128
                    bi = SB[idx % 3]
                    S.matmul(pb[bi][:, qlo:512], KTx[:, h, kt * 128:(kt + 1) * 128], QTx[:, h, qlo:512],
                             True, True, ['KTx', 'QTq', 'QTm', 'cstb'], ['b%d' % bi])

                def emit_exp(idx):
                    h, kt = iters[idx]
                    i0 = kt - 4 * g
                    bi = SB[idx % 3]
                    bk = 'b%d' % bi
                    bank = pb[bi]
                    ps_ = idx % 4
                    ptk = 'PT%d' % ps_
                    qlo = max(0, i0) * 128
                    if i0 >= -1:
                        na = max(0, i0) * 128
                        nb_ = min(i0 + 2, 4) * 128
                        off = 128 if i0 == -1 else 0
                        n = nb_ - na
                        S.tt('dve', bank[:, na:nb_], bank[:, na:nb_], t5b[:, h, off:off + n], ALU.add,
                             [bk, 'cst2'], [bk])
                    S.act(PT[:, ps_, qlo:512], bank[:, qlo:512], AF.Exp, [bk], [ptk])

                def emit_pv(idx):
                    h, kt = iters[idx]
                    j, e_ = h // 2, h % 2
                    qlo = max(0, kt - 4 * g) * 128
                    ob = OB[h % 2]
                    obk = 'b%d' % ob
                    ps_ = idx % 4
                    S.matmul(pb[ob][:, qlo:512], VA[:, kt, j, e_ * 64:e_ * 64 + 128], PT[:, ps_, qlo:512],
                             kt == 0, kt == nkt - 1, ['VA', 'PT%d' % ps_], [obk])
                    if kt == nkt - 1:
                        sl = 0
                        lo, hi = (64, 128) if e_ == 0 else (0, 64)
                        oo, oh = (0, 64) if e_ == 0 else (64, 128)
                        tmpf = tmpS[:, :, :].rearrange("p a b -> p (a b)")
                        recip2(rl_sb[oo:oh, sl, :], pb[ob][lo:hi, :], tmpf[lo:hi, :], [obk], 'rl%d' % sl, 'rscrA')
                        S.tt('dve', mixT[oo:oh, j, gs], pb[ob][oo:oh, :], rl_sb[oo:oh, sl, :], ALU.mult,
                             [obk, 'rl%d' % sl], ['mix%d' % j])

                nit = len(iters)
                n_def = len(deferred)
                emit_qk(0)
                if nit > 1:
                    emit_qk(1)
                for idx in range(nit):
                    if idx + 2 < nit:
                        emit_qk(idx + 2)
                    emit_exp(idx)
                    emit_pv(idx)
                    want = min(n_def, -(-((idx + 1) * n_def) // max(1, (3 * nit) // 4)))
                    drain(deferred, want - (n_def - len(deferred)))
                drain(deferred, len(deferred))
                S.checkpoint('B_attn')
            S.checkpoint('B_gla')
            if debug:
                S.dma('sp', 'dbg', dbg_d, mixT[:].rearrange("p a b -> p (a b)"), ['mix%d' % k for k in range(8)], ['dbg_d'])

            S.barrier()
            AR.reset()
            kxT = AR.bf16(8 * 256).rearrange("p (a b) -> p a b", b=256)
            vx = AR.bf16(2 * 1024).rearrange("p (a b) -> p a b", b=1024)
            qxT = AR.bf16(2 * GC).rearrange("p (a b) -> p a b", b=GC)
            big = AR.bf16(8 * GC).rearrange("p (a b) -> p a b", b=GC)
            PTx = AR.bf16(2 * 512).rearrange("p (a b) -> p a b", b=512)
            rtmp = AR.bf16(2 * 512).rearrange("p (a b) -> p a b", b=512)
            x1 = AR.f32(TC * D).rearrange("p (a b) -> p a b", b=D)
            rlx = AR.f32(512)
            rscr = AR.f32(512)
            gfin = AR.f32(D)
            S.dma('sp', 'gfin', gfin, gfin_d, (), ['gfin'])

            mkeys = ['hT0', 'hT1']
            for mt in range(2):
                xs = mt
                S.dma('sp', 'xt%d' % xs, xt[:, xs, :], mem_d[s, mt * 128:(mt + 1) * 128, :], (), ['xt%d' % xs])
                norm_to_hT(xt[:, xs, :], 'xt%d' % xs, gmem, mt * 128, mkeys[mt])
            for c in range(4):
                w, wk = wget()
                if c < 2:
                    for cb in range(4):
                        bi = 1 + cb % 2
                        for k in range(8):
                            S.matmul(pb[bi][:, 0:256], w[:, k, cb * 128:(cb + 1) * 128], hT[:, k, 0:256], k == 0, k == 7,
                                     mkeys + [wk], ['b%d' % bi])
                        evac_copy(kxT[:, c * 4 + cb, :], pb[bi][:, 0:256], ['b%d' % bi], ['kxT'])
                else:
                    for mt in range(2):
                        bi = 1 + mt % 2
                        for k in range(8):
                            S.matmul(pb[bi][:, :], hT[:, k, mt * 128:(mt + 1) * 128], w[:, k, :], k == 0, k == 7,
                                     [mkeys[mt], wk], ['b%d' % bi])
                        evac_copy(vx[:, mt, (c - 2) * 512:(c - 1) * 512], pb[bi][:, :], ['b%d' % bi], ['vx'])

            S.checkpoint('C_mem')
            for cg in range(NCG):
                tok0 = cg * GC
                hk = ['hT%d' % t for t in range(TC)]
                mixk = ['mix%d' % k for k in range(8)]
                for t in range(TC):
                    S.dma('sp', 'x1_%d' % t, x1[:, t, :], x_d[s, tok0 + t * 128:tok0 + (t + 1) * 128, :], (), ['x1_%d' % t])
                wA = wget()
                wB = wget(ahead=1)
                pend = None
                for t in range(TC):
                    for c, (w, wk) in enumerate((wA, wB)):
                        bi = 3 + (2 * t + c) % 4
                        for k in range(8):
                            S.matmul(pb[bi][:, :], mixT[:, k, tok0 + t * 128:tok0 + (t + 1) * 128], w[:, k, :],
                                     k == 0, k == 7, mixk + [wk], ['b%d' % bi])
                        S.tt('dve', x1[:, t, c * 512:(c + 1) * 512], pb[bi][:, :], x1[:, t, c * 512:(c + 1) * 512],
                             ALU.add, ['b%d' % bi, 'x1_%d' % t], ['x1_%d' % t])
                    r_ = norm_part1(x1[:, t, :], 'x1_%d' % t)
                    if pend is not None:
                        norm_part2(*pend)
                    pend = (r_, gxat, t * 128, hk[t])
                norm_part2(*pend)
                S.checkpoint('C_out')
                def xq_proj(xh, w, wk):
                    for c2 in range(2):
                        cb = (xh % 2) * 2 + c2
                        for hf in range(GC // 512):
                            bi = 1 + hf % 2
                            for k in range(8):
                                S.matmul(pb[bi][:, :], w[:, k, cb * 128:(cb + 1) * 128], hT[:, k, hf * 512:(hf + 1) * 512],
                                         k == 0, k == 7, hk + [wk], ['b%d' % bi])
                            S.act(qxT[:, c2, hf * 512:(hf + 1) * 512], pb[bi][:, :], AF.Copy, ['b%d' % bi], ['qxT'],
                                  scale=1.0 / 16)

                NHF = GC // 512
                STB = [(5, 6), (3, 4)]
                w, wk = wget()
                xq_proj(0, w, wk)
                for xh in range(4):
                    for hf in range(NHF):
                        hs = slice(hf * 512, (hf + 1) * 512)
                        for mt in range(2):
                            bi = STB[hf % 2][mt]
                            for c2 in range(2):
                                S.matmul(pb[bi][:, :], kxT[:, 2 * xh + c2, mt * 128:(mt + 1) * 128], qxT[:, c2, hs],
                                         c2 == 0, c2 == 1, ['kxT', 'qxT'], ['b%d' % bi])
                    for hf in range(NHF):
                        hs = slice(hf * 512, (hf + 1) * 512)
                        for mt in range(2):
                            bi = STB[hf % 2][mt]
                            S.act(PTx[:, mt, :], pb[bi][:, :], AF.Exp, ['b%d' % bi], ['PTx%d' % mt])
                        if hf == 0 and xh + 1 < 4:
                            if (xh + 1) % 2 == 0:
                                w, wk = wget()
                            xq_proj(xh + 1, w, wk)
                        for mt in range(2):
                            S.matmul(pb[7][:, :], ones_b[:], PTx[:, mt, :], mt == 0, mt == 1, ['ones_b', 'PTx%d' % mt], ['b7'])
                        recip2(rlx, pb[7][:, :], rscr, ['b7'], 'rlx', 'rscr')
                        for c2 in range(2):
                            bi = 1 + c2
                            for mt in range(2):
                                S.matmul(pb[bi][:, :], vx[:, mt, (2 * xh + c2) * 128:(2 * xh + c2 + 1) * 128], PTx[:, mt, :],
                                         mt == 0, mt == 1, ['vx', 'PTx%d' % mt], ['b%d' % bi])
                            S.tt('dve', big[:, 2 * xh + c2, hs], pb[bi][:, :], rlx, ALU.mult, ['b%d' % bi, 'rlx'], ['big'])
                wA = wget()
                wB = wget(ahead=1)
                pend = None
                for t in range(TC):
                    for c, (w, wk) in enumerate((wA, wB)):
                        bi = 3 + (2 * t + c) % 4
                        for k in range(8):
                            S.matmul(pb[bi][:, :], big[:, k, t * 128:(t + 1) * 128], w[:, k, :], k == 0, k == 7,
                                     ['big', wk], ['b%d' % bi])
                        S.tt('dve', x1[:, t, c * 512:(c + 1) * 512], pb[bi][:, :], x1[:, t, c * 512:(c + 1) * 512],
                             ALU.add, ['b%d' % bi, 'x1_%d' % t], ['x1_%d' % t])
                    r_ = norm_part1(x1[:, t, :], 'x1_%d' % t)
                    if pend is not None:
                        norm_part2(*pend)
                    pend = (r_, gmlp, t * 128, hk[t])
                norm_part2(*pend)
                S.checkpoint('C_xattn')
                for q in range(4):
                    for c in range(2):
                        w, wk = wget()
                        for cb in range(4):
                            for hf in range(GC // 512):
                                bi = 1 + rot('acc', 2)
                                for k in range(8):
                                    S.matmul(pb[bi][:, :], w[:, k, cb * 128:(cb + 1) * 128], hT[:, k, hf * 512:(hf + 1) * 512],
                                             k == 0, k == 7, hk + [wk], ['b%d' % bi])
                                rsl = rot('ts', 2)
                                S.act(rtmp[:, rsl, :], pb[bi][:, :], AF.Relu, ['b%d' % bi], ['rtmp%d' % rsl])
                                S.tt('dve', big[:, c * 4 + cb, hf * 512:(hf + 1) * 512], rtmp[:, rsl, :], rtmp[:, rsl, :],
                                     ALU.mult, ['rtmp%d' % rsl], ['big'])
                    for c in range(2):
                        w, wk = wget()
                        for t in range(TC):
                            bi = 3 + t % 4
                            for k in range(8):
                                S.matmul(pb[bi][:, :], big[:, k, t * 128:(t + 1) * 128], w[:, k, :], k == 0, k == 7,
                                         ['big', wk], ['b%d' % bi])
                            S.tt('dve', x1[:, t, c * 512:(c + 1) * 512], pb[bi][:, :], x1[:, t, c * 512:(c + 1) * 512],
                                 ALU.add, ['b%d' % bi, 'x1_%d' % t], ['x1_%d' % t])
                S.checkpoint('C_mlp')
                for t in range(TC):
                    rs, rk = rstd_of(x1[:, t, :], 'x1_%d' % t, D)
                    xs = rot('xs', 2)
                    S.stt(xt[:, xs, :], x1[:, t, :], rs, gfin, ALU.mult, ALU.mult, ['x1_%d' % t, rk, 'gfin'], ['xt%d' % xs])
                    S.dma('sp', 'out%d' % xs, out_d[s, tok0 + t * 128:tok0 + (t + 1) * 128, :], xt[:, xs, :],
                          ['xt%d' % xs], ['out_d%d' % xs])
            S.barrier()

        final_keys = ['out_d0', 'out_d1'] + (['dbg_d'] if debug else [])
        S.wait_all('sp', final_keys)
        S.barrier()
        assert S.stopped or wstate['i'] == len(wplan), (wstate['i'], len(wplan))
        with nc.allow_non_contiguous_dma(reason="strided weight / constant loads"):
            S.emit()
    return nc


def _rel_bucket_np(d):
    d = np.maximum(d, 0)
    large = 16 + (np.log(np.maximum(d, 1).astype(np.float32) / np.float32(16)) / np.float32(math.log(128 / 16))
                  * np.float32(16)).astype(np.int32)
    large = np.minimum(large, 31)
    return np.where(d < 16, d, large)


def _consts(T=2048):
    cst = np.zeros((128, 1024), np.float32)
    i = np.arange(128)
    cst[:, 0:128] = np.eye(128, dtype=np.float32)
    cst[:, 128:256] = np.where(i[:, None] <= i[None, :], -1.0 / 16, 0.0)
    cst[:, 256:384] = np.where(i[:, None] > i[None, :], -1.0 / 16, 0.0)
    cst[:, 384:512] = np.where(i[:, None] <= i[None, :], 1.0, 0.0)
    for B in range(8):
        row = np.where(np.arange(8) < B, 0.0, np.where(np.arange(8) == B, 64.0, -64.0)).astype(np.float32)
        cst[:, 512 + B * 64:512 + (B + 1) * 64] = np.tile(row, 8)[None, :]
    blk = np.zeros((9, T), np.float32)
    kpos = np.arange(T)
    for b in range(8):
        blk[b, :] = (kpos // 256 == b)
    blk[8, :] = 1.0
    return cst, blk


def _host_inputs(rp_table, norm_mix, g_norm, norm_xattn, norm_mem, norm_mlp, norm_final, T=2048):
    cst, blk = _consts(T)
    vecs = np.zeros((128, 48), np.float32)
    vecs[:, 0:8] = np.asarray(norm_mix, np.float32).reshape(8, 128).T
    vecs[:, 8:16] = np.asarray(norm_xattn, np.float32).reshape(8, 128).T
    vecs[:, 16:24] = np.asarray(norm_mlp, np.float32).reshape(8, 128).T
    vecs[:, 24:32] = np.asarray(norm_mem, np.float32).reshape(8, 128).T
    vecs[:, 32] = np.asarray(g_norm, np.float32).reshape(128)
    rp = np.asarray(rp_table, np.float32)
    vecs[:, 33:41] = rp[31][None, :]
    i = np.arange(128)[:, None]
    jj = np.arange(256)[None, :]
    dist = jj - i
    idx = _rel_bucket_np(dist)
    t5 = rp[idx]
    t5 = np.where((dist >= 0)[:, :, None], t5, np.float32(-30000.0)).astype(np.float32)
    t5 = np.ascontiguousarray(t5.transpose(0, 2, 1)).reshape(128, 2048)
    gfin = np.ascontiguousarray(np.broadcast_to(np.asarray(norm_final, np.float32).reshape(1, D), (128, D)))
    return cst, blk, vecs, t5, gfin


_NC_CACHE = {}


def kernel(x, mem, rp_table, norm_mix, w_in, w_gate_up, b_gate, g_norm, w_out, norm_xattn, norm_mem, w_xq, w_xkv,
           w_xo, norm_mlp, w_up, w_down, norm_final):
    x = np.asarray(x, np.float32)
    mem = np.asarray(mem, np.float32)
    Bt, T, _ = x.shape
    nseq = Bt // NCORES
    cst, blk, vecs, t5, gfin = _host_inputs(rp_table, norm_mix, g_norm, norm_xattn, norm_mem, norm_mlp, norm_final, T)
    key = (nseq, T)
    if key not in _NC_CACHE:
        _NC_CACHE[key] = build(nseq, T)
    nc = _NC_CACHE[key]
    f = lambda a: np.ascontiguousarray(np.asarray(a, np.float32))
    shared = {
        "w_in": f(w_in[0]), "w_gate_up": f(w_gate_up[0]), "b_gate": f(b_gate[0]).reshape(1, 256),
        "w_out": f(w_out[0]), "w_xq": f(w_xq[0]), "w_xkv": f(w_xkv[0]), "w_xo": f(w_xo[0]),
        "w_up": f(w_up[0]), "w_down": f(w_down[0]),
        "cst": cst, "blkind": blk, "vecs": vecs, "t5b": t5, "gfin": gfin,
    }
    in_maps = []
    for c in range(NCORES):
        m = dict(shared)
        m["x"] = np.ascontiguousarray(x[c * nseq:(c + 1) * nseq])
        m["mem"] = np.ascontiguousarray(mem[c * nseq:(c + 1) * nseq])
        in_maps.append(m)
    res = run_bass_kernel_spmd(nc, in_maps, core_ids=list(range(NCORES)))
    out = np.concatenate([np.asarray(r["out"], np.float32) for r in res.results], axis=0)
    return out
```

```python
import math
from contextlib import ExitStack

import numpy as np
import concourse.bass as bass
import concourse.mybir as mybir
from concourse.bass_utils import run_bass_kernel_spmd

F32 = mybir.dt.float32
BF16 = mybir.dt.bfloat16
AF = mybir.ActivationFunctionType
ALU = mybir.AluOpType
AXL = mybir.AxisListType

D = 1024
MEM = 256
IN_COLS = 3088
DFF = 4096
EPS = 1e-6
NCORES = 8

ENGS = ['pe', 'act', 'dve', 'pool', 'sp']
PSUM_KEYS = set('b%d' % i for i in range(8))
SAME_ENG_SYNC = {'pe': False, 'act': True, 'dve': True, 'pool': True, 'sp': False}


class Sched:
    def __init__(self, nc, st):
        self.nc = nc
        self.st = st
        self.semobj = {}
        for e in ENGS:
            self.semobj[e] = st.enter_context(nc.semaphore('q_' + e))
        self.cnt = {e: 0 for e in ENGS}
        self.known = {e: {} for e in ENGS}
        self.prog = {e: [] for e in ENGS}
        self.lastw = {}
        self.readers = {}
        self.dmacnt = {}
        self.stopped = False
        self.stop_at = None

    def checkpoint(self, label):
        if self.stop_at is not None and label == self.stop_at:
            self.stopped = True

    def _sem(self, key):
        if key not in self.semobj:
            self.semobj[key] = self.st.enter_context(self.nc.semaphore('d_' + key))
            self.dmacnt[key] = 0
        return self.semobj[key]

    def _deps(self, reads, writes, eng=None):
        deps = {}

        def add(s, v):
            if deps.get(s, 0) < v:
                deps[s] = v
        for r in reads:
            w = self.lastw.get(r)
            if w is not None:
                add(*w)
            if r in PSUM_KEYS:
                for s_, v in self.readers.get(r, {}).items():
                    if s_ != eng:
                        add(s_, v)
        for w_ in writes:
            w = self.lastw.get(w_)
            if w is not None:
                add(*w)
            for s, v in self.readers.get(w_, {}).items():
                add(s, v)
        return deps

    def _waits(self, eng, deps):
        waits = []
        for s, v in deps.items():
            if s == eng and not SAME_ENG_SYNC[eng]:
                continue
            if self.known[eng].get(s, 0) < v:
                waits.append((s, v))
                self.known[eng][s] = v
        return waits

    def _mark(self, reads, writes, s, v):
        for r in reads:
            d = self.readers.setdefault(r, {})
            if d.get(s, 0) < v:
                d[s] = v
        for w in writes:
            self.lastw[w] = (s, v)
            self.readers[w] = {}

    def op(self, eng, fn, reads=(), writes=()):
        if self.stopped:
            return
        deps = self._deps(reads, writes, eng)
        waits = self._waits(eng, deps)
        self.cnt[eng] += 1
        v = self.cnt[eng]
        self.prog[eng].append((waits, fn, eng, 1))
        self._mark(reads, writes, eng, v)

    def dma(self, q, semkey, out, in_, reads=(), writes=()):
        if self.stopped:
            return
        self._sem(semkey)
        deps = self._deps(reads, writes)
        waits = self._waits(q, deps)
        self.dmacnt[semkey] += 16
        v = self.dmacnt[semkey]
        fn = (lambda e, o=out, i=in_: e.dma_start(out=o, in_=i))
        self.prog[q].append((waits, fn, semkey, 16))
        self._mark(reads, writes, semkey, v)

    def wait_all(self, eng, keys):
        deps = self._deps(keys, ())
        waits = self._waits(eng, deps)
        if waits:
            self.prog[eng].append((waits, None, None, 0))

    def barrier(self):
        for e in ENGS:
            deps = {}
            for e2 in ENGS:
                if e2 != e and e2 != 'sp' and self.cnt[e2] > 0:
                    deps[e2] = self.cnt[e2]
            for k, v in self.dmacnt.items():
                if v > 0:
                    deps[k] = v
            waits = self._waits(e, deps)
            if waits:
                self.prog[e].append((waits, None, None, 0))

    def replay(self, name, e):
        for waits, fn, semname, inc in self.prog[name]:
            for s, v in waits:
                e.wait_ge(self.semobj[s], v)
            if fn is None:
                continue
            inst = fn(e)
            inst.then_inc(self.semobj[semname], inc)

    def emit(self):
        nc = self.nc
        with nc.Block() as block:
            @block.tensor
            def _(e):
                self.replay('pe', e)

            @block.scalar
            def _(e):
                self.replay('act', e)

            @block.vector
            def _(e):
                self.replay('dve', e)

            @block.gpsimd
            def _(e):
                self.replay('pool', e)

            @block.sync
            def _(e):
                self.replay('sp', e)

    def matmul(self, out, lhsT, rhs, start, stop, reads, writes):
        self.op('pe', lambda e, a=(out, lhsT, rhs, start, stop): e.matmul(a[0], a[1], a[2], start=a[3], stop=a[4]),
                reads, writes)

    def transpose(self, out, in_, ident, reads, writes):
        self.op('pe', lambda e, a=(out, in_, ident): e.transpose(a[0], a[1], a[2]), reads, writes)

    def act(self, out, in_, func, reads, writes, bias=None, scale=None, accum_out=None):
        kw = {}
        if bias is not None:
            kw['bias'] = bias
        if scale is not None:
            kw['scale'] = scale
        if accum_out is not None:
            kw['accum_out'] = accum_out
        self.op('act', lambda e, a=(out, in_, func), kw=kw: e.activation(a[0], a[1], a[2], **kw), reads, writes)

    def tt(self, eng, out, in0, in1, op, reads, writes):
        self.op(eng, lambda e, a=(out, in0, in1, op): e.tensor_tensor(a[0], a[1], a[2], a[3]), reads, writes)

    def ts(self, eng, out, in0, s1, op0, reads, writes, s2=None, op1=None):
        kw = {}
        if op1 is not None:
            kw['op1'] = op1
        self.op(eng, lambda e, a=(out, in0, s1, s2, op0), kw=kw: e.tensor_scalar(a[0], a[1], a[2], a[3], a[4], **kw),
                reads, writes)

    def stt(self, out, in0, scalar, in1, op0, op1, reads, writes):
        self.op('dve', lambda e, a=(out, in0, scalar, in1, op0, op1):
                e.scalar_tensor_tensor(a[0], a[1], a[2], a[3], a[4], a[5]), reads, writes)

    def copy(self, eng, out, in_, reads, writes):
        if eng == 'act':
            self.op('act', lambda e, a=(out, in_): e.copy(a[0], a[1]), reads, writes)
        else:
            self.op(eng, lambda e, a=(out, in_): e.tensor_copy(a[0], a[1]), reads, writes)

    def memset(self, eng, ap, val, writes):
        self.op(eng, lambda e, a=(ap, val): e.memset(a[0], a[1]), (), writes)


class Arena:
    def __init__(self, tile, ncols):
        self.t = tile
        self.n = ncols
        self.off = 0

    def reset(self):
        self.off = 0

    def f32(self, ncols):
        assert self.off + ncols <= self.n, (self.off, ncols, self.n)
        ap = self.t[:, self.off:self.off + ncols]
        self.off += ncols
        return ap

    def bf16(self, ncols):
        n32 = (ncols + 1) // 2
        assert self.off + n32 <= self.n, (self.off, n32, self.n)
        ap = self.t[:, self.off:self.off + n32].bitcast(BF16)
        self.off += n32
        return ap


def build(nseq=2, T=2048, GC=1024, debug=False, stop_at=None):
    NT = T // 128
    NG = T // 512
    GC = min(GC, T)
    NCG = T // GC
    TC = GC // 128
    nc = bass.Bass("TRN2", target_bir_lowering=False)

    def dt(name, shape, dtype=F32, kind="ExternalInput"):
        return nc.dram_tensor(name, shape, dtype, kind=kind).ap()

    x_d = dt("x", [nseq, T, D])
    mem_d = dt("mem", [nseq, MEM, D])
    w_in_d = dt("w_in", [D, IN_COLS])
    w_gu_d = dt("w_gate_up", [16, 256])
    b_gate_d = dt("b_gate", [1, 256])
    w_out_d = dt("w_out", [D, D])
    w_xq_d = dt("w_xq", [D, D])
    w_xkv_d = dt("w_xkv", [D, 2 * D])
    w_xo_d = dt("w_xo", [D, D])
    w_up_d = dt("w_up", [D, DFF])
    w_down_d = dt("w_down", [DFF, D])
    cst_d = dt("cst", [128, 1024])
    blk_d = dt("blkind", [9, T])
    vecs_d = dt("vecs", [128, 48])
    t5_d = dt("t5b", [128, 2048])
    gfin_d = dt("gfin", [128, D])
    out_d = dt("out", [nseq, T, D], kind="ExternalOutput")
    if debug:
        dbg_d = dt("dbg", [128, 8 * T], BF16, kind="ExternalOutput")

    def wv(w, c0, n):
        return w.rearrange("(k p) c -> p k c", p=128)[:, :, c0:c0 + n]

    with ExitStack() as st:
        S = Sched(nc, st)
        S.stop_at = stop_at

        def sb(name, shape, dtype):
            return st.enter_context(nc.sbuf_tensor("sb_" + name, shape, dtype))

        def psb(name, shape, dtype):
            return st.enter_context(nc.psum_tensor(name, shape, dtype))

        cst = sb("cst", [128, 1024], F32)
        ident_f = cst[:, 0:128]
        triU = cst[:, 128:256]
        strictL = cst[:, 256:384]
        pbias = cst[:, 512:1024]
        ident_b = sb("ident_b", [128, 128], BF16)
        causal_b = sb("causal_b", [128, 128], BF16)
        ones_b = sb("ones_b", [128, 128], BF16)
        ones_f = sb("ones_f", [128, 64], F32)
        KTx = sb("KTx", [128, 8, T], BF16)
        t5b = sb("t5b", [128, 8, 256], F32)
        vecs = sb("vecs", [128, 48], F32)
        gmix, gxat, gmlp, gmem = vecs[:, 0:8], vecs[:, 8:16], vecs[:, 16:24], vecs[:, 24:32]
        gnorm = vecs[:, 32:33]
        c31 = vecs[:, 33:41]
        w_aug = sb("w_aug", [32, 256], BF16)
        wglr = sb("wglr", [128, 8, 16], BF16)
        stat = sb("stat", [128, 48], F32)
        xt = sb("xt", [128, 2, D], F32)
        junk = sb("junk", [128, D], BF16)
        xn = sb("xn", [128, 2, D], BF16)
        hT = sb("hT", [128, 8, 1024], BF16)
        wsl = sb("wsl", [128, 3, 8, 512], BF16)
        mixT = sb("mixT", [128, 8, T], BF16)
        ARENA_COLS = 18496
        arena_t = sb("arena", [128, ARENA_COLS], F32)
        AR = Arena(arena_t, ARENA_COLS)

        pb = [psb("pbank0", [128, 1024], BF16)] + [psb("pbank%d" % i, [128, 512], F32) for i in range(1, 8)]

        S.dma('sp', 'cst', cst[:], cst_d, (), ['cst'])
        S.dma('sp', 'cst', vecs[:], vecs_d, (), ['cst'])
        S.dma('sp', 'cst', t5b[:].rearrange("p a b -> p (a b)"), t5_d, (), ['cst'])
        S.dma('pool', 'cstb', ident_b[:], cst_d[:, 0:128], (), ['cstb'])
        S.dma('pool', 'cstb', causal_b[:], cst_d[:, 384:512], (), ['cstb'])
        S.memset('pool', KTx[64:128, :, :], 0.0, ['KTx'])
        for h in range(8):
            S.dma('pool', 'cstb', KTx[64:73, h, :], blk_d, ['KTx'], ['cstb', 'KTx'])
        S.dma('pool', 'cstb', w_aug[0:16, :], w_gu_d, (), ['cstb'])
        S.dma('pool', 'cstb', w_aug[16:17, :], b_gate_d, (), ['cstb'])
        S.dma('pool', 'cstb', wglr[:], wv(w_in_d, 2560, 16), (), ['cstb'])
        S.memset('dve', ones_b[:], 1.0, ['ones_b'])
        S.memset('dve', ones_f[:], 1.0, ['ones_f'])
        for h in range(8):
            S.ts('dve', t5b[:, h, :], t5b[:, h, :], c31[:, h:h + 1], ALU.subtract, ['cst'], ['cst2'])
        S.checkpoint('const')

        wplan = []
        for s in range(nseq):
            for g in range(NG):
                for c0 in (0, 512, 1024, 1536, 2048, 2576):
                    wplan.append(wv(w_in_d, c0, 512))
            for c in range(4):
                wplan.append(wv(w_xkv_d, c * 512, 512))
            for cg in range(NCG):
                for c in range(2):
                    wplan.append(wv(w_out_d, c * 512, 512))
                for c in range(2):
                    wplan.append(wv(w_xq_d, c * 512, 512))
                for c in range(2):
                    wplan.append(wv(w_xo_d, c * 512, 512))
                for q in range(4):
                    for c in range(2):
                        wplan.append(wv(w_up_d, q * 1024 + c * 512, 512))
                    for c in range(2):
                        wplan.append(w_down_d.rearrange("(q k p) n -> p q k n", k=8, p=128)[:, q, :, c * 512:(c + 1) * 512])
        wstate = {'i': 0, 'issued': 0}

        def wget(ahead=2):
            i = wstate['i']
            while wstate['issued'] <= min(i + ahead, len(wplan) - 1):
                n = wstate['issued']
                S.dma('pool', 'w%d' % (n % 3), wsl[:, n % 3], wplan[n], (), ['w%d' % (n % 3)])
                wstate['issued'] += 1
            wstate['i'] += 1
            return wsl[:, i % 3], 'w%d' % (i % 3)

        MG = [0, 0, 0, 1, 1, 2, 1, 3]
        MA = [0, 64, 32, 64, 0, 64, 32, 64]
        cnt = {'gst': 0, 'stat': 0, 'xs': 0, 'acc': 0, 'sc': 0, 'pt': 0, 'ts': 0, 'ev': 0}

        def rot(name, n):
            v = cnt[name] % n
            cnt[name] += 1
            return v

        def evac_copy(out, in_, reads, writes):
            if rot('ev', 2) == 0:
                S.copy('act', out, in_, reads, writes)
            else:
                S.copy('dve', out, in_, reads, writes)

        def recip2(out, in_, scratch, reads, okey, skey):
            S.act(scratch, in_, AF.Ln, reads, [skey])
            S.act(out, scratch, AF.Exp, [skey], [okey], scale=-1.0)

        def rstd_of(src, skey, nfeat):
            sl = rot('stat', 16)
            k = 'st%d' % sl
            ssq, lnv, rs = stat[:, 3 * sl:3 * sl + 1], stat[:, 3 * sl + 1:3 * sl + 2], stat[:, 3 * sl + 2:3 * sl + 3]
            S.act(junk[:, 0:nfeat], src, AF.Square, [skey], [k], accum_out=ssq)
            S.act(lnv, ssq, AF.Ln, [k], [k], bias=EPS, scale=1.0 / nfeat)
            S.act(rs, lnv, AF.Exp, [k], [k], scale=-0.5)
            return rs, k

        def norm_part1(src, skey):
            rs, k = rstd_of(src, skey, D)
            xs = rot('xs', 2)
            xk = 'xn%d' % xs
            S.act(xn[:, xs, :], src, AF.Copy, [skey, k], [xk], scale=rs)
            return xs, xk

        def norm_part2(r, gcols, col0, hkey):
            xs, xk = r
            for kk in range(8):
                S.transpose(pb[0][:, kk * 128:(kk + 1) * 128], xn[:, xs, kk * 128:(kk + 1) * 128], ident_b[:],
                            [xk, 'cstb'], ['b0'])
            S.tt('dve', hT[:, :, col0:col0 + 128], pb[0][:, :].rearrange("p (a b) -> p a b", b=128),
                 gcols.unsqueeze(2).to_broadcast([128, 8, 128]), ALU.mult, ['b0', 'cst'], [hkey])

        def norm_to_hT(src, skey, gcols, col0, hkey):
            norm_part2(norm_part1(src, skey), gcols, col0, hkey)

        for s in range(nseq):
            AR.reset()
            VA = AR.bf16(NT * 4 * 192).rearrange("p (a h d) -> p a h d", h=4, d=192)
            QTx = AR.bf16(8 * 512).rearrange("p (a b) -> p a b", b=512)
            QGT = AR.bf16(2 * 512).rearrange("p (a b) -> p a b", b=512)
            KGT = AR.bf16(2 * 512).rearrange("p (a b) -> p a b", b=512)
            KGtok = AR.bf16(4 * 256).rearrange("p (a b) -> p a b", b=256)
            VGtok = AR.bf16(4 * 512).rearrange("p (a b) -> p a b", b=512)
            silu_rg = AR.bf16(4 * 512).rearrange("p (a b) -> p a b", b=512)
            glrT = AR.bf16(512)
            PT = AR.bf16(4 * 512).rearrange("p (a b) -> p a b", b=512)
            allowed = AR.bf16(8 * 80).rearrange("p (a b) -> p a b", b=80)
            qtl2 = AR.bf16(4 * 128).rearrange("p (j h b) -> p j h b", h=2, b=128)
            ktl2 = AR.bf16(4 * 128).rearrange("p (j h b) -> p j h b", h=2, b=128)
            khat = AR.bf16(2 * 128).rearrange("p (a b) -> p a b", b=128)
            ATm = AR.bf16(4 * 128).rearrange("p (a b) -> p a b", b=128)
            S_b = AR.bf16(2 * 128).rearrange("p (a b) -> p a b", b=128)
            ksum_xf = AR.bf16(64)
            ksum_x3 = ksum_xf.rearrange("p (h b) -> p h b", b=8)
            ksum_x4 = ksum_xf.rearrange("p (j e b) -> p j e b", e=2, b=8)
            sp_tok = AR.f32(4 * 256).rearrange("p (a b) -> p a b", b=256)
            etmp = AR.f32(1 * 256).rearrange("p (a b) -> p a b", b=256)
            tmpS = AR.f32(2 * 256).rearrange("p (a b) -> p a b", b=256)
            gm = AR.f32(64)
            top8 = AR.f32(64)
            rl_sb = AR.f32(1 * 512).rearrange("p (a b) -> p a b", b=512)
            eb = AR.f32(2 * 128).rearrange("p (a b) -> p a b", b=128)
            enb = AR.f32(2 * 128).rearrange("p (a b) -> p a b", b=128)
            erev = AR.f32(2 * 128).rearrange("p (a b) -> p a b", b=128)
            o_r = AR.f32(4 * 128).rearrange("p (a b) -> p a b", b=128)
            S_f = AR.f32(2 * 128).rearrange("p (a b) -> p a b", b=128)
            ksum_f = AR.f32(32).rearrange("p (a b) -> p a b", b=8)
            gstat = AR.f32(48).rearrange("p (a b) -> p a b", b=6)

            def norm_stages(g):
                out_ = []
                hs0 = (g % 2) * 512
                for t in range(4):
                    tt = g * 4 + t
                    xs = tt % 2
                    box = {}

                    def n1(tt=tt, xs=xs, box=box):
                        S.dma('sp', 'xt%d' % xs, xt[:, xs, :], x_d[s, tt * 128:(tt + 1) * 128, :], (), ['xt%d' % xs])
                        box['r'] = norm_part1(xt[:, xs, :], 'xt%d' % xs)

                    def n2(t=t, box=box, g=g, hs0=hs0):
                        norm_part2(box['r'], gmix, hs0 + t * 128, 'hTa%d_%d' % (g % 2, t))
                    out_ += [n1, n2]
                return out_

            def mask_stages(g):
                out_ = []
                for t in range(4):
                    B = (4 * g + t) // 2
                    tsl_ = slice(t * 128, (t + 1) * 128)

                    def m1(B=B, tsl_=tsl_):
                        for h in range(8):
                            S.matmul(pb[6][:, h * 8:(h + 1) * 8], QTx[:, h, tsl_], ksum_x3[:, h, :], True, True,
                                     ['QTq', 'QTm', 'ksum_x'], ['b6'])
                        S.tt('dve', gm, pb[6][:, 0:64], pbias[:, B * 64:(B + 1) * 64], ALU.add, ['b6', 'cst'], ['gm'])
                        for h in range(8):
                            S.op('dve', lambda e, h=h: e.max(top8[:, h * 8:(h + 1) * 8], gm[:, h * 8:(h + 1) * 8]),
                                 ['gm'], ['top8'])
                        for h in range(8):
                            S.ts('dve', allowed[:, h, 64:72], gm[:, h * 8:(h + 1) * 8], top8[:, h * 8 + 3:h * 8 + 4],
                                 ALU.is_ge, ['gm', 'top8'], ['allowed'], s2=64.0, op1=ALU.mult)

                    def m2(tsl_=tsl_):
                        for hg in range(2):
                            for h4 in range(4):
                                S.matmul(pb[7][0:73, h4 * 128:(h4 + 1) * 128], allowed[:, hg * 4 + h4, 0:73], ident_b[:],
                                         True, True, ['allowed', 'cstb'], ['b7'])
                            S.copy('act', QTx[64:73, hg * 4:(hg + 1) * 4, tsl_],
                                   pb[7][64:73, :].rearrange("p (a b) -> p a b", b=128), ['b7'], ['QTm'])
                    out_ += [m1, m2]
                return out_

            def gla_stages(g):
                lists = []
                for t in range(4):
                    n = 4 * g + t
                    tsl_ = slice(t * 128, (t + 1) * 128)
                    pair_lists = []
                    for j in range(2):
                        Bk = pb[1 + j]
                        bk = 'b%d' % (1 + j)
                        js = slice(j * 128, (j + 1) * 128)
                        box = {}

                        def g1(t=t, j=j, Bk=Bk, bk=bk, js=js):
                            S.matmul(Bk[:, 0:128], sp_tok[:, t, js], triU, True, True, ['sp_tok', 'cst'], [bk])
                            S.matmul(Bk[:, 128:256], strictL, sp_tok[:, t, js], True, True, ['sp_tok', 'cst'], [bk])

                        def g2(j=j, Bk=Bk, bk=bk):
                            S.act(eb[:, j, :], Bk[:, 0:128], AF.Exp, [bk], ['eb%d' % j])
                            S.act(enb[:, j, :], Bk[:, 0:128], AF.Exp, [bk], ['enb%d' % j], scale=-1.0)
                            S.act(erev[:, j, :], Bk[:, 128:256], AF.Exp, [bk], ['erev%d' % j])

                        def g3(t=t, j=j, js=js, tsl_=tsl_):
                            for hh in range(2):
                                rr = slice(hh * 64, (hh + 1) * 64)
                                S.tt('dve', qtl2[rr, j, hh, :], QGT[rr, j, tsl_], eb[rr, j, :], ALU.mult,
                                     ['QGT', 'eb%d' % j], ['qtl%d' % j])
                                S.tt('dve', ktl2[rr, j, hh, :], KGT[rr, j, tsl_], enb[rr, j, :], ALU.mult,
                                     ['KGT', 'enb%d' % j], ['ktl%d' % j])
                            S.tt('dve', khat[:, j, :], KGtok[:, t, js], erev[:, j, :], ALU.mult,
                                 ['KGtok', 'erev%d' % j], ['khat%d' % j])

                        def g4(j=j, Bk=Bk, bk=bk):
                            for hh in range(2):
                                S.matmul(Bk[:, 256 + hh * 128:384 + hh * 128], ktl2[:, j, hh, :], qtl2[:, j, hh, :],
                                         True, True, ['ktl%d' % j, 'qtl%d' % j], [bk])

                        def g5(j=j, Bk=Bk, bk=bk):
                            for hh in range(2):
                                S.tt('dve', ATm[:, 2 * j + hh, :], Bk[:, 256 + hh * 128:384 + hh * 128], causal_b[:],
                                     ALU.mult, [bk, 'cstb'], ['ATm%d' % (2 * j + hh)])

                        def g6(t=t, j=j, Bk=Bk, bk=bk):
                            for hh in range(2):
                                head = 2 * j + hh
                                S.matmul(Bk[:, hh * 128:(hh + 1) * 128], qtl2[:, j, hh, :], S_b[:, j, :], True, False,
                                         ['qtl%d' % j, 'S_b%d' % j], [bk])
                                S.matmul(Bk[:, hh * 128:(hh + 1) * 128], ATm[:, head, :],
                                         VGtok[:, t, head * 128:(head + 1) * 128], False, True,
                                         ['ATm%d' % head, 'VGtok'], [bk])
                            for hh in range(2):
                                head = 2 * j + hh
                                S.matmul(Bk[:, 256 + hh * 128:384 + hh * 128], khat[:, j, :],
                                         VGtok[:, t, head * 128:(head + 1) * 128], True, True,
                                         ['khat%d' % j, 'VGtok'], [bk])

                        def g7(j=j, Bk=Bk, bk=bk, box=box):
                            sl = rot('gst', 8)
                            k = 'gst%d' % sl
                            box['sl'] = sl
                            for hh in range(2):
                                S.act(junk[:, 0:128], Bk[:, hh * 128:(hh + 1) * 128], AF.Square, [bk], [k],
                                      accum_out=gstat[:, sl, hh:hh + 1])
                            S.act(gstat[:, sl, 2:4], gstat[:, sl, 0:2], AF.Ln, [k], [k], bias=EPS, scale=1.0 / 128)
                            S.act(gstat[:, sl, 4:6], gstat[:, sl, 2:4], AF.Exp, [k], [k], scale=-0.5)
                            for hh in range(2):
                                rr = slice(hh * 64, (hh + 1) * 64)
                                S.stt(S_f[rr, j, :], S_f[rr, j, :], eb[rr, j, 127:128],
                                      Bk[rr, 256 + hh * 128:384 + hh * 128], ALU.mult, ALU.add,
                                      ['S_f%d' % j, 'eb%d' % j, bk], ['S_f%d' % j])
                            S.copy('act', S_b[:, j, :], S_f[:, j, :], ['S_f%d' % j], ['S_b%d' % j])

                        def g8(j=j, Bk=Bk, bk=bk, box=box):
                            sl = box['sl']
                            for hh in range(2):
                                S.ts('dve', o_r[:, 2 * j + hh, :], Bk[:, hh * 128:(hh + 1) * 128],
                                     gstat[:, sl, 4 + hh:5 + hh], ALU.mult, [bk, 'gst%d' % sl], ['o_r%d' % (2 * j + hh)])

                        def g9(j=j, Bk=Bk, bk=bk):
                            for hh in range(2):
                                S.transpose(Bk[:, 256 + hh * 128:384 + hh * 128], o_r[:, 2 * j + hh, :], ident_f,
                                            ['o_r%d' % (2 * j + hh), 'cst'], [bk])

                        def g10(j=j, Bk=Bk, bk=bk, n=n, tsl_=tsl_):
                            for hh in range(2):
                                head = 2 * j + hh
                                S.stt(mixT[:, 4 + head, n * 128:(n + 1) * 128], Bk[:, 256 + hh * 128:384 + hh * 128],
                                      gnorm, silu_rg[:, head, tsl_], ALU.mult, ALU.mult, [bk, 'cst', 'silu_rg'],
                                      ['mix%d' % (4 + head)])
                        pair_lists.append([g1, g2, g3, g4, g5, g6, g7, g8, g9, g10])
                    for a_, b_ in zip(pair_lists[0], pair_lists[1]):
                        lists.append(lambda a_=a_, b_=b_: (a_(), b_()))
                return lists

            def drain(lst, n):
                for _ in range(min(n, len(lst))):
                    lst.pop(0)()

            def proj_fm(w, wk, hkeys, hs, cb, bi):
                for k in range(8):
                    S.matmul(pb[bi][:, :], w[:, k, cb * 128:(cb + 1) * 128], hT[:, k, hs], k == 0, k == 7,
                             hkeys + [wk], ['b%d' % bi])

            def proj_tm(w, wk, hkey, hcol, bi, c0, c1):
                for k in range(8):
                    S.matmul(pb[bi][:, 0:c1 - c0], hT[:, k, hcol:hcol + 128], w[:, k, c0:c1], k == 0, k == 7,
                             [hkey, wk], ['b%d' % bi])

            first_norm = norm_stages(0)
            first_norm[0]()
            first_norm[2]()
            for j in range(4):
                S.memset('dve', VA[:, :, j, 64:128], 1.0, ['VA'])
            S.memset('dve', QTx[:, :, :], 0.0, ['QTq', 'QTm'])
            S.memset('dve', glrT[0:32, :], 1.0, ['glrT'])
            S.memset('dve', allowed[:, :, :], 0.0, ['allowed'])
            S.ts('dve', allowed[:, :, 72:73], c31.unsqueeze(2), -64.0, ALU.add, ['cst', 'allowed'], ['allowed'])
            S.memset('dve', qtl2[:, :, :, :], 0.0, ['qtl0', 'qtl1'])
            S.memset('dve', ktl2[:, :, :, :], 0.0, ['ktl0', 'ktl1'])
            S.memset('dve', S_f[:, :, :], 0.0, ['S_f0', 'S_f1'])
            S.memset('dve', S_b[:, :, :], 0.0, ['S_b0', 'S_b1'])
            S.memset('dve', ksum_f[:, :, :], 0.0, ['ksum_f'])
            S.memset('dve', ksum_xf, 0.0, ['ksum_x'])

            for i_ in (1, 4, 3, 6, 5, 7):
                first_norm[i_]()
            for g in range(NG):
                hs0 = (g % 2) * 512
                hs = slice(hs0, hs0 + 512)
                hkeys = ['hTa%d_%d' % (g % 2, t) for t in range(4)]
                gs = slice(g * 512, (g + 1) * 512)
                S.checkpoint('A_norm')
                ns = norm_stages(g + 1) if g + 1 < NG else []
                for k in range(8):
                    S.matmul(pb[5][0:16, :], wglr[:, k, :], hT[:, k, hs], k == 0, k == 7, hkeys + ['cstb'], ['b5'])
                S.copy('act', glrT[0:16, :], pb[5][0:16, :], ['b5'], ['glrT'])
                w, wk = wget()
                for cb in range(4):
                    bi = 1 + cb % 2
                    proj_fm(w, wk, hkeys, hs, cb, bi)
                    S.act(QTx[0:64, 2 * cb, :], pb[bi][0:64, :], AF.Copy, ['b%d' % bi], ['QTq'], scale=0.125)
                    S.ts('dve', QTx[0:64, 2 * cb + 1, :], pb[bi][64:128, :], 0.125, ALU.mult, ['b%d' % bi], ['QTq'])
                for t in range(4):
                    bi = 5 + t % 2
                    S.matmul(pb[bi][:, 0:256], glrT[0:17, t * 128:(t + 1) * 128], w_aug[0:17, :], True, True,
                             ['glrT', 'cstb'], ['b%d' % bi])
                    S.act(etmp[:, 0, :], pb[bi][:, 0:256], AF.Exp, ['b%d' % bi], ['etmp0'], scale=-1.0)
                    S.act(sp_tok[:, t, :], etmp[:, 0, :], AF.Ln, ['etmp0'], ['sp_tok'], bias=1.0)
                w, wk = wget()
                for cb in range(4):
                    bi = 1 + cb % 2
                    proj_fm(w, wk, hkeys, hs, cb, bi)
                    S.op('dve', lambda e, a=(ksum_f[:, cb, 2 * g:2 * g + 2],
                                             pb[bi][:, :].rearrange("p (b k) -> p b k", k=256)):
                         e.reduce_sum(a[0], a[1], AXL.X), ['b%d' % bi], ['ksum_f'])
                    S.copy('act', KTx[0:64, 2 * cb, gs], pb[bi][0:64, :], ['b%d' % bi], ['KTx'])
                    S.copy('dve', KTx[0:64, 2 * cb + 1, gs], pb[bi][64:128, :], ['b%d' % bi], ['KTx'])
                S.ts('dve', ksum_x4[0:64, :, 0, :], ksum_f[0:64, :, :], 1.0 / 256, ALU.mult, ['ksum_f'], ['ksum_x'])
                S.ts('dve', ksum_x4[0:64, :, 1, :], ksum_f[64:128, :, :], 1.0 / 256, ALU.mult, ['ksum_f'], ['ksum_x'])
                ms = mask_stages(g)
                drain(ms, 1)
                drain(ns, 1)
                w, wk = wget()
                for t in range(4):
                    bi = 1 + t % 2
                    proj_tm(w, wk, hkeys[t], hs0 + t * 128, bi, 0, 512)
                    pv4 = pb[bi][:, :].rearrange("p (j e d) -> p j e d", e=2, d=64)
                    S.copy('act', VA[:, g * 4 + t, :, 0:64], pv4[:, :, 0, :], ['b%d' % bi], ['VA'])
                    S.copy('dve', VA[:, g * 4 + t, :, 128:192], pv4[:, :, 1, :], ['b%d' % bi], ['VA'])
                drain(ms, 2)
                drain(ns, 2)
                w, wk = wget()
                for cb in range(4):
                    bi = 1 + cb % 2
                    proj_fm(w, wk, hkeys, hs, cb, bi)
                    if cb < 2:
                        S.act(QGT[:, cb, :], pb[bi][:, :], AF.Copy, ['b%d' % bi], ['QGT'], scale=0.125)
                    else:
                        S.copy('dve', KGT[:, cb - 2, :], pb[bi][:, :], ['b%d' % bi], ['KGT'])
                for t in range(4):
                    bi = 1 + t % 2
                    proj_tm(w, wk, hkeys[t], hs0 + t * 128, bi, 256, 512)
                    evac_copy(KGtok[:, t, :], pb[bi][:, 0:256], ['b%d' % bi], ['KGtok'])
                drain(ms, 2)
                drain(ns, 2)
                w, wk = wget()
                for t in range(4):
                    bi = 1 + t % 2
                    proj_tm(w, wk, hkeys[t], hs0 + t * 128, bi, 0, 512)
                    evac_copy(VGtok[:, t, :], pb[bi][:, :], ['b%d' % bi], ['VGtok'])
                drain(ms, 2)
                drain(ns, 2)
                w, wk = wget()
                for cb in range(4):
                    bi = 1 + cb % 2
                    proj_fm(w, wk, hkeys, hs, cb, bi)
                    S.act(silu_rg[:, cb, :], pb[bi][:, :], AF.Silu, ['b%d' % bi], ['silu_rg'])
                drain(ms, len(ms))
                drain(ns, len(ns))
                S.checkpoint('A_proj')
                S.checkpoint('B_mask')
                deferred = gla_stages(g)
                nkt = 4 * g + 4
                iters = [(h, kt) for h in range(8) for kt in range(nkt)]
                SB = [3, 4, 7]
                OB = [5, 6]

                def emit_qk(idx):
                    h, kt = iters[idx]
                    qlo = max(0, kt - 4 * g) * 128
                    bi = SB[idx % 3]
                    S.matmul(pb[bi][:, qlo:512], KTx[:, h, kt * 128:(kt + 1) * 128], QTx[:, h, qlo:512],
                             True, True, ['KTx', 'QTq', 'QTm', 'cstb'], ['b%d' % bi])

                def emit_exp(idx):
                    h, kt = iters[idx]
                    i0 = kt - 4 * g
                    bi = SB[idx % 3]
                    bk = 'b%d' % bi
                    bank = pb[bi]
                    ps_ = idx % 4
                    ptk = 'PT%d' % ps_
                    qlo = max(0, i0) * 128
                    if i0 >= -1:
                        na = max(0, i0) * 128
                        nb_ = min(i0 + 2, 4) * 128
                        off = 128 if i0 == -1 else 0
                        n = nb_ - na
                        S.tt('dve', bank[:, na:nb_], bank[:, na:nb_], t5b[:, h, off:off + n], ALU.add,
                             [bk, 'cst2'], [bk])
                    S.act(PT[:, ps_, qlo:512], bank[:, qlo:512], AF.Exp, [bk], [ptk])

                def emit_pv(idx):
                    h, kt = iters[idx]
                    j, e_ = h // 2, h % 2
                    qlo = max(0, kt - 4 * g) * 128
                    ob = OB[h % 2]
                    obk = 'b%d' % ob
                    ps_ = idx % 4
                    S.matmul(pb[ob][:, qlo:512], VA[:, kt, j, e_ * 64:e_ * 64 + 128], PT[:, ps_, qlo:512],
                             kt == 0, kt == nkt - 1, ['VA', 'PT%d' % ps_], [obk])
                    if kt == nkt - 1:
                        sl = 0
                        lo, hi = (64, 128) if e_ == 0 else (0, 64)
                        oo, oh = (0, 64) if e_ == 0 else (64, 128)
                        tmpf = tmpS[:, :, :].rearrange("p a b -> p (a b)")
                        recip2(rl_sb[oo:oh, sl, :], pb[ob][lo:hi, :], tmpf[lo:hi, :], [obk], 'rl%d' % sl, 'rscrA')
                        S.tt('dve', mixT[oo:oh, j, gs], pb[ob][oo:oh, :], rl_sb[oo:oh, sl, :], ALU.mult,
                             [obk, 'rl%d' % sl], ['mix%d' % j])

                nit = len(iters)
                n_def = len(deferred)
                emit_qk(0)
                if nit > 1:
                    emit_qk(1)
                for idx in range(nit):
                    if idx + 2 < nit:
                        emit_qk(idx + 2)
                    emit_exp(idx)
                    emit_pv(idx)
                    want = min(n_def, -(-((idx + 1) * n_def) // max(1, (3 * nit) // 4)))
                    drain(deferred, want - (n_def - len(deferred)))
                drain(deferred, len(deferred))
                S.checkpoint('B_attn')
            S.checkpoint('B_gla')
            if debug:
                S.dma('sp', 'dbg', dbg_d, mixT[:].rearrange("p a b -> p (a b)"), ['mix%d' % k for k in range(8)], ['dbg_d'])

            S.barrier()
            AR.reset()
            kxT = AR.bf16(8 * 256).rearrange("p (a b) -> p a b", b=256)
            vx = AR.bf16(2 * 1024).rearrange("p (a b) -> p a b", b=1024)
            qxT = AR.bf16(2 * GC).rearrange("p (a b) -> p a b", b=GC)
            big = AR.bf16(8 * GC).rearrange("p (a b) -> p a b", b=GC)
            PTx = AR.bf16(2 * 512).rearrange("p (a b) -> p a b", b=512)
            rtmp = AR.bf16(2 * 512).rearrange("p (a b) -> p a b", b=512)
            x1 = AR.f32(TC * D).rearrange("p (a b) -> p a b", b=D)
            rlx = AR.f32(512)
            rscr = AR.f32(512)
            gfin = AR.f32(D)
            S.dma('sp', 'gfin', gfin, gfin_d, (), ['gfin'])

            mkeys = ['hT0', 'hT1']
            for mt in range(2):
                xs = mt
                S.dma('sp', 'xt%d' % xs, xt[:, xs, :], mem_d[s, mt * 128:(mt + 1) * 128, :], (), ['xt%d' % xs])
                norm_to_hT(xt[:, xs, :], 'xt%d' % xs, gmem, mt * 128, mkeys[mt])
            for c in range(4):
                w, wk = wget()
                if c < 2:
                    for cb in range(4):
                        bi = 1 + cb % 2
                        for k in range(8):
                            S.matmul(pb[bi][:, 0:256], w[:, k, cb * 128:(cb + 1) * 128], hT[:, k, 0:256], k == 0, k == 7,
                                     mkeys + [wk], ['b%d' % bi])
                        evac_copy(kxT[:, c * 4 + cb, :], pb[bi][:, 0:256], ['b%d' % bi], ['kxT'])
                else:
                    for mt in range(2):
                        bi = 1 + mt % 2
                        for k in range(8):
                            S.matmul(pb[bi][:, :], hT[:, k, mt * 128:(mt + 1) * 128], w[:, k, :], k == 0, k == 7,
                                     [mkeys[mt], wk], ['b%d' % bi])
                        evac_copy(vx[:, mt, (c - 2) * 512:(c - 1) * 512], pb[bi][:, :], ['b%d' % bi], ['vx'])

            S.checkpoint('C_mem')
            for cg in range(NCG):
                tok0 = cg * GC
                hk = ['hT%d' % t for t in range(TC)]
                mixk = ['mix%d' % k for k in range(8)]
                for t in range(TC):
                    S.dma('sp', 'x1_%d' % t, x1[:, t, :], x_d[s, tok0 + t * 128:tok0 + (t + 1) * 128, :], (), ['x1_%d' % t])
                wA = wget()
                wB = wget(ahead=1)
                pend = None
                for t in range(TC):
                    for c, (w, wk) in enumerate((wA, wB)):
                        bi = 3 + (2 * t + c) % 4
                        for k in range(8):
                            S.matmul(pb[bi][:, :], mixT[:, k, tok0 + t * 128:tok0 + (t + 1) * 128], w[:, k, :],
                                     k == 0, k == 7, mixk + [wk], ['b%d' % bi])
                        S.tt('dve', x1[:, t, c * 512:(c + 1) * 512], pb[bi][:, :], x1[:, t, c * 512:(c + 1) * 512],
                             ALU.add, ['b%d' % bi, 'x1_%d' % t], ['x1_%d' % t])
                    r_ = norm_part1(x1[:, t, :], 'x1_%d' % t)
                    if pend is not None:
                        norm_part2(*pend)
                    pend = (r_, gxat, t * 128, hk[t])
                norm_part2(*pend)
                S.checkpoint('C_out')
                def xq_proj(xh, w, wk):
                    for c2 in range(2):
                        cb = (xh % 2) * 2 + c2
                        for hf in range(GC // 512):
                            bi = 1 + hf % 2
                            for k in range(8):
                                S.matmul(pb[bi][:, :], w[:, k, cb * 128:(cb + 1) * 128], hT[:, k, hf * 512:(hf + 1) * 512],
                                         k == 0, k == 7, hk + [wk], ['b%d' % bi])
                            S.act(qxT[:, c2, hf * 512:(hf + 1) * 512], pb[bi][:, :], AF.Copy, ['b%d' % bi], ['qxT'],
                                  scale=1.0 / 16)

                NHF = GC // 512
                STB = [(5, 6), (3, 4)]
                w, wk = wget()
                xq_proj(0, w, wk)
                for xh in range(4):
                    for hf in range(NHF):
                        hs = slice(hf * 512, (hf + 1) * 512)
                        for mt in range(2):
                            bi = STB[hf % 2][mt]
                            for c2 in range(2):
                                S.matmul(pb[bi][:, :], kxT[:, 2 * xh + c2, mt * 128:(mt + 1) * 128], qxT[:, c2, hs],
                                         c2 == 0, c2 == 1, ['kxT', 'qxT'], ['b%d' % bi])
                    for hf in range(NHF):
                        hs = slice(hf * 512, (hf + 1) * 512)
                        for mt in range(2):
                            bi = STB[hf % 2][mt]
                            S.act(PTx[:, mt, :], pb[bi][:, :], AF.Exp, ['b%d' % bi], ['PTx%d' % mt])
                        if hf == 0 and xh + 1 < 4:
                            if (xh + 1) % 2 == 0:
                                w, wk = wget()
                            xq_proj(xh + 1, w, wk)
                        for mt in range(2):
                            S.matmul(pb[7][:, :], ones_b[:], PTx[:, mt, :], mt == 0, mt == 1, ['ones_b', 'PTx%d' % mt], ['b7'])
                        recip2(rlx, pb[7][:, :], rscr, ['b7'], 'rlx', 'rscr')
                        for c2 in range(2):
                            bi = 1 + c2
                            for mt in range(2):
                                S.matmul(pb[bi][:, :], vx[:, mt, (2 * xh + c2) * 128:(2 * xh + c2 + 1) * 128], PTx[:, mt, :],
                                         mt == 0, mt == 1, ['vx', 'PTx%d' % mt], ['b%d' % bi])
                            S.tt('dve', big[:, 2 * xh + c2, hs], pb[bi][:, :], rlx, ALU.mult, ['b%d' % bi, 'rlx'], ['big'])
                wA = wget()
                wB = wget(ahead=1)
                pend = None
                for t in range(TC):
                    for c, (w, wk) in enumerate((wA, wB)):
                        bi = 3 + (2 * t + c) % 4
                        for k in range(8):
                            S.matmul(pb[bi][:, :], big[:, k, t * 128:(t + 1) * 128], w[:, k, :], k == 0, k == 7,
                                     ['big', wk], ['b%d' % bi])
                        S.tt('dve', x1[:, t, c * 512:(c + 1) * 512], pb[bi][:, :], x1[:, t, c * 512:(c + 1) * 512],
                             ALU.add, ['b%d' % bi, 'x1_%d' % t], ['x1_%d' % t])
                    r_ = norm_part1(x1[:, t, :], 'x1_%d' % t)
                    if pend is not None:
                        norm_part2(*pend)
                    pend = (r_, gmlp, t * 128, hk[t])
                norm_part2(*pend)
                S.checkpoint('C_xattn')
                for q in range(4):
                    for c in range(2):
                        w, wk = wget()
                        for cb in range(4):
                            for hf in range(GC // 512):
                                bi = 1 + rot('acc', 2)
                                for k in range(8):
                                    S.matmul(pb[bi][:, :], w[:, k, cb * 128:(cb + 1) * 128], hT[:, k, hf * 512:(hf + 1) * 512],
                                             k == 0, k == 7, hk + [wk], ['b%d' % bi])
                                rsl = rot('ts', 2)
                                S.act(rtmp[:, rsl, :], pb[bi][:, :], AF.Relu, ['b%d' % bi], ['rtmp%d' % rsl])
                                S.tt('dve', big[:, c * 4 + cb, hf * 512:(hf + 1) * 512], rtmp[:, rsl, :], rtmp[:, rsl, :],
                                     ALU.mult, ['rtmp%d' % rsl], ['big'])
                    for c in range(2):
                        w, wk = wget()
                        for t in range(TC):
                            bi = 3 + t % 4
                            for k in range(8):
                                S.matmul(pb[bi][:, :], big[:, k, t * 128:(t + 1) * 128], w[:, k, :], k == 0, k == 7,
                                         ['big', wk], ['b%d' % bi])
                            S.tt('dve', x1[:, t, c * 512:(c + 1) * 512], pb[bi][:, :], x1[:, t, c * 512:(c + 1) * 512],
                                 ALU.add, ['b%d' % bi, 'x1_%d' % t], ['x1_%d' % t])
                S.checkpoint('C_mlp')
                for t in range(TC):
                    rs, rk = rstd_of(x1[:, t, :], 'x1_%d' % t, D)
                    xs = rot('xs', 2)
                    S.stt(xt[:, xs, :], x1[:, t, :], rs, gfin, ALU.mult, ALU.mult, ['x1_%d' % t, rk, 'gfin'], ['xt%d' % xs])
                    S.dma('sp', 'out%d' % xs, out_d[s, tok0 + t * 128:tok0 + (t + 1) * 128, :], xt[:, xs, :],
                          ['xt%d' % xs], ['out_d%d' % xs])
            S.barrier()

        final_keys = ['out_d0', 'out_d1'] + (['dbg_d'] if debug else [])
        S.wait_all('sp', final_keys)
        S.barrier()
        assert S.stopped or wstate['i'] == len(wplan), (wstate['i'], len(wplan))
        with nc.allow_non_contiguous_dma(reason="strided weight / constant loads"):
            S.emit()
    return nc


def _rel_bucket_np(d):
    d = np.maximum(d, 0)
    large = 16 + (np.log(np.maximum(d, 1).astype(np.float32) / np.float32(16)) / np.float32(math.log(128 / 16))
                  * np.float32(16)).astype(np.int32)
    large = np.minimum(large, 31)
    return np.where(d < 16, d, large)


def _consts(T=2048):
    cst = np.zeros((128, 1024), np.float32)
    i = np.arange(128)
    cst[:, 0:128] = np.eye(128, dtype=np.float32)
    cst[:, 128:256] = np.where(i[:, None] <= i[None, :], -1.0 / 16, 0.0)
    cst[:, 256:384] = np.where(i[:, None] > i[None, :], -1.0 / 16, 0.0)
    cst[:, 384:512] = np.where(i[:, None] <= i[None, :], 1.0, 0.0)
    for B in range(8):
        row = np.where(np.arange(8) < B, 0.0, np.where(np.arange(8) == B, 64.0, -64.0)).astype(np.float32)
        cst[:, 512 + B * 64:512 + (B + 1) * 64] = np.tile(row, 8)[None, :]
    blk = np.zeros((9, T), np.float32)
    kpos = np.arange(T)
    for b in range(8):
        blk[b, :] = (kpos // 256 == b)
    blk[8, :] = 1.0
    return cst, blk


def _host_inputs(rp_table, norm_mix, g_norm, norm_xattn, norm_mem, norm_mlp, norm_final, T=2048):
    cst, blk = _consts(T)
    vecs = np.zeros((128, 48), np.float32)
    vecs[:, 0:8] = np.asarray(norm_mix, np.float32).reshape(8, 128).T
    vecs[:, 8:16] = np.asarray(norm_xattn, np.float32).reshape(8, 128).T
    vecs[:, 16:24] = np.asarray(norm_mlp, np.float32).reshape(8, 128).T
    vecs[:, 24:32] = np.asarray(norm_mem, np.float32).reshape(8, 128).T
    vecs[:, 32] = np.asarray(g_norm, np.float32).reshape(128)
    rp = np.asarray(rp_table, np.float32)
    vecs[:, 33:41] = rp[31][None, :]
    i = np.arange(128)[:, None]
    jj = np.arange(256)[None, :]
    dist = jj - i
    idx = _rel_bucket_np(dist)
    t5 = rp[idx]
    t5 = np.where((dist >= 0)[:, :, None], t5, np.float32(-30000.0)).astype(np.float32)
    t5 = np.ascontiguousarray(t5.transpose(0, 2, 1)).reshape(128, 2048)
    gfin = np.ascontiguousarray(np.broadcast_to(np.asarray(norm_final, np.float32).reshape(1, D), (128, D)))
    return cst, blk, vecs, t5, gfin


_NC_CACHE = {}


def kernel(x, mem, rp_table, norm_mix, w_in, w_gate_up, b_gate, g_norm, w_out, norm_xattn, norm_mem, w_xq, w_xkv,
           w_xo, norm_mlp, w_up, w_down, norm_final):
    x = np.asarray(x, np.float32)
    mem = np.asarray(mem, np.float32)
    Bt, T, _ = x.shape
    nseq = Bt // NCORES
    cst, blk, vecs, t5, gfin = _host_inputs(rp_table, norm_mix, g_norm, norm_xattn, norm_mem, norm_mlp, norm_final, T)
    key = (nseq, T)
    if key not in _NC_CACHE:
        _NC_CACHE[key] = build(nseq, T)
    nc = _NC_CACHE[key]
    f = lambda a: np.ascontiguousarray(np.asarray(a, np.float32))
    shared = {
        "w_in": f(w_in[0]), "w_gate_up": f(w_gate_up[0]), "b_gate": f(b_gate[0]).reshape(1, 256),
        "w_out": f(w_out[0]), "w_xq": f(w_xq[0]), "w_xkv": f(w_xkv[0]), "w_xo": f(w_xo[0]),
        "w_up": f(w_up[0]), "w_down": f(w_down[0]),
        "cst": cst, "blkind": blk, "vecs": vecs, "t5b": t5, "gfin": gfin,
    }
    in_maps = []
    for c in range(NCORES):
        m = dict(shared)
        m["x"] = np.ascontiguousarray(x[c * nseq:(c + 1) * nseq])
        m["mem"] = np.ascontiguousarray(mem[c * nseq:(c + 1) * nseq])
        in_maps.append(m)
    res = run_bass_kernel_spmd(nc, in_maps, core_ids=list(range(NCORES)))
    out = np.concatenate([np.asarray(r["out"], np.float32) for r in res.results], axis=0)
    return out
```

```python
import math
from contextlib import ExitStack

import numpy as np
import concourse.bass as bass
import concourse.mybir as mybir
from concourse.bass_utils import run_bass_kernel_spmd

F32 = mybir.dt.float32
BF16 = mybir.dt.bfloat16
AF = mybir.ActivationFunctionType
ALU = mybir.AluOpType
AXL = mybir.AxisListType

D = 1024
MEM = 256
IN_COLS = 3088
DFF = 4096
EPS = 1e-6
NCORES = 8

ENGS = ['pe', 'act', 'dve', 'pool', 'sp']
PSUM_KEYS = set('b%d' % i for i in range(8))
SAME_ENG_SYNC = {'pe': False, 'act': True, 'dve': True, 'pool': True, 'sp': False}


class Sched:
    def __init__(self, nc, st):
        self.nc = nc
        self.st = st
        self.semobj = {}
        for e in ENGS:
            self.semobj[e] = st.enter_context(nc.semaphore('q_' + e))
        self.cnt = {e: 0 for e in ENGS}
        self.known = {e: {} for e in ENGS}
        self.prog = {e: [] for e in ENGS}
        self.lastw = {}
        self.readers = {}
        self.dmacnt = {}
        self.stopped = False
        self.stop_at = None

    def checkpoint(self, label):
        if self.stop_at is not None and label == self.stop_at:
            self.stopped = True

    def _sem(self, key):
        if key not in self.semobj:
            self.semobj[key] = self.st.enter_context(self.nc.semaphore('d_' + key))
            self.dmacnt[key] = 0
        return self.semobj[key]

    def _deps(self, reads, writes, eng=None):
        deps = {}

        def add(s, v):
            if deps.get(s, 0) < v:
                deps[s] = v
        for r in reads:
            w = self.lastw.get(r)
            if w is not None:
                add(*w)
            if r in PSUM_KEYS:
                for s_, v in self.readers.get(r, {}).items():
                    if s_ != eng:
                        add(s_, v)
        for w_ in writes:
            w = self.lastw.get(w_)
            if w is not None:
                add(*w)
            for s, v in self.readers.get(w_, {}).items():
                add(s, v)
        return deps

    def _waits(self, eng, deps):
        waits = []
        for s, v in deps.items():
            if s == eng and not SAME_ENG_SYNC[eng]:
                continue
            if self.known[eng].get(s, 0) < v:
                waits.append((s, v))
                self.known[eng][s] = v
        return waits

    def _mark(self, reads, writes, s, v):
        for r in reads:
            d = self.readers.setdefault(r, {})
            if d.get(s, 0) < v:
                d[s] = v
        for w in writes:
            self.lastw[w] = (s, v)
            self.readers[w] = {}

    def op(self, eng, fn, reads=(), writes=()):
        if self.stopped:
            return
        deps = self._deps(reads, writes, eng)
        waits = self._waits(eng, deps)
        self.cnt[eng] += 1
        v = self.cnt[eng]
        self.prog[eng].append((waits, fn, eng, 1))
        self._mark(reads, writes, eng, v)

    def dma(self, q, semkey, out, in_, reads=(), writes=()):
        if self.stopped:
            return
        self._sem(semkey)
        deps = self._deps(reads, writes)
        waits = self._waits(q, deps)
        self.dmacnt[semkey] += 16
        v = self.dmacnt[semkey]
        fn = (lambda e, o=out, i=in_: e.dma_start(out=o, in_=i))
        self.prog[q].append((waits, fn, semkey, 16))
        self._mark(reads, writes, semkey, v)

    def wait_all(self, eng, keys):
        deps = self._deps(keys, ())
        waits = self._waits(eng, deps)
        if waits:
            self.prog[eng].append((waits, None, None, 0))

    def barrier(self):
        for e in ENGS:
            deps = {}
            for e2 in ENGS:
                if e2 != e and e2 != 'sp' and self.cnt[e2] > 0:
                    deps[e2] = self.cnt[e2]
            for k, v in self.dmacnt.items():
                if v > 0:
                    deps[k] = v
            waits = self._waits(e, deps)
            if waits:
                self.prog[e].append((waits, None, None, 0))

    def replay(self, name, e):
        for waits, fn, semname, inc in self.prog[name]:
            for s, v in waits:
                e.wait_ge(self.semobj[s], v)
            if fn is None:
                continue
            inst = fn(e)
            inst.then_inc(self.semobj[semname], inc)

    def emit(self):
        nc = self.nc
        with nc.Block() as block:
            @block.tensor
            def _(e):
                self.replay('pe', e)

            @block.scalar
            def _(e):
                self.replay('act', e)

            @block.vector
            def _(e):
                self.replay('dve', e)

            @block.gpsimd
            def _(e):
                self.replay('pool', e)

            @block.sync
            def _(e):
                self.replay('sp', e)

    def matmul(self, out, lhsT, rhs, start, stop, reads, writes):
        self.op('pe', lambda e, a=(out, lhsT, rhs, start, stop): e.matmul(a[0], a[1], a[2], start=a[3], stop=a[4]),
                reads, writes)

    def transpose(self, out, in_, ident, reads, writes):
        self.op('pe', lambda e, a=(out, in_, ident): e.transpose(a[0], a[1], a[2]), reads, writes)

    def act(self, out, in_, func, reads, writes, bias=None, scale=None, accum_out=None):
        kw = {}
        if bias is not None:
            kw['bias'] = bias
        if scale is not None:
            kw['scale'] = scale
        if accum_out is not None:
            kw['accum_out'] = accum_out
        self.op('act', lambda e, a=(out, in_, func), kw=kw: e.activation(a[0], a[1], a[2], **kw), reads, writes)

    def tt(self, eng, out, in0, in1, op, reads, writes):
        self.op(eng, lambda e, a=(out, in0, in1, op): e.tensor_tensor(a[0], a[1], a[2], a[3]), reads, writes)

    def ts(self, eng, out, in0, s1, op0, reads, writes, s2=None, op1=None):
        kw = {}
        if op1 is not None:
            kw['op1'] = op1
        self.op(eng, lambda e, a=(out, in0, s1, s2, op0), kw=kw: e.tensor_scalar(a[0], a[1], a[2], a[3], a[4], **kw),
                reads, writes)

    def stt(self, out, in0, scalar, in1, op0, op1, reads, writes):
        self.op('dve', lambda e, a=(out, in0, scalar, in1, op0, op1):
                e.scalar_tensor_tensor(a[0], a[1], a[2], a[3], a[4], a[5]), reads, writes)

    def copy(self, eng, out, in_, reads, writes):
        if eng == 'act':
            self.op('act', lambda e, a=(out, in_): e.copy(a[0], a[1]), reads, writes)
        else:
            self.op(eng, lambda e, a=(out, in_): e.tensor_copy(a[0], a[1]), reads, writes)

    def memset(self, eng, ap, val, writes):
        self.op(eng, lambda e, a=(ap, val): e.memset(a[0], a[1]), (), writes)


class Arena:
    def __init__(self, tile, ncols):
        self.t = tile
        self.n = ncols
        self.off = 0

    def reset(self):
        self.off = 0

    def f32(self, ncols):
        assert self.off + ncols <= self.n, (self.off, ncols, self.n)
        ap = self.t[:, self.off:self.off + ncols]
        self.off += ncols
        return ap

    def bf16(self, ncols):
        n32 = (ncols + 1) // 2
        assert self.off + n32 <= self.n, (self.off, n32, self.n)
        ap = self.t[:, self.off:self.off + n32].bitcast(BF16)
        self.off += n32
        return ap


def build(nseq=2, T=2048, GC=1024, debug=False, stop_at=None):
    NT = T // 128
    NG = T // 512
    GC = min(GC, T)
    NCG = T // GC
    TC = GC // 128
    nc = bass.Bass("TRN2", target_bir_lowering=False)

    def dt(name, shape, dtype=F32, kind="ExternalInput"):
        return nc.dram_tensor(name, shape, dtype, kind=kind).ap()

    x_d = dt("x", [nseq, T, D])
    mem_d = dt("mem", [nseq, MEM, D])
    w_in_d = dt("w_in", [D, IN_COLS])
    w_gu_d = dt("w_gate_up", [16, 256])
    b_gate_d = dt("b_gate", [1, 256])
    w_out_d = dt("w_out", [D, D])
    w_xq_d = dt("w_xq", [D, D])
    w_xkv_d = dt("w_xkv", [D, 2 * D])
    w_xo_d = dt("w_xo", [D, D])
    w_up_d = dt("w_up", [D, DFF])
    w_down_d = dt("w_down", [DFF, D])
    cst_d = dt("cst", [128, 1024])
    blk_d = dt("blkind", [9, T])
    vecs_d = dt("vecs", [128, 48])
    t5_d = dt("t5b", [128, 2048])
    gfin_d = dt("gfin", [128, D])
    out_d = dt("out", [nseq, T, D], kind="ExternalOutput")
    if debug:
        dbg_d = dt("dbg", [128, 8 * T], BF16, kind="ExternalOutput")

    def wv(w, c0, n):
        return w.rearrange("(k p) c -> p k c", p=128)[:, :, c0:c0 + n]

    with ExitStack() as st:
        S = Sched(nc, st)
        S.stop_at = stop_at

        def sb(name, shape, dtype):
            return st.enter_context(nc.sbuf_tensor("sb_" + name, shape, dtype))

        def psb(name, shape, dtype):
            return st.enter_context(nc.psum_tensor(name, shape, dtype))

        cst = sb("cst", [128, 1024], F32)
        ident_f = cst[:, 0:128]
        triU = cst[:, 128:256]
        strictL = cst[:, 256:384]
        pbias = cst[:, 512:1024]
        ident_b = sb("ident_b", [128, 128], BF16)
        causal_b = sb("causal_b", [128, 128], BF16)
        ones_b = sb("ones_b", [128, 128], BF16)
        ones_f = sb("ones_f", [128, 64], F32)
        KTx = sb("KTx", [128, 8, T], BF16)
        t5b = sb("t5b", [128, 8, 256], F32)
        vecs = sb("vecs", [128, 48], F32)
        gmix, gxat, gmlp, gmem = vecs[:, 0:8], vecs[:, 8:16], vecs[:, 16:24], vecs[:, 24:32]
        gnorm = vecs[:, 32:33]
        c31 = vecs[:, 33:41]
        w_aug = sb("w_aug", [32, 256], BF16)
        wglr = sb("wglr", [128, 8, 16], BF16)
        stat = sb("stat", [128, 48], F32)
        xt = sb("xt", [128, 2, D], F32)
        junk = sb("junk", [128, D], BF16)
        xn = sb("xn", [128, 2, D], BF16)
        hT = sb("hT", [128, 8, 1024], BF16)
        wsl = sb("wsl", [128, 3, 8, 512], BF16)
        mixT = sb("mixT", [128, 8, T], BF16)
        ARENA_COLS = 18496
        arena_t = sb("arena", [128, ARENA_COLS], F32)
        AR = Arena(arena_t, ARENA_COLS)

        pb = [psb("pbank0", [128, 1024], BF16)] + [psb("pbank%d" % i, [128, 512], F32) for i in range(1, 8)]

        S.dma('sp', 'cst', cst[:], cst_d, (), ['cst'])
        S.dma('sp', 'cst', vecs[:], vecs_d, (), ['cst'])
        S.dma('sp', 'cst', t5b[:].rearrange("p a b -> p (a b)"), t5_d, (), ['cst'])
        S.dma('pool', 'cstb', ident_b[:], cst_d[:, 0:128], (), ['cstb'])
        S.dma('pool', 'cstb', causal_b[:], cst_d[:, 384:512], (), ['cstb'])
        S.memset('pool', KTx[64:128, :, :], 0.0, ['KTx'])
        for h in range(8):
            S.dma('pool', 'cstb', KTx[64:73, h, :], blk_d, ['KTx'], ['cstb', 'KTx'])
        S.dma('pool', 'cstb', w_aug[0:16, :], w_gu_d, (), ['cstb'])
        S.dma('pool', 'cstb', w_aug[16:17, :], b_gate_d, (), ['cstb'])
        S.dma('pool', 'cstb', wglr[:], wv(w_in_d, 2560, 16), (), ['cstb'])
        S.memset('dve', ones_b[:], 1.0, ['ones_b'])
        S.memset('dve', ones_f[:], 1.0, ['ones_f'])
        for h in range(8):
            S.ts('dve', t5b[:, h, :], t5b[:, h, :], c31[:, h:h + 1], ALU.subtract, ['cst'], ['cst2'])
        S.checkpoint('const')

        wplan = []
        for s in range(nseq):
            for g in range(NG):
                for c0 in (0, 512, 1024, 1536, 2048, 2576):
                    wplan.append(wv(w_in_d, c0, 512))
            for c in range(4):
                wplan.append(wv(w_xkv_d, c * 512, 512))
            for cg in range(NCG):
                for c in range(2):
                    wplan.append(wv(w_out_d, c * 512, 512))
                for c in range(2):
                    wplan.append(wv(w_xq_d, c * 512, 512))
                for c in range(2):
                    wplan.append(wv(w_xo_d, c * 512, 512))
                for q in range(4):
                    for c in range(2):
                        wplan.append(wv(w_up_d, q * 1024 + c * 512, 512))
                    for c in range(2):
                        wplan.append(w_down_d.rearrange("(q k p) n -> p q k n", k=8, p=128)[:, q, :, c * 512:(c + 1) * 512])
        wstate = {'i': 0, 'issued': 0}

        def wget(ahead=2):
            i = wstate['i']
            while wstate['issued'] <= min(i + ahead, len(wplan) - 1):
                n = wstate['issued']
                S.dma('pool', 'w%d' % (n % 3), wsl[:, n % 3], wplan[n], (), ['w%d' % (n % 3)])
                wstate['issued'] += 1
            wstate['i'] += 1
            return wsl[:, i % 3], 'w%d' % (i % 3)

        MG = [0, 0, 0, 1, 1, 2, 1, 3]
        MA = [0, 64, 32, 64, 0, 64, 32, 64]
        cnt = {'gst': 0, 'stat': 0, 'xs': 0, 'acc': 0, 'sc': 0, 'pt': 0, 'ts': 0, 'ev': 0}

        def rot(name, n):
            v = cnt[name] % n
            cnt[name] += 1
            return v

        def evac_copy(out, in_, reads, writes):
            if rot('ev', 2) == 0:
                S.copy('act', out, in_, reads, writes)
            else:
                S.copy('dve', out, in_, reads, writes)

        def recip2(out, in_, scratch, reads, okey, skey):
            S.act(scratch, in_, AF.Ln, reads, [skey])
            S.act(out, scratch, AF.Exp, [skey], [okey], scale=-1.0)

        def rstd_of(src, skey, nfeat):
            sl = rot('stat', 16)
            k = 'st%d' % sl
            ssq, lnv, rs = stat[:, 3 * sl:3 * sl + 1], stat[:, 3 * sl + 1:3 * sl + 2], stat[:, 3 * sl + 2:3 * sl + 3]
            S.act(junk[:, 0:nfeat], src, AF.Square, [skey], [k], accum_out=ssq)
            S.act(lnv, ssq, AF.Ln, [k], [k], bias=EPS, scale=1.0 / nfeat)
            S.act(rs, lnv, AF.Exp, [k], [k], scale=-0.5)
            return rs, k

        def norm_part1(src, skey):
            rs, k = rstd_of(src, skey, D)
            xs = rot('xs', 2)
            xk = 'xn%d' % xs
            S.act(xn[:, xs, :], src, AF.Copy, [skey, k], [xk], scale=rs)
            return xs, xk

        def norm_part2(r, gcols, col0, hkey):
            xs, xk = r
            for kk in range(8):
                S.transpose(pb[0][:, kk * 128:(kk + 1) * 128], xn[:, xs, kk * 128:(kk + 1) * 128], ident_b[:],
                            [xk, 'cstb'], ['b0'])
            S.tt('dve', hT[:, :, col0:col0 + 128], pb[0][:, :].rearrange("p (a b) -> p a b", b=128),
                 gcols.unsqueeze(2).to_broadcast([128, 8, 128]), ALU.mult, ['b0', 'cst'], [hkey])

        def norm_to_hT(src, skey, gcols, col0, hkey):
            norm_part2(norm_part1(src, skey), gcols, col0, hkey)

        for s in range(nseq):
            AR.reset()
            VA = AR.bf16(NT * 4 * 192).rearrange("p (a h d) -> p a h d", h=4, d=192)
            QTx = AR.bf16(8 * 512).rearrange("p (a b) -> p a b", b=512)
            QGT = AR.bf16(2 * 512).rearrange("p (a b) -> p a b", b=512)
            KGT = AR.bf16(2 * 512).rearrange("p (a b) -> p a b", b=512)
            KGtok = AR.bf16(4 * 256).rearrange("p (a b) -> p a b", b=256)
            VGtok = AR.bf16(4 * 512).rearrange("p (a b) -> p a b", b=512)
            silu_rg = AR.bf16(4 * 512).rearrange("p (a b) -> p a b", b=512)
            glrT = AR.bf16(512)
            PT = AR.bf16(4 * 512).rearrange("p (a b) -> p a b", b=512)
            allowed = AR.bf16(8 * 80).rearrange("p (a b) -> p a b", b=80)
            qtl2 = AR.bf16(4 * 128).rearrange("p (j h b) -> p j h b", h=2, b=128)
            ktl2 = AR.bf16(4 * 128).rearrange("p (j h b) -> p j h b", h=2, b=128)
            khat = AR.bf16(2 * 128).rearrange("p (a b) -> p a b", b=128)
            ATm = AR.bf16(4 * 128).rearrange("p (a b) -> p a b", b=128)
            S_b = AR.bf16(2 * 128).rearrange("p (a b) -> p a b", b=128)
            ksum_xf = AR.bf16(64)
            ksum_x3 = ksum_xf.rearrange("p (h b) -> p h b", b=8)
            ksum_x4 = ksum_xf.rearrange("p (j e b) -> p j e b", e=2, b=8)
            sp_tok = AR.f32(4 * 256).rearrange("p (a b) -> p a b", b=256)
            etmp = AR.f32(1 * 256).rearrange("p (a b) -> p a b", b=256)
            tmpS = AR.f32(2 * 256).rearrange("p (a b) -> p a b", b=256)
            gm = AR.f32(64)
            top8 = AR.f32(64)
            rl_sb = AR.f32(1 * 512).rearrange("p (a b) -> p a b", b=512)
            eb = AR.f32(2 * 128).rearrange("p (a b) -> p a b", b=128)
            enb = AR.f32(2 * 128).rearrange("p (a b) -> p a b", b=128)
            erev = AR.f32(2 * 128).rearrange("p (a b) -> p a b", b=128)
            o_r = AR.f32(4 * 128).rearrange("p (a b) -> p a b", b=128)
            S_f = AR.f32(2 * 128).rearrange("p (a b) -> p a b", b=128)
            ksum_f = AR.f32(32).rearrange("p (a b) -> p a b", b=8)
            gstat = AR.f32(48).rearrange("p (a b) -> p a b", b=6)

            def norm_stages(g):
                out_ = []
                hs0 = (g % 2) * 512
                for t in range(4):
                    tt = g * 4 + t
                    xs = tt % 2
                    box = {}

                    def n1(tt=tt, xs=xs, box=box):
                        S.dma('sp', 'xt%d' % xs, xt[:, xs, :], x_d[s, tt * 128:(tt + 1) * 128, :], (), ['xt%d' % xs])
                        box['r'] = norm_part1(xt[:, xs, :], 'xt%d' % xs)

                    def n2(t=t, box=box, g=g, hs0=hs0):
                        norm_part2(box['r'], gmix, hs0 + t * 128, 'hTa%d_%d' % (g % 2, t))
                    out_ += [n1, n2]
                return out_

            def mask_stages(g):
                out_ = []
                for t in range(4):
                    B = (4 * g + t) // 2
                    tsl_ = slice(t * 128, (t + 1) * 128)

                    def m1(B=B, tsl_=tsl_):
                        for h in range(8):
                            S.matmul(pb[6][:, h * 8:(h + 1) * 8], QTx[:, h, tsl_], ksum_x3[:, h, :], True, True,
                                     ['QTq', 'QTm', 'ksum_x'], ['b6'])
                        S.tt('dve', gm, pb[6][:, 0:64], pbias[:, B * 64:(B + 1) * 64], ALU.add, ['b6', 'cst'], ['gm'])
                        for h in range(8):
                            S.op('dve', lambda e, h=h: e.max(top8[:, h * 8:(h + 1) * 8], gm[:, h * 8:(h + 1) * 8]),
                                 ['gm'], ['top8'])
                        for h in range(8):
                            S.ts('dve', allowed[:, h, 64:72], gm[:, h * 8:(h + 1) * 8], top8[:, h * 8 + 3:h * 8 + 4],
                                 ALU.is_ge, ['gm', 'top8'], ['allowed'], s2=64.0, op1=ALU.mult)

                    def m2(tsl_=tsl_):
                        for hg in range(2):
                            for h4 in range(4):
                                S.matmul(pb[7][0:73, h4 * 128:(h4 + 1) * 128], allowed[:, hg * 4 + h4, 0:73], ident_b[:],
                                         True, True, ['allowed', 'cstb'], ['b7'])
                            S.copy('act', QTx[64:73, hg * 4:(hg + 1) * 4, tsl_],
                                   pb[7][64:73, :].rearrange("p (a b) -> p a b", b=128), ['b7'], ['QTm'])
                    out_ += [m1, m2]
                return out_

            def gla_stages(g):
                lists = []
                for t in range(4):
                    n = 4 * g + t
                    tsl_ = slice(t * 128, (t + 1) * 128)
                    pair_lists = []
                    for j in range(2):
                        Bk = pb[1 + j]
                        bk = 'b%d' % (1 + j)
                        js = slice(j * 128, (j + 1) * 128)
                        box = {}

                        def g1(t=t, j=j, Bk=Bk, bk=bk, js=js):
                            S.matmul(Bk[:, 0:128], sp_tok[:, t, js], triU, True, True, ['sp_tok', 'cst'], [bk])
                            S.matmul(Bk[:, 128:256], strictL, sp_tok[:, t, js], True, True, ['sp_tok', 'cst'], [bk])

                        def g2(j=j, Bk=Bk, bk=bk):
                            S.act(eb[:, j, :], Bk[:, 0:128], AF.Exp, [bk], ['eb%d' % j])
                            S.act(enb[:, j, :], Bk[:, 0:128], AF.Exp, [bk], ['enb%d' % j], scale=-1.0)
                            S.act(erev[:, j, :], Bk[:, 128:256], AF.Exp, [bk], ['erev%d' % j])

                        def g3(t=t, j=j, js=js, tsl_=tsl_):
                            for hh in range(2):
                                rr = slice(hh * 64, (hh + 1) * 64)
                                S.tt('dve', qtl2[rr, j, hh, :], QGT[rr, j, tsl_], eb[rr, j, :], ALU.mult,
                                     ['QGT', 'eb%d' % j], ['qtl%d' % j])
                                S.tt('dve', ktl2[rr, j, hh, :], KGT[rr, j, tsl_], enb[rr, j, :], ALU.mult,
                                     ['KGT', 'enb%d' % j], ['ktl%d' % j])
                            S.tt('dve', khat[:, j, :], KGtok[:, t, js], erev[:, j, :], ALU.mult,
                                 ['KGtok', 'erev%d' % j], ['khat%d' % j])

                        def g4(j=j, Bk=Bk, bk=bk):
                            for hh in range(2):
                                S.matmul(Bk[:, 256 + hh * 128:384 + hh * 128], ktl2[:, j, hh, :], qtl2[:, j, hh, :],
                                         True, True, ['ktl%d' % j, 'qtl%d' % j], [bk])

                        def g5(j=j, Bk=Bk, bk=bk):
                            for hh in range(2):
                                S.tt('dve', ATm[:, 2 * j + hh, :], Bk[:, 256 + hh * 128:384 + hh * 128], causal_b[:],
                                     ALU.mult, [bk, 'cstb'], ['ATm%d' % (2 * j + hh)])

                        def g6(t=t, j=j, Bk=Bk, bk=bk):
                            for hh in range(2):
                                head = 2 * j + hh
                                S.matmul(Bk[:, hh * 128:(hh + 1) * 128], qtl2[:, j, hh, :], S_b[:, j, :], True, False,
                                         ['qtl%d' % j, 'S_b%d' % j], [bk])
                                S.matmul(Bk[:, hh * 128:(hh + 1) * 128], ATm[:, head, :],
                                         VGtok[:, t, head * 128:(head + 1) * 128], False, True,
                                         ['ATm%d' % head, 'VGtok'], [bk])
                            for hh in range(2):
                                head = 2 * j + hh
                                S.matmul(Bk[:, 256 + hh * 128:384 + hh * 128], khat[:, j, :],
                                         VGtok[:, t, head * 128:(head + 1) * 128], True, True,
                                         ['khat%d' % j, 'VGtok'], [bk])

                        def g7(j=j, Bk=Bk, bk=bk, box=box):
                            sl = rot('gst', 8)
                            k = 'gst%d' % sl
                            box['sl'] = sl
                            for hh in range(2):
                                S.act(junk[:, 0:128], Bk[:, hh * 128:(hh + 1) * 128], AF.Square, [bk], [k],
                                      accum_out=gstat[:, sl, hh:hh + 1])
                            S.act(gstat[:, sl, 2:4], gstat[:, sl, 0:2], AF.Ln, [k], [k], bias=EPS, scale=1.0 / 128)
                            S.act(gstat[:, sl, 4:6], gstat[:, sl, 2:4], AF.Exp, [k], [k], scale=-0.5)
                            for hh in range(2):
                                rr = slice(hh * 64, (hh + 1) * 64)
                                S.stt(S_f[rr, j, :], S_f[rr, j, :], eb[rr, j, 127:128],
                                      Bk[rr, 256 + hh * 128:384 + hh * 128], ALU.mult, ALU.add,
                                      ['S_f%d' % j, 'eb%d' % j, bk], ['S_f%d' % j])
                            S.copy('act', S_b[:, j, :], S_f[:, j, :], ['S_f%d' % j], ['S_b%d' % j])

                        def g8(j=j, Bk=Bk, bk=bk, box=box):
                            sl = box['sl']
                            for hh in range(2):
                                S.ts('dve', o_r[:, 2 * j + hh, :], Bk[:, hh * 128:(hh + 1) * 128],
                                     gstat[:, sl, 4 + hh:5 + hh], ALU.mult, [bk, 'gst%d' % sl], ['o_r%d' % (2 * j + hh)])

                        def g9(j=j, Bk=Bk, bk=bk):
                            for hh in range(2):
                                S.transpose(Bk[:, 256 + hh * 128:384 + hh * 128], o_r[:, 2 * j + hh, :], ident_f,
                                            ['o_r%d' % (2 * j + hh), 'cst'], [bk])

                        def g10(j=j, Bk=Bk, bk=bk, n=n, tsl_=tsl_):
                            for hh in range(2):
                                head = 2 * j + hh
                                S.stt(mixT[:, 4 + head, n * 128:(n + 1) * 128], Bk[:, 256 + hh * 128:384 + hh * 128],
                                      gnorm, silu_rg[:, head, tsl_], ALU.mult, ALU.mult, [bk, 'cst', 'silu_rg'],
                                      ['mix%d' % (4 + head)])
                        pair_lists.append([g1, g2, g3, g4, g5, g6, g7, g8, g9, g10])
                    for a_, b_ in zip(pair_lists[0], pair_lists[1]):
                        lists.append(lambda a_=a_, b_=b_: (a_(), b_()))
                return lists

            def drain(lst, n):
                for _ in range(min(n, len(lst))):
                    lst.pop(0)()

            def proj_fm(w, wk, hkeys, hs, cb, bi):
                for k in range(8):
                    S.matmul(pb[bi][:, :], w[:, k, cb * 128:(cb + 1) * 128], hT[:, k, hs], k == 0, k == 7,
                             hkeys + [wk], ['b%d' % bi])

            def proj_tm(w, wk, hkey, hcol, bi, c0, c1):
                for k in range(8):
                    S.matmul(pb[bi][:, 0:c1 - c0], hT[:, k, hcol:hcol + 128], w[:, k, c0:c1], k == 0, k == 7,
                             [hkey, wk], ['b%d' % bi])

            first_norm = norm_stages(0)
            first_norm[0]()
            first_norm[2]()
            for j in range(4):
                S.memset('dve', VA[:, :, j, 64:128], 1.0, ['VA'])
            S.memset('dve', QTx[:, :, :], 0.0, ['QTq', 'QTm'])
            S.memset('dve', glrT[0:32, :], 1.0, ['glrT'])
            S.memset('dve', allowed[:, :, :], 0.0, ['allowed'])
            S.ts('dve', allowed[:, :, 72:73], c31.unsqueeze(2), -64.0, ALU.add, ['cst', 'allowed'], ['allowed'])
            S.memset('dve', qtl2[:, :, :, :], 0.0, ['qtl0', 'qtl1'])
            S.memset('dve', ktl2[:, :, :, :], 0.0, ['ktl0', 'ktl1'])
            S.memset('dve', S_f[:, :, :], 0.0, ['S_f0', 'S_f1'])
            S.memset('dve', S_b[:, :, :], 0.0, ['S_b0', 'S_b1'])
            S.memset('dve', ksum_f[:, :, :], 0.0, ['ksum_f'])
            S.memset('dve', ksum_xf, 0.0, ['ksum_x'])

            for i_ in (1, 4, 3, 6, 5, 7):
                first_norm[i_]()
            for g in range(NG):
                hs0 = (g % 2) * 512
                hs = slice(hs0, hs0 + 512)
                hkeys = ['hTa%d_%d' % (g % 2, t) for t in range(4)]
                gs = slice(g * 512, (g + 1) * 512)
                S.checkpoint('A_norm')
                ns = norm_stages(g + 1) if g + 1 < NG else []
                for k in range(8):
                    S.matmul(pb[5][0:16, :], wglr[:, k, :], hT[:, k, hs], k == 0, k == 7, hkeys + ['cstb'], ['b5'])
                S.copy('act', glrT[0:16, :], pb[5][0:16, :], ['b5'], ['glrT'])
                w, wk = wget()
                for cb in range(4):
                    bi = 1 + cb % 2
                    proj_fm(w, wk, hkeys, hs, cb, bi)
                    S.act(QTx[0:64, 2 * cb, :], pb[bi][0:64, :], AF.Copy, ['b%d' % bi], ['QTq'], scale=0.125)
                    S.ts('dve', QTx[0:64, 2 * cb + 1, :], pb[bi][64:128, :], 0.125, ALU.mult, ['b%d' % bi], ['QTq'])
                for t in range(4):
                    bi = 5 + t % 2
                    S.matmul(pb[bi][:, 0:256], glrT[0:17, t * 128:(t + 1) * 128], w_aug[0:17, :], True, True,
                             ['glrT', 'cstb'], ['b%d' % bi])
                    S.act(etmp[:, 0, :], pb[bi][:, 0:256], AF.Exp, ['b%d' % bi], ['etmp0'], scale=-1.0)
                    S.act(sp_tok[:, t, :], etmp[:, 0, :], AF.Ln, ['etmp0'], ['sp_tok'], bias=1.0)
                w, wk = wget()
                for cb in range(4):
                    bi = 1 + cb % 2
                    proj_fm(w, wk, hkeys, hs, cb, bi)
                    S.op('dve', lambda e, a=(ksum_f[:, cb, 2 * g:2 * g + 2],
                                             pb[bi][:, :].rearrange("p (b k) -> p b k", k=256)):
                         e.reduce_sum(a[0], a[1], AXL.X), ['b%d' % bi], ['ksum_f'])
                    S.copy('act', KTx[0:64, 2 * cb, gs], pb[bi][0:64, :], ['b%d' % bi], ['KTx'])
                    S.copy('dve', KTx[0:64, 2 * cb + 1, gs], pb[bi][64:128, :], ['b%d' % bi], ['KTx'])
                S.ts('dve', ksum_x4[0:64, :, 0, :], ksum_f[0:64, :, :], 1.0 / 256, ALU.mult, ['ksum_f'], ['ksum_x'])
                S.ts('dve', ksum_x4[0:64, :, 1, :], ksum_f[64:128, :, :], 1.0 / 256, ALU.mult, ['ksum_f'], ['ksum_x'])
                ms = mask_stages(g)
                drain(ms, 1)
                drain(ns, 1)
                w, wk = wget()
                for t in range(4):
                    bi = 1 + t % 2
                    proj_tm(w, wk, hkeys[t], hs0 + t * 128, bi, 0, 512)
                    pv4 = pb[bi][:, :].rearrange("p (j e d) -> p j e d", e=2, d=64)
                    S.copy('act', VA[:, g * 4 + t, :, 0:64], pv4[:, :, 0, :], ['b%d' % bi], ['VA'])
                    S.copy('dve', VA[:, g * 4 + t, :, 128:192], pv4[:, :, 1, :], ['b%d' % bi], ['VA'])
                drain(ms, 2)
                drain(ns, 2)
                w, wk = wget()
                for cb in range(4):
                    bi = 1 + cb % 2
                    proj_fm(w, wk, hkeys, hs, cb, bi)
                    if cb < 2:
                        S.act(QGT[:, cb, :], pb[bi][:, :], AF.Copy, ['b%d' % bi], ['QGT'], scale=0.125)
                    else:
                        S.copy('dve', KGT[:, cb - 2, :], pb[bi][:, :], ['b%d' % bi], ['KGT'])
                for t in range(4):
                    bi = 1 + t % 2
                    proj_tm(w, wk, hkeys[t], hs0 + t * 128, bi, 256, 512)
                    evac_copy(KGtok[:, t, :], pb[bi][:, 0:256], ['b%d' % bi], ['KGtok'])
                drain(ms, 2)
                drain(ns, 2)
                w, wk = wget()
                for t in range(4):
                    bi = 1 + t % 2
                    proj_tm(w, wk, hkeys[t], hs0 + t * 128, bi, 0, 512)
                    evac_copy(VGtok[:, t, :], pb[bi][:, :], ['b%d' % bi], ['VGtok'])
                drain(ms, 2)
                drain(ns, 2)
                w, wk = wget()
                for cb in range(4):
                    bi = 1 + cb % 2
                    proj_fm(w, wk, hkeys, hs, cb, bi)
                    S.act(silu_rg[:, cb, :], pb[bi][:, :], AF.Silu, ['b%d' % bi], ['silu_rg'])
                drain(ms, len(ms))
                drain(ns, len(ns))
                S.checkpoint('A_proj')
                S.checkpoint('B_mask')
                deferred = gla_stages(g)
                nkt = 4 * g + 4
                iters = [(h, kt) for h in range(8) for kt in range(nkt)]
                SB = [3, 4, 7]
                OB = [5, 6]

                def emit_qk(idx):
                    h, kt = iters[idx]
                    qlo = max(0, kt - 4 * g) * 128
                    bi = SB[idx % 3]
                    S.matmul(pb[bi][:, qlo:512], KTx[:, h, kt * 128:(kt + 1) * 128], QTx[:, h, qlo:512],
                             True, True, ['KTx', 'QTq', 'QTm', 'cstb'], ['b%d' % bi])

                def emit_exp(idx):
                    h, kt = iters[idx]
                    i0 = kt - 4 * g
                    bi = SB[idx % 3]
                    bk = 'b%d' % bi
                    bank = pb[bi]
                    ps_ = idx % 4
                    ptk = 'PT%d' % ps_
                    qlo = max(0, i0) * 128
                    if i0 >= -1:
                        na = max(0, i0) * 128
                        nb_ = min(i0 + 2, 4) * 128
                        off = 128 if i0 == -1 else 0
                        n = nb_ - na
                        S.tt('dve', bank[:, na:nb_], bank[:, na:nb_], t5b[:, h, off:off + n], ALU.add,
                             [bk, 'cst2'], [bk])
                    S.act(PT[:, ps_, qlo:512], bank[:, qlo:512], AF.Exp, [bk], [ptk])

                def emit_pv(idx):
                    h, kt = iters[idx]
                    j, e_ = h // 2, h % 2
                    qlo = max(0, kt - 4 * g) * 128
                    ob = OB[h % 2]
                    obk = 'b%d' % ob
                    ps_ = idx % 4
                    S.matmul(pb[ob][:, qlo:512], VA[:, kt, j, e_ * 64:e_ * 64 + 128], PT[:, ps_, qlo:512],
                             kt == 0, kt == nkt - 1, ['VA', 'PT%d' % ps_], [obk])
                    if kt == nkt - 1:
                        sl = 0
                        lo, hi = (64, 128) if e_ == 0 else (0, 64)
                        oo, oh = (0, 64) if e_ == 0 else (64, 128)
                        tmpf = tmpS[:, :, :].rearrange("p a b -> p (a b)")
                        if e_ == 0:
                            recip2(rl_sb[oo:oh, sl, :], pb[ob][lo:hi, :], tmpf[lo:hi, :], [obk], 'rl%d' % sl, 'rscrA')
                        else:
                            S.op('dve', lambda e, a=(rl_sb[oo:oh, sl, :], pb[ob][lo:hi, :]): e.reciprocal(a[0], a[1]),
                                 [obk], ['rl%d' % sl])
                        S.tt('dve', mixT[oo:oh, j, gs], pb[ob][oo:oh, :], rl_sb[oo:oh, sl, :], ALU.mult,
                             [obk, 'rl%d' % sl], ['mix%d' % j])

                nit = len(iters)
                n_def = len(deferred)
                emit_qk(0)
                if nit > 1:
                    emit_qk(1)
                for idx in range(nit):
                    if idx + 2 < nit:
                        emit_qk(idx + 2)
                    emit_exp(idx)
                    emit_pv(idx)
                    want = min(n_def, -(-((idx + 1) * n_def) // max(1, (3 * nit) // 4)))
                    drain(deferred, want - (n_def - len(deferred)))
                drain(deferred, len(deferred))
                S.checkpoint('B_attn')
            S.checkpoint('B_gla')
            if debug:
                S.dma('sp', 'dbg', dbg_d, mixT[:].rearrange("p a b -> p (a b)"), ['mix%d' % k for k in range(8)], ['dbg_d'])

            S.barrier()
            AR.reset()
            kxT = AR.bf16(8 * 256).rearrange("p (a b) -> p a b", b=256)
            vx = AR.bf16(2 * 1024).rearrange("p (a b) -> p a b", b=1024)
            qxT = AR.bf16(2 * GC).rearrange("p (a b) -> p a b", b=GC)
            big = AR.bf16(8 * GC).rearrange("p (a b) -> p a b", b=GC)
            PTx = AR.bf16(2 * 512).rearrange("p (a b) -> p a b", b=512)
            rtmp = AR.bf16(2 * 512).rearrange("p (a b) -> p a b", b=512)
            x1 = AR.f32(TC * D).rearrange("p (a b) -> p a b", b=D)
            rlx = AR.f32(512)
            rscr = AR.f32(512)
            gfin = AR.f32(D)
            S.dma('sp', 'gfin', gfin, gfin_d, (), ['gfin'])

            mkeys = ['hT0', 'hT1']
            for mt in range(2):
                xs = mt
                S.dma('sp', 'xt%d' % xs, xt[:, xs, :], mem_d[s, mt * 128:(mt + 1) * 128, :], (), ['xt%d' % xs])
                norm_to_hT(xt[:, xs, :], 'xt%d' % xs, gmem, mt * 128, mkeys[mt])
            for c in range(4):
                w, wk = wget()
                if c < 2:
                    for cb in range(4):
                        bi = 1 + cb % 2
                        for k in range(8):
                            S.matmul(pb[bi][:, 0:256], w[:, k, cb * 128:(cb + 1) * 128], hT[:, k, 0:256], k == 0, k == 7,
                                     mkeys + [wk], ['b%d' % bi])
                        evac_copy(kxT[:, c * 4 + cb, :], pb[bi][:, 0:256], ['b%d' % bi], ['kxT'])
                else:
                    for mt in range(2):
                        bi = 1 + mt % 2
                        for k in range(8):
                            S.matmul(pb[bi][:, :], hT[:, k, mt * 128:(mt + 1) * 128], w[:, k, :], k == 0, k == 7,
                                     [mkeys[mt], wk], ['b%d' % bi])
                        evac_copy(vx[:, mt, (c - 2) * 512:(c - 1) * 512], pb[bi][:, :], ['b%d' % bi], ['vx'])

            S.checkpoint('C_mem')
            for cg in range(NCG):
                tok0 = cg * GC
                hk = ['hT%d' % t for t in range(TC)]
                mixk = ['mix%d' % k for k in range(8)]
                for t in range(TC):
                    S.dma('sp', 'x1_%d' % t, x1[:, t, :], x_d[s, tok0 + t * 128:tok0 + (t + 1) * 128, :], (), ['x1_%d' % t])
                wA = wget()
                wB = wget(ahead=1)
                pend = None
                for t in range(TC):
                    for c, (w, wk) in enumerate((wA, wB)):
                        bi = 3 + (2 * t + c) % 4
                        for k in range(8):
                            S.matmul(pb[bi][:, :], mixT[:, k, tok0 + t * 128:tok0 + (t + 1) * 128], w[:, k, :],
                                     k == 0, k == 7, mixk + [wk], ['b%d' % bi])
                        S.tt('dve', x1[:, t, c * 512:(c + 1) * 512], pb[bi][:, :], x1[:, t, c * 512:(c + 1) * 512],
                             ALU.add, ['b%d' % bi, 'x1_%d' % t], ['x1_%d' % t])
                    r_ = norm_part1(x1[:, t, :], 'x1_%d' % t)
                    if pend is not None:
                        norm_part2(*pend)
                    pend = (r_, gxat, t * 128, hk[t])
                norm_part2(*pend)
                S.checkpoint('C_out')
                def xq_proj(xh, w, wk):
                    for c2 in range(2):
                        cb = (xh % 2) * 2 + c2
                        for hf in range(GC // 512):
                            bi = 1 + hf % 2
                            for k in range(8):
                                S.matmul(pb[bi][:, :], w[:, k, cb * 128:(cb + 1) * 128], hT[:, k, hf * 512:(hf + 1) * 512],
                                         k == 0, k == 7, hk + [wk], ['b%d' % bi])
                            S.act(qxT[:, c2, hf * 512:(hf + 1) * 512], pb[bi][:, :], AF.Copy, ['b%d' % bi], ['qxT'],
                                  scale=1.0 / 16)

                NHF = GC // 512
                STB = [(5, 6), (3, 4)]
                w, wk = wget()
                xq_proj(0, w, wk)
                for xh in range(4):
                    for hf in range(NHF):
                        hs = slice(hf * 512, (hf + 1) * 512)
                        for mt in range(2):
                            bi = STB[hf % 2][mt]
                            for c2 in range(2):
                                S.matmul(pb[bi][:, :], kxT[:, 2 * xh + c2, mt * 128:(mt + 1) * 128], qxT[:, c2, hs],
                                         c2 == 0, c2 == 1, ['kxT', 'qxT'], ['b%d' % bi])
                    for hf in range(NHF):
                        hs = slice(hf * 512, (hf + 1) * 512)
                        for mt in range(2):
                            bi = STB[hf % 2][mt]
                            S.act(PTx[:, mt, :], pb[bi][:, :], AF.Exp, ['b%d' % bi], ['PTx%d' % mt])
                        if hf == 0 and xh + 1 < 4:
                            if (xh + 1) % 2 == 0:
                                w, wk = wget()
                            xq_proj(xh + 1, w, wk)
                        for mt in range(2):
                            S.matmul(pb[7][:, :], ones_b[:], PTx[:, mt, :], mt == 0, mt == 1, ['ones_b', 'PTx%d' % mt], ['b7'])
                        recip2(rlx, pb[7][:, :], rscr, ['b7'], 'rlx', 'rscr')
                        for c2 in range(2):
                            bi = 1 + c2
                            for mt in range(2):
                                S.matmul(pb[bi][:, :], vx[:, mt, (2 * xh + c2) * 128:(2 * xh + c2 + 1) * 128], PTx[:, mt, :],
                                         mt == 0, mt == 1, ['vx', 'PTx%d' % mt], ['b%d' % bi])
                            S.tt('dve', big[:, 2 * xh + c2, hs], pb[bi][:, :], rlx, ALU.mult, ['b%d' % bi, 'rlx'], ['big'])
                wA = wget()
                wB = wget(ahead=1)
                pend = None
                for t in range(TC):
                    for c, (w, wk) in enumerate((wA, wB)):
                        bi = 3 + (2 * t + c) % 4
                        for k in range(8):
                            S.matmul(pb[bi][:, :], big[:, k, t * 128:(t + 1) * 128], w[:, k, :], k == 0, k == 7,
                                     ['big', wk], ['b%d' % bi])
                        S.tt('dve', x1[:, t, c * 512:(c + 1) * 512], pb[bi][:, :], x1[:, t, c * 512:(c + 1) * 512],
                             ALU.add, ['b%d' % bi, 'x1_%d' % t], ['x1_%d' % t])
                    r_ = norm_part1(x1[:, t, :], 'x1_%d' % t)
                    if pend is not None:
                        norm_part2(*pend)
                    pend = (r_, gmlp, t * 128, hk[t])
                norm_part2(*pend)
                S.checkpoint('C_xattn')
                for q in range(4):
                    for c in range(2):
                        w, wk = wget()
                        for cb in range(4):
                            for hf in range(GC // 512):
                                bi = 1 + rot('acc', 2)
                                for k in range(8):
                                    S.matmul(pb[bi][:, :], w[:, k, cb * 128:(cb + 1) * 128], hT[:, k, hf * 512:(hf + 1) * 512],
                                             k == 0, k == 7, hk + [wk], ['b%d' % bi])
                                rsl = rot('ts', 2)
                                S.act(rtmp[:, rsl, :], pb[bi][:, :], AF.Relu, ['b%d' % bi], ['rtmp%d' % rsl])
                                S.tt('dve', big[:, c * 4 + cb, hf * 512:(hf + 1) * 512], rtmp[:, rsl, :], rtmp[:, rsl, :],
                                     ALU.mult, ['rtmp%d' % rsl], ['big'])
                    for c in range(2):
                        w, wk = wget()
                        for t in range(TC):
                            bi = 3 + t % 4
                            for k in range(8):
                                S.matmul(pb[bi][:, :], big[:, k, t * 128:(t + 1) * 128], w[:, k, :], k == 0, k == 7,
                                         ['big', wk], ['b%d' % bi])
                            S.tt('dve', x1[:, t, c * 512:(c + 1) * 512], pb[bi][:, :], x1[:, t, c * 512:(c + 1) * 512],
                                 ALU.add, ['b%d' % bi, 'x1_%d' % t], ['x1_%d' % t])
                S.checkpoint('C_mlp')
                for t in range(TC):
                    rs, rk = rstd_of(x1[:, t, :], 'x1_%d' % t, D)
                    xs = rot('xs', 2)
                    S.stt(xt[:, xs, :], x1[:, t, :], rs, gfin, ALU.mult, ALU.mult, ['x1_%d' % t, rk, 'gfin'], ['xt%d' % xs])
                    S.dma('sp', 'out%d' % xs, out_d[s, tok0 + t * 128:tok0 + (t + 1) * 128, :], xt[:, xs, :],
                          ['xt%d' % xs], ['out_d%d' % xs])
            S.barrier()

        final_keys = ['out_d0', 'out_d1'] + (['dbg_d'] if debug else [])
        S.wait_all('sp', final_keys)
        S.barrier()
        assert S.stopped or wstate['i'] == len(wplan), (wstate['i'], len(wplan))
        with nc.allow_non_contiguous_dma(reason="strided weight / constant loads"):
            S.emit()
    return nc


def _rel_bucket_np(d):
    d = np.maximum(d, 0)
    large = 16 + (np.log(np.maximum(d, 1).astype(np.float32) / np.float32(16)) / np.float32(math.log(128 / 16))
                  * np.float32(16)).astype(np.int32)
    large = np.minimum(large, 31)
    return np.where(d < 16, d, large)


def _consts(T=2048):
    cst = np.zeros((128, 1024), np.float32)
    i = np.arange(128)
    cst[:, 0:128] = np.eye(128, dtype=np.float32)
    cst[:, 128:256] = np.where(i[:, None] <= i[None, :], -1.0 / 16, 0.0)
    cst[:, 256:384] = np.where(i[:, None] > i[None, :], -1.0 / 16, 0.0)
    cst[:, 384:512] = np.where(i[:, None] <= i[None, :], 1.0, 0.0)
    for B in range(8):
        row = np.where(np.arange(8) < B, 0.0, np.where(np.arange(8) == B, 64.0, -64.0)).astype(np.float32)
        cst[:, 512 + B * 64:512 + (B + 1) * 64] = np.tile(row, 8)[None, :]
    blk = np.zeros((9, T), np.float32)
    kpos = np.arange(T)
    for b in range(8):
        blk[b, :] = (kpos // 256 == b)
    blk[8, :] = 1.0
    return cst, blk


def _host_inputs(rp_table, norm_mix, g_norm, norm_xattn, norm_mem, norm_mlp, norm_final, T=2048):
    cst, blk = _consts(T)
    vecs = np.zeros((128, 48), np.float32)
    vecs[:, 0:8] = np.asarray(norm_mix, np.float32).reshape(8, 128).T
    vecs[:, 8:16] = np.asarray(norm_xattn, np.float32).reshape(8, 128).T
    vecs[:, 16:24] = np.asarray(norm_mlp, np.float32).reshape(8, 128).T
    vecs[:, 24:32] = np.asarray(norm_mem, np.float32).reshape(8, 128).T
    vecs[:, 32] = np.asarray(g_norm, np.float32).reshape(128)
    rp = np.asarray(rp_table, np.float32)
    vecs[:, 33:41] = rp[31][None, :]
    i = np.arange(128)[:, None]
    jj = np.arange(256)[None, :]
    dist = jj - i
    idx = _rel_bucket_np(dist)
    t5 = rp[idx]
    t5 = np.where((dist >= 0)[:, :, None], t5, np.float32(-30000.0)).astype(np.float32)
    t5 = np.ascontiguousarray(t5.transpose(0, 2, 1)).reshape(128, 2048)
    gfin = np.ascontiguousarray(np.broadcast_to(np.asarray(norm_final, np.float32).reshape(1, D), (128, D)))
    return cst, blk, vecs, t5, gfin


_NC_CACHE = {}


def kernel(x, mem, rp_table, norm_mix, w_in, w_gate_up, b_gate, g_norm, w_out, norm_xattn, norm_mem, w_xq, w_xkv,
           w_xo, norm_mlp, w_up, w_down, norm_final):
    x = np.asarray(x, np.float32)
    mem = np.asarray(mem, np.float32)
    Bt, T, _ = x.shape
    nseq = Bt // NCORES
    cst, blk, vecs, t5, gfin = _host_inputs(rp_table, norm_mix, g_norm, norm_xattn, norm_mem, norm_mlp, norm_final, T)
    key = (nseq, T)
    if key not in _NC_CACHE:
        _NC_CACHE[key] = build(nseq, T)
    nc = _NC_CACHE[key]
    f = lambda a: np.ascontiguousarray(np.asarray(a, np.float32))
    shared = {
        "w_in": f(w_in[0]), "w_gate_up": f(w_gate_up[0]), "b_gate": f(b_gate[0]).reshape(1, 256),
        "w_out": f(w_out[0]), "w_xq": f(w_xq[0]), "w_xkv": f(w_xkv[0]), "w_xo": f(w_xo[0]),
        "w_up": f(w_up[0]), "w_down": f(w_down[0]),
        "cst": cst, "blkind": blk, "vecs": vecs, "t5b": t5, "gfin": gfin,
    }
    in_maps = []
    for c in range(NCORES):
        m = dict(shared)
        m["x"] = np.ascontiguousarray(x[c * nseq:(c + 1) * nseq])
        m["mem"] = np.ascontiguousarray(mem[c * nseq:(c + 1) * nseq])
        in_maps.append(m)
    res = run_bass_kernel_spmd(nc, in_maps, core_ids=list(range(NCORES)))
    out = np.concatenate([np.asarray(r["out"], np.float32) for r in res.results], axis=0)
    return out
```

```python
import math
from contextlib import ExitStack

import numpy as np
import concourse.bass as bass
import concourse.mybir as mybir
from concourse.bass_utils import run_bass_kernel_spmd

F32 = mybir.dt.float32
BF16 = mybir.dt.bfloat16
AF = mybir.ActivationFunctionType
ALU = mybir.AluOpType
AXL = mybir.AxisListType

D = 1024
MEM = 256
IN_COLS = 3088
DFF = 4096
EPS = 1e-6
NCORES = 8

ENGS = ['pe', 'act', 'dve', 'pool', 'sp']
PSUM_KEYS = set('b%d' % i for i in range(8))
SAME_ENG_SYNC = {'pe': False, 'act': True, 'dve': True, 'pool': True, 'sp': False}


class Sched:
    def __init__(self, nc, st):
        self.nc = nc
        self.st = st
        self.semobj = {}
        for e in ENGS:
            self.semobj[e] = st.enter_context(nc.semaphore('q_' + e))
        self.cnt = {e: 0 for e in ENGS}
        self.known = {e: {} for e in ENGS}
        self.prog = {e: [] for e in ENGS}
        self.lastw = {}
        self.readers = {}
        self.dmacnt = {}
        self.stopped = False
        self.stop_at = None

    def checkpoint(self, label):
        if self.stop_at is not None and label == self.stop_at:
            self.stopped = True

    def _sem(self, key):
        if key not in self.semobj:
            self.semobj[key] = self.st.enter_context(self.nc.semaphore('d_' + key))
            self.dmacnt[key] = 0
        return self.semobj[key]

    def _deps(self, reads, writes, eng=None):
        deps = {}

        def add(s, v):
            if deps.get(s, 0) < v:
                deps[s] = v
        for r in reads:
            w = self.lastw.get(r)
            if w is not None:
                add(*w)
            if r in PSUM_KEYS:
                for s_, v in self.readers.get(r, {}).items():
                    if s_ != eng:
                        add(s_, v)
        for w_ in writes:
            w = self.lastw.get(w_)
            if w is not None:
                add(*w)
            for s, v in self.readers.get(w_, {}).items():
                add(s, v)
        return deps

    def _waits(self, eng, deps):
        waits = []
        for s, v in deps.items():
            if s == eng and not SAME_ENG_SYNC[eng]:
                continue
            if self.known[eng].get(s, 0) < v:
                waits.append((s, v))
                self.known[eng][s] = v
        return waits

    def _mark(self, reads, writes, s, v):
        for r in reads:
            d = self.readers.setdefault(r, {})
            if d.get(s, 0) < v:
                d[s] = v
        for w in writes:
            self.lastw[w] = (s, v)
            self.readers[w] = {}

    def op(self, eng, fn, reads=(), writes=()):
        if self.stopped:
            return
        deps = self._deps(reads, writes, eng)
        waits = self._waits(eng, deps)
        self.cnt[eng] += 1
        v = self.cnt[eng]
        self.prog[eng].append((waits, fn, eng, 1))
        self._mark(reads, writes, eng, v)

    def dma(self, q, semkey, out, in_, reads=(), writes=()):
        if self.stopped:
            return
        self._sem(semkey)
        deps = self._deps(reads, writes)
        waits = self._waits(q, deps)
        self.dmacnt[semkey] += 16
        v = self.dmacnt[semkey]
        fn = (lambda e, o=out, i=in_: e.dma_start(out=o, in_=i))
        self.prog[q].append((waits, fn, semkey, 16))
        self._mark(reads, writes, semkey, v)

    def wait_all(self, eng, keys):
        deps = self._deps(keys, ())
        waits = self._waits(eng, deps)
        if waits:
            self.prog[eng].append((waits, None, None, 0))

    def barrier(self):
        for e in ENGS:
            deps = {}
            for e2 in ENGS:
                if e2 != e and e2 != 'sp' and self.cnt[e2] > 0:
                    deps[e2] = self.cnt[e2]
            for k, v in self.dmacnt.items():
                if v > 0:
                    deps[k] = v
            waits = self._waits(e, deps)
            if waits:
                self.prog[e].append((waits, None, None, 0))

    def replay(self, name, e):
        for waits, fn, semname, inc in self.prog[name]:
            for s, v in waits:
                e.wait_ge(self.semobj[s], v)
            if fn is None:
                continue
            inst = fn(e)
            inst.then_inc(self.semobj[semname], inc)

    def emit(self):
        nc = self.nc
        with nc.Block() as block:
            @block.tensor
            def _(e):
                self.replay('pe', e)

            @block.scalar
            def _(e):
                self.replay('act', e)

            @block.vector
            def _(e):
                self.replay('dve', e)

            @block.gpsimd
            def _(e):
                self.replay('pool', e)

            @block.sync
            def _(e):
                self.replay('sp', e)

    def matmul(self, out, lhsT, rhs, start, stop, reads, writes):
        self.op('pe', lambda e, a=(out, lhsT, rhs, start, stop): e.matmul(a[0], a[1], a[2], start=a[3], stop=a[4]),
                reads, writes)

    def transpose(self, out, in_, ident, reads, writes):
        self.op('pe', lambda e, a=(out, in_, ident): e.transpose(a[0], a[1], a[2]), reads, writes)

    def act(self, out, in_, func, reads, writes, bias=None, scale=None, accum_out=None):
        kw = {}
        if bias is not None:
            kw['bias'] = bias
        if scale is not None:
            kw['scale'] = scale
        if accum_out is not None:
            kw['accum_out'] = accum_out
        self.op('act', lambda e, a=(out, in_, func), kw=kw: e.activation(a[0], a[1], a[2], **kw), reads, writes)

    def tt(self, eng, out, in0, in1, op, reads, writes):
        self.op(eng, lambda e, a=(out, in0, in1, op): e.tensor_tensor(a[0], a[1], a[2], a[3]), reads, writes)

    def ts(self, eng, out, in0, s1, op0, reads, writes, s2=None, op1=None):
        kw = {}
        if op1 is not None:
            kw['op1'] = op1
        self.op(eng, lambda e, a=(out, in0, s1, s2, op0), kw=kw: e.tensor_scalar(a[0], a[1], a[2], a[3], a[4], **kw),
                reads, writes)

    def stt(self, out, in0, scalar, in1, op0, op1, reads, writes):
        self.op('dve', lambda e, a=(out, in0, scalar, in1, op0, op1):
                e.scalar_tensor_tensor(a[0], a[1], a[2], a[3], a[4], a[5]), reads, writes)

    def copy(self, eng, out, in_, reads, writes):
        if eng == 'act':
            self.op('act', lambda e, a=(out, in_): e.copy(a[0], a[1]), reads, writes)
        else:
            self.op(eng, lambda e, a=(out, in_): e.tensor_copy(a[0], a[1]), reads, writes)

    def memset(self, eng, ap, val, writes):
        self.op(eng, lambda e, a=(ap, val): e.memset(a[0], a[1]), (), writes)


class Arena:
    def __init__(self, tile, ncols):
        self.t = tile
        self.n = ncols
        self.off = 0

    def reset(self):
        self.off = 0

    def f32(self, ncols):
        assert self.off + ncols <= self.n, (self.off, ncols, self.n)
        ap = self.t[:, self.off:self.off + ncols]
        self.off += ncols
        return ap

    def bf16(self, ncols):
        n32 = (ncols + 1) // 2
        assert self.off + n32 <= self.n, (self.off, n32, self.n)
        ap = self.t[:, self.off:self.off + n32].bitcast(BF16)
        self.off += n32
        return ap


def build(nseq=2, T=2048, GC=1024, debug=False, stop_at=None):
    NT = T // 128
    NG = T // 512
    GC = min(GC, T)
    NCG = T // GC
    TC = GC // 128
    nc = bass.Bass("TRN2", target_bir_lowering=False)

    def dt(name, shape, dtype=F32, kind="ExternalInput"):
        return nc.dram_tensor(name, shape, dtype, kind=kind).ap()

    x_d = dt("x", [nseq, T, D])
    mem_d = dt("mem", [nseq, MEM, D])
    w_in_d = dt("w_in", [D, IN_COLS])
    w_gu_d = dt("w_gate_up", [16, 256])
    b_gate_d = dt("b_gate", [1, 256])
    w_out_d = dt("w_out", [D, D])
    w_xq_d = dt("w_xq", [D, D])
    w_xkv_d = dt("w_xkv", [D, 2 * D])
    w_xo_d = dt("w_xo", [D, D])
    w_up_d = dt("w_up", [D, DFF])
    w_down_d = dt("w_down", [DFF, D])
    cst_d = dt("cst", [128, 1024])
    blk_d = dt("blkind", [9, T])
    vecs_d = dt("vecs", [128, 48])
    t5_d = dt("t5b", [128, 2048])
    gfin_d = dt("gfin", [128, D])
    out_d = dt("out", [nseq, T, D], kind="ExternalOutput")
    if debug:
        dbg_d = dt("dbg", [128, 8 * T], BF16, kind="ExternalOutput")

    def wv(w, c0, n):
        return w.rearrange("(k p) c -> p k c", p=128)[:, :, c0:c0 + n]

    with ExitStack() as st:
        S = Sched(nc, st)
        S.stop_at = stop_at

        def sb(name, shape, dtype):
            return st.enter_context(nc.sbuf_tensor("sb_" + name, shape, dtype))

        def psb(name, shape, dtype):
            return st.enter_context(nc.psum_tensor(name, shape, dtype))

        cst = sb("cst", [128, 1024], F32)
        ident_f = cst[:, 0:128]
        triU = cst[:, 128:256]
        strictL = cst[:, 256:384]
        pbias = cst[:, 512:1024]
        ident_b = sb("ident_b", [128, 128], BF16)
        causal_b = sb("causal_b", [128, 128], BF16)
        ones_b = sb("ones_b", [128, 128], BF16)
        ones_f = sb("ones_f", [128, 64], F32)
        KTx = sb("KTx", [128, 8, T], BF16)
        t5b = sb("t5b", [128, 8, 256], F32)
        vecs = sb("vecs", [128, 48], F32)
        gmix, gxat, gmlp, gmem = vecs[:, 0:8], vecs[:, 8:16], vecs[:, 16:24], vecs[:, 24:32]
        gnorm = vecs[:, 32:33]
        c31 = vecs[:, 33:41]
        w_aug = sb("w_aug", [32, 256], BF16)
        wglr = sb("wglr", [128, 8, 16], BF16)
        stat = sb("stat", [128, 48], F32)
        xt = sb("xt", [128, 2, D], F32)
        junk = sb("junk", [128, D], BF16)
        xn = sb("xn", [128, 2, D], BF16)
        hT = sb("hT", [128, 8, 1024], BF16)
        wsl = sb("wsl", [128, 3, 8, 512], BF16)
        mixT = sb("mixT", [128, 8, T], BF16)
        ARENA_COLS = 18496
        arena_t = sb("arena", [128, ARENA_COLS], F32)
        AR = Arena(arena_t, ARENA_COLS)

        pb = [psb("pbank0", [128, 1024], BF16)] + [psb("pbank%d" % i, [128, 512], F32) for i in range(1, 8)]

        S.dma('sp', 'cst', cst[:], cst_d, (), ['cst'])
        S.dma('sp', 'cst', vecs[:], vecs_d, (), ['cst'])
        S.dma('sp', 'cst', t5b[:].rearrange("p a b -> p (a b)"), t5_d, (), ['cst'])
        S.dma('pool', 'cstb', ident_b[:], cst_d[:, 0:128], (), ['cstb'])
        S.dma('pool', 'cstb', causal_b[:], cst_d[:, 384:512], (), ['cstb'])
        S.memset('pool', KTx[64:128, :, :], 0.0, ['KTx'])
        for h in range(8):
            S.dma('pool', 'cstb', KTx[64:73, h, :], blk_d, ['KTx'], ['cstb', 'KTx'])
        S.dma('pool', 'cstb', w_aug[0:16, :], w_gu_d, (), ['cstb'])
        S.dma('pool', 'cstb', w_aug[16:17, :], b_gate_d, (), ['cstb'])
        S.dma('pool', 'cstb', wglr[:], wv(w_in_d, 2560, 16), (), ['cstb'])
        S.memset('dve', ones_b[:], 1.0, ['ones_b'])
        S.memset('dve', ones_f[:], 1.0, ['ones_f'])
        for h in range(8):
            S.ts('dve', t5b[:, h, :], t5b[:, h, :], c31[:, h:h + 1], ALU.subtract, ['cst'], ['cst2'])
        S.checkpoint('const')

        wplan = []
        for s in range(nseq):
            for g in range(NG):
                for c0 in (0, 512, 1024, 1536, 2048, 2576):
                    wplan.append(wv(w_in_d, c0, 512))
            for c in range(4):
                wplan.append(wv(w_xkv_d, c * 512, 512))
            for cg in range(NCG):
                for c in range(2):
                    wplan.append(wv(w_out_d, c * 512, 512))
                for c in range(2):
                    wplan.append(wv(w_xq_d, c * 512, 512))
                for c in range(2):
                    wplan.append(wv(w_xo_d, c * 512, 512))
                for q in range(4):
                    for c in range(2):
                        wplan.append(wv(w_up_d, q * 1024 + c * 512, 512))
                    for c in range(2):
                        wplan.append(w_down_d.rearrange("(q k p) n -> p q k n", k=8, p=128)[:, q, :, c * 512:(c + 1) * 512])
        wstate = {'i': 0, 'issued': 0}

        def wget(ahead=2):
            i = wstate['i']
            while wstate['issued'] <= min(i + ahead, len(wplan) - 1):
                n = wstate['issued']
                S.dma('pool', 'w%d' % (n % 3), wsl[:, n % 3], wplan[n], (), ['w%d' % (n % 3)])
                wstate['issued'] += 1
            wstate['i'] += 1
            return wsl[:, i % 3], 'w%d' % (i % 3)

        MG = [0, 0, 0, 1, 1, 2, 1, 3]
        MA = [0, 64, 32, 64, 0, 64, 32, 64]
        cnt = {'gst': 0, 'stat': 0, 'xs': 0, 'acc': 0, 'sc': 0, 'pt': 0, 'ts': 0, 'ev': 0}

        def rot(name, n):
            v = cnt[name] % n
            cnt[name] += 1
            return v

        def evac_copy(out, in_, reads, writes):
            if rot('ev', 2) == 0:
                S.copy('act', out, in_, reads, writes)
            else:
                S.copy('dve', out, in_, reads, writes)

        def recip2(out, in_, scratch, reads, okey, skey):
            S.act(scratch, in_, AF.Ln, reads, [skey])
            S.act(out, scratch, AF.Exp, [skey], [okey], scale=-1.0)

        def rstd_of(src, skey, nfeat):
            sl = rot('stat', 16)
            k = 'st%d' % sl
            ssq, lnv, rs = stat[:, 3 * sl:3 * sl + 1], stat[:, 3 * sl + 1:3 * sl + 2], stat[:, 3 * sl + 2:3 * sl + 3]
            S.act(junk[:, 0:nfeat], src, AF.Square, [skey], [k], accum_out=ssq)
            S.act(lnv, ssq, AF.Ln, [k], [k], bias=EPS, scale=1.0 / nfeat)
            S.act(rs, lnv, AF.Exp, [k], [k], scale=-0.5)
            return rs, k

        def norm_part1(src, skey):
            rs, k = rstd_of(src, skey, D)
            xs = rot('xs', 2)
            xk = 'xn%d' % xs
            S.act(xn[:, xs, :], src, AF.Copy, [skey, k], [xk], scale=rs)
            return xs, xk

        def norm_part2(r, gcols, col0, hkey):
            xs, xk = r
            for kk in range(8):
                S.transpose(pb[0][:, kk * 128:(kk + 1) * 128], xn[:, xs, kk * 128:(kk + 1) * 128], ident_b[:],
                            [xk, 'cstb'], ['b0'])
            S.tt('dve', hT[:, :, col0:col0 + 128], pb[0][:, :].rearrange("p (a b) -> p a b", b=128),
                 gcols.unsqueeze(2).to_broadcast([128, 8, 128]), ALU.mult, ['b0', 'cst'], [hkey])

        def norm_to_hT(src, skey, gcols, col0, hkey):
            norm_part2(norm_part1(src, skey), gcols, col0, hkey)

        for s in range(nseq):
            AR.reset()
            VA = AR.bf16(NT * 4 * 192).rearrange("p (a h d) -> p a h d", h=4, d=192)
            QTx = AR.bf16(8 * 512).rearrange("p (a b) -> p a b", b=512)
            QGT = AR.bf16(2 * 512).rearrange("p (a b) -> p a b", b=512)
            KGT = AR.bf16(2 * 512).rearrange("p (a b) -> p a b", b=512)
            KGtok = AR.bf16(4 * 256).rearrange("p (a b) -> p a b", b=256)
            VGtok = AR.bf16(4 * 512).rearrange("p (a b) -> p a b", b=512)
            silu_rg = AR.bf16(4 * 512).rearrange("p (a b) -> p a b", b=512)
            glrT = AR.bf16(512)
            PT = AR.bf16(4 * 512).rearrange("p (a b) -> p a b", b=512)
            allowed = AR.bf16(8 * 80).rearrange("p (a b) -> p a b", b=80)
            qtl2 = AR.bf16(4 * 128).rearrange("p (j h b) -> p j h b", h=2, b=128)
            ktl2 = AR.bf16(4 * 128).rearrange("p (j h b) -> p j h b", h=2, b=128)
            khat = AR.bf16(2 * 128).rearrange("p (a b) -> p a b", b=128)
            ATm = AR.bf16(4 * 128).rearrange("p (a b) -> p a b", b=128)
            S_b = AR.bf16(2 * 128).rearrange("p (a b) -> p a b", b=128)
            ksum_xf = AR.bf16(64)
            ksum_x3 = ksum_xf.rearrange("p (h b) -> p h b", b=8)
            ksum_x4 = ksum_xf.rearrange("p (j e b) -> p j e b", e=2, b=8)
            sp_tok = AR.f32(4 * 256).rearrange("p (a b) -> p a b", b=256)
            etmp = AR.f32(1 * 256).rearrange("p (a b) -> p a b", b=256)
            tmpS = AR.f32(2 * 256).rearrange("p (a b) -> p a b", b=256)
            gm = AR.f32(64)
            top8 = AR.f32(64)
            rl_sb = AR.f32(1 * 512).rearrange("p (a b) -> p a b", b=512)
            eb = AR.f32(2 * 128).rearrange("p (a b) -> p a b", b=128)
            enb = AR.f32(2 * 128).rearrange("p (a b) -> p a b", b=128)
            erev = AR.f32(2 * 128).rearrange("p (a b) -> p a b", b=128)
            o_r = AR.f32(4 * 128).rearrange("p (a b) -> p a b", b=128)
            S_f = AR.f32(2 * 128).rearrange("p (a b) -> p a b", b=128)
            ksum_f = AR.f32(32).rearrange("p (a b) -> p a b", b=8)
            gstat = AR.f32(48).rearrange("p (a b) -> p a b", b=6)

            def norm_stages(g):
                out_ = []
                hs0 = (g % 2) * 512
                for t in range(4):
                    tt = g * 4 + t
                    xs = tt % 2
                    box = {}

                    def n1(tt=tt, xs=xs, box=box):
                        S.dma('sp', 'xt%d' % xs, xt[:, xs, :], x_d[s, tt * 128:(tt + 1) * 128, :], (), ['xt%d' % xs])
                        box['r'] = norm_part1(xt[:, xs, :], 'xt%d' % xs)

                    def n2(t=t, box=box, g=g, hs0=hs0):
                        norm_part2(box['r'], gmix, hs0 + t * 128, 'hTa%d_%d' % (g % 2, t))
                    out_ += [n1, n2]
                return out_

            def mask_stages(g):
                out_ = []
                for t in range(4):
                    B = (4 * g + t) // 2
                    tsl_ = slice(t * 128, (t + 1) * 128)

                    def m1(B=B, tsl_=tsl_):
                        for h in range(8):
                            S.matmul(pb[6][:, h * 8:(h + 1) * 8], QTx[:, h, tsl_], ksum_x3[:, h, :], True, True,
                                     ['QTq', 'QTm', 'ksum_x'], ['b6'])
                        S.tt('dve', gm, pb[6][:, 0:64], pbias[:, B * 64:(B + 1) * 64], ALU.add, ['b6', 'cst'], ['gm'])
                        for h in range(8):
                            S.op('dve', lambda e, h=h: e.max(top8[:, h * 8:(h + 1) * 8], gm[:, h * 8:(h + 1) * 8]),
                                 ['gm'], ['top8'])
                        for h in range(8):
                            S.ts('dve', allowed[:, h, 64:72], gm[:, h * 8:(h + 1) * 8], top8[:, h * 8 + 3:h * 8 + 4],
                                 ALU.is_ge, ['gm', 'top8'], ['allowed'], s2=64.0, op1=ALU.mult)

                    def m2(tsl_=tsl_):
                        for hg in range(2):
                            for h4 in range(4):
                                S.matmul(pb[7][0:73, h4 * 128:(h4 + 1) * 128], allowed[:, hg * 4 + h4, 0:73], ident_b[:],
                                         True, True, ['allowed', 'cstb'], ['b7'])
                            S.copy('act', QTx[64:73, hg * 4:(hg + 1) * 4, tsl_],
                                   pb[7][64:73, :].rearrange("p (a b) -> p a b", b=128), ['b7'], ['QTm'])
                    out_ += [m1, m2]
                return out_

            def gla_stages(g):
                lists = []
                for t in range(4):
                    n = 4 * g + t
                    tsl_ = slice(t * 128, (t + 1) * 128)
                    pair_lists = []
                    for j in range(2):
                        Bk = pb[1 + j]
                        bk = 'b%d' % (1 + j)
                        js = slice(j * 128, (j + 1) * 128)
                        box = {}

                        def g1(t=t, j=j, Bk=Bk, bk=bk, js=js):
                            S.matmul(Bk[:, 0:128], sp_tok[:, t, js], triU, True, True, ['sp_tok', 'cst'], [bk])
                            S.matmul(Bk[:, 128:256], strictL, sp_tok[:, t, js], True, True, ['sp_tok', 'cst'], [bk])

                        def g2(j=j, Bk=Bk, bk=bk):
                            S.act(eb[:, j, :], Bk[:, 0:128], AF.Exp, [bk], ['eb%d' % j])
                            S.act(enb[:, j, :], Bk[:, 0:128], AF.Exp, [bk], ['enb%d' % j], scale=-1.0)
                            S.act(erev[:, j, :], Bk[:, 128:256], AF.Exp, [bk], ['erev%d' % j])

                        def g3(t=t, j=j, js=js, tsl_=tsl_):
                            for hh in range(2):
                                rr = slice(hh * 64, (hh + 1) * 64)
                                S.tt('dve', qtl2[rr, j, hh, :], QGT[rr, j, tsl_], eb[rr, j, :], ALU.mult,
                                     ['QGT', 'eb%d' % j], ['qtl%d' % j])
                                S.tt('dve', ktl2[rr, j, hh, :], KGT[rr, j, tsl_], enb[rr, j, :], ALU.mult,
                                     ['KGT', 'enb%d' % j], ['ktl%d' % j])
                            S.tt('dve', khat[:, j, :], KGtok[:, t, js], erev[:, j, :], ALU.mult,
                                 ['KGtok', 'erev%d' % j], ['khat%d' % j])

                        def g4(j=j, Bk=Bk, bk=bk):
                            for hh in range(2):
                                S.matmul(Bk[:, 256 + hh * 128:384 + hh * 128], ktl2[:, j, hh, :], qtl2[:, j, hh, :],
                                         True, True, ['ktl%d' % j, 'qtl%d' % j], [bk])

                        def g5(j=j, Bk=Bk, bk=bk):
                            for hh in range(2):
                                S.tt('dve', ATm[:, 2 * j + hh, :], Bk[:, 256 + hh * 128:384 + hh * 128], causal_b[:],
                                     ALU.mult, [bk, 'cstb'], ['ATm%d' % (2 * j + hh)])

                        def g6(t=t, j=j, Bk=Bk, bk=bk):
                            for hh in range(2):
                                head = 2 * j + hh
                                S.matmul(Bk[:, hh * 128:(hh + 1) * 128], qtl2[:, j, hh, :], S_b[:, j, :], True, False,
                                         ['qtl%d' % j, 'S_b%d' % j], [bk])
                                S.matmul(Bk[:, hh * 128:(hh + 1) * 128], ATm[:, head, :],
                                         VGtok[:, t, head * 128:(head + 1) * 128], False, True,
                                         ['ATm%d' % head, 'VGtok'], [bk])
                            for hh in range(2):
                                head = 2 * j + hh
                                S.matmul(Bk[:, 256 + hh * 128:384 + hh * 128], khat[:, j, :],
                                         VGtok[:, t, head * 128:(head + 1) * 128], True, True,
                                         ['khat%d' % j, 'VGtok'], [bk])

                        def g7(j=j, Bk=Bk, bk=bk, box=box):
                            sl = rot('gst', 8)
                            k = 'gst%d' % sl
                            box['sl'] = sl
                            for hh in range(2):
                                S.act(junk[:, 0:128], Bk[:, hh * 128:(hh + 1) * 128], AF.Square, [bk], [k],
                                      accum_out=gstat[:, sl, hh:hh + 1])
                            S.act(gstat[:, sl, 2:4], gstat[:, sl, 0:2], AF.Ln, [k], [k], bias=EPS, scale=1.0 / 128)
                            S.act(gstat[:, sl, 4:6], gstat[:, sl, 2:4], AF.Exp, [k], [k], scale=-0.5)
                            for hh in range(2):
                                rr = slice(hh * 64, (hh + 1) * 64)
                                S.stt(S_f[rr, j, :], S_f[rr, j, :], eb[rr, j, 127:128],
                                      Bk[rr, 256 + hh * 128:384 + hh * 128], ALU.mult, ALU.add,
                                      ['S_f%d' % j, 'eb%d' % j, bk], ['S_f%d' % j])
                            S.copy('dve', S_b[:, j, :], S_f[:, j, :], ['S_f%d' % j], ['S_b%d' % j])

                        def g8(j=j, Bk=Bk, bk=bk, box=box):
                            sl = box['sl']
                            for hh in range(2):
                                S.ts('dve', o_r[:, 2 * j + hh, :], Bk[:, hh * 128:(hh + 1) * 128],
                                     gstat[:, sl, 4 + hh:5 + hh], ALU.mult, [bk, 'gst%d' % sl], ['o_r%d' % (2 * j + hh)])

                        def g9(j=j, Bk=Bk, bk=bk):
                            for hh in range(2):
                                S.transpose(Bk[:, 256 + hh * 128:384 + hh * 128], o_r[:, 2 * j + hh, :], ident_f,
                                            ['o_r%d' % (2 * j + hh), 'cst'], [bk])

                        def g10(j=j, Bk=Bk, bk=bk, n=n, tsl_=tsl_):
                            for hh in range(2):
                                head = 2 * j + hh
                                S.stt(mixT[:, 4 + head, n * 128:(n + 1) * 128], Bk[:, 256 + hh * 128:384 + hh * 128],
                                      gnorm, silu_rg[:, head, tsl_], ALU.mult, ALU.mult, [bk, 'cst', 'silu_rg'],
                                      ['mix%d' % (4 + head)])
                        pair_lists.append([g1, g2, g3, g4, g5, g6, g7, g8, g9, g10])
                    for a_, b_ in zip(pair_lists[0], pair_lists[1]):
                        lists.append(lambda a_=a_, b_=b_: (a_(), b_()))
                return lists

            def drain(lst, n):
                for _ in range(min(n, len(lst))):
                    lst.pop(0)()

            def proj_fm(w, wk, hkeys, hs, cb, bi):
                for k in range(8):
                    S.matmul(pb[bi][:, :], w[:, k, cb * 128:(cb + 1) * 128], hT[:, k, hs], k == 0, k == 7,
                             hkeys + [wk], ['b%d' % bi])

            def proj_tm(w, wk, hkey, hcol, bi, c0, c1):
                for k in range(8):
                    S.matmul(pb[bi][:, 0:c1 - c0], hT[:, k, hcol:hcol + 128], w[:, k, c0:c1], k == 0, k == 7,
                             [hkey, wk], ['b%d' % bi])

            first_norm = norm_stages(0)
            first_norm[0]()
            first_norm[2]()
            for j in range(4):
                S.memset('dve', VA[:, :, j, 64:128], 1.0, ['VA'])
            S.memset('dve', QTx[:, :, :], 0.0, ['QTq', 'QTm'])
            S.memset('dve', glrT[0:32, :], 1.0, ['glrT'])
            S.memset('dve', allowed[:, :, :], 0.0, ['allowed'])
            S.ts('dve', allowed[:, :, 72:73], c31.unsqueeze(2), -64.0, ALU.add, ['cst', 'allowed'], ['allowed'])
            S.memset('dve', qtl2[:, :, :, :], 0.0, ['qtl0', 'qtl1'])
            S.memset('dve', ktl2[:, :, :, :], 0.0, ['ktl0', 'ktl1'])
            S.memset('dve', S_f[:, :, :], 0.0, ['S_f0', 'S_f1'])
            S.memset('dve', S_b[:, :, :], 0.0, ['S_b0', 'S_b1'])
            S.memset('dve', ksum_f[:, :, :], 0.0, ['ksum_f'])
            S.memset('dve', ksum_xf, 0.0, ['ksum_x'])

            for i_ in (1, 4, 3, 6, 5, 7):
                first_norm[i_]()
            for g in range(NG):
                hs0 = (g % 2) * 512
                hs = slice(hs0, hs0 + 512)
                hkeys = ['hTa%d_%d' % (g % 2, t) for t in range(4)]
                gs = slice(g * 512, (g + 1) * 512)
                S.checkpoint('A_norm')
                ns = norm_stages(g + 1) if g + 1 < NG else []
                for k in range(8):
                    S.matmul(pb[5][0:16, :], wglr[:, k, :], hT[:, k, hs], k == 0, k == 7, hkeys + ['cstb'], ['b5'])
                S.copy('act', glrT[0:16, :], pb[5][0:16, :], ['b5'], ['glrT'])
                w, wk = wget()
                for cb in range(4):
                    bi = 1 + cb % 2
                    proj_fm(w, wk, hkeys, hs, cb, bi)
                    S.act(QTx[0:64, 2 * cb, :], pb[bi][0:64, :], AF.Copy, ['b%d' % bi], ['QTq'], scale=0.125)
                    S.ts('dve', QTx[0:64, 2 * cb + 1, :], pb[bi][64:128, :], 0.125, ALU.mult, ['b%d' % bi], ['QTq'])
                for t in range(4):
                    bi = 5 + t % 2
                    S.matmul(pb[bi][:, 0:256], glrT[0:17, t * 128:(t + 1) * 128], w_aug[0:17, :], True, True,
                             ['glrT', 'cstb'], ['b%d' % bi])
                    S.act(etmp[:, 0, :], pb[bi][:, 0:256], AF.Exp, ['b%d' % bi], ['etmp0'], scale=-1.0)
                    S.act(sp_tok[:, t, :], etmp[:, 0, :], AF.Ln, ['etmp0'], ['sp_tok'], bias=1.0)
                w, wk = wget()
                for cb in range(4):
                    bi = 1 + cb % 2
                    proj_fm(w, wk, hkeys, hs, cb, bi)
                    S.op('dve', lambda e, a=(ksum_f[:, cb, 2 * g:2 * g + 2],
                                             pb[bi][:, :].rearrange("p (b k) -> p b k", k=256)):
                         e.reduce_sum(a[0], a[1], AXL.X), ['b%d' % bi], ['ksum_f'])
                    S.copy('act', KTx[0:64, 2 * cb, gs], pb[bi][0:64, :], ['b%d' % bi], ['KTx'])
                    S.copy('dve', KTx[0:64, 2 * cb + 1, gs], pb[bi][64:128, :], ['b%d' % bi], ['KTx'])
                S.ts('dve', ksum_x4[0:64, :, 0, :], ksum_f[0:64, :, :], 1.0 / 256, ALU.mult, ['ksum_f'], ['ksum_x'])
                S.ts('dve', ksum_x4[0:64, :, 1, :], ksum_f[64:128, :, :], 1.0 / 256, ALU.mult, ['ksum_f'], ['ksum_x'])
                ms = mask_stages(g)
                drain(ms, 1)
                drain(ns, 1)
                w, wk = wget()
                for t in range(4):
                    bi = 1 + t % 2
                    proj_tm(w, wk, hkeys[t], hs0 + t * 128, bi, 0, 512)
                    pv4 = pb[bi][:, :].rearrange("p (j e d) -> p j e d", e=2, d=64)
                    S.copy('act', VA[:, g * 4 + t, :, 0:64], pv4[:, :, 0, :], ['b%d' % bi], ['VA'])
                    S.copy('dve', VA[:, g * 4 + t, :, 128:192], pv4[:, :, 1, :], ['b%d' % bi], ['VA'])
                drain(ms, 2)
                drain(ns, 2)
                w, wk = wget()
                for cb in range(4):
                    bi = 1 + cb % 2
                    proj_fm(w, wk, hkeys, hs, cb, bi)
                    if cb < 2:
                        S.act(QGT[:, cb, :], pb[bi][:, :], AF.Copy, ['b%d' % bi], ['QGT'], scale=0.125)
                    else:
                        S.copy('dve', KGT[:, cb - 2, :], pb[bi][:, :], ['b%d' % bi], ['KGT'])
                for t in range(4):
                    bi = 1 + t % 2
                    proj_tm(w, wk, hkeys[t], hs0 + t * 128, bi, 256, 512)
                    evac_copy(KGtok[:, t, :], pb[bi][:, 0:256], ['b%d' % bi], ['KGtok'])
                drain(ms, 2)
                drain(ns, 2)
                w, wk = wget()
                for t in range(4):
                    bi = 1 + t % 2
                    proj_tm(w, wk, hkeys[t], hs0 + t * 128, bi, 0, 512)
                    evac_copy(VGtok[:, t, :], pb[bi][:, :], ['b%d' % bi], ['VGtok'])
                drain(ms, 2)
                drain(ns, 2)
                w, wk = wget()
                for cb in range(4):
                    bi = 1 + cb % 2
                    proj_fm(w, wk, hkeys, hs, cb, bi)
                    S.act(silu_rg[:, cb, :], pb[bi][:, :], AF.Silu, ['b%d' % bi], ['silu_rg'])
                drain(ms, len(ms))
                drain(ns, len(ns))
                S.checkpoint('A_proj')
                S.checkpoint('B_mask')
                deferred = gla_stages(g)
                nkt = 4 * g + 4
                iters = [(h, kt) for h in range(8) for kt in range(nkt)]
                SB = [3, 4, 7]
                OB = [5, 6]

                def emit_qk(idx):
                    h, kt = iters[idx]
                    qlo = max(0, kt - 4 * g) * 128
                    bi = SB[idx % 3]
                    S.matmul(pb[bi][:, qlo:512], KTx[:, h, kt * 128:(kt + 1) * 128], QTx[:, h, qlo:512],
                             True, True, ['KTx', 'QTq', 'QTm', 'cstb'], ['b%d' % bi])

                def emit_exp(idx):
                    h, kt = iters[idx]
                    i0 = kt - 4 * g
                    bi = SB[idx % 3]
                    bk = 'b%d' % bi
                    bank = pb[bi]
                    ps_ = idx % 4
                    ptk = 'PT%d' % ps_
                    qlo = max(0, i0) * 128
                    if i0 >= -1:
                        na = max(0, i0) * 128
                        nb_ = min(i0 + 2, 4) * 128
                        off = 128 if i0 == -1 else 0
                        n = nb_ - na
                        S.tt('dve', bank[:, na:nb_], bank[:, na:nb_], t5b[:, h, off:off + n], ALU.add,
                             [bk, 'cst2'], [bk])
                    S.act(PT[:, ps_, qlo:512], bank[:, qlo:512], AF.Exp, [bk], [ptk])

                def emit_pv(idx):
                    h, kt = iters[idx]
                    j, e_ = h // 2, h % 2
                    qlo = max(0, kt - 4 * g) * 128
                    ob = OB[h % 2]
                    obk = 'b%d' % ob
                    ps_ = idx % 4
                    S.matmul(pb[ob][:, qlo:512], VA[:, kt, j, e_ * 64:e_ * 64 + 128], PT[:, ps_, qlo:512],
                             kt == 0, kt == nkt - 1, ['VA', 'PT%d' % ps_], [obk])
                    if kt == nkt - 1:
                        sl = 0
                        lo, hi = (64, 128) if e_ == 0 else (0, 64)
                        oo, oh = (0, 64) if e_ == 0 else (64, 128)
                        tmpf = tmpS[:, :, :].rearrange("p a b -> p (a b)")
                        recip2(rl_sb[oo:oh, sl, :], pb[ob][lo:hi, :], tmpf[lo:hi, :], [obk], 'rl%d' % sl, 'rscrA')
                        S.tt('dve', mixT[oo:oh, j, gs], pb[ob][oo:oh, :], rl_sb[oo:oh, sl, :], ALU.mult,
                             [obk, 'rl%d' % sl], ['mix%d' % j])

                nit = len(iters)
                n_def = len(deferred)
                emit_qk(0)
                if nit > 1:
                    emit_qk(1)
                for idx in range(nit):
                    if idx + 2 < nit:
                        emit_qk(idx + 2)
                    emit_exp(idx)
                    emit_pv(idx)
                    want = min(n_def, -(-((idx + 1) * n_def) // max(1, (3 * nit) // 4)))
                    drain(deferred, want - (n_def - len(deferred)))
                drain(deferred, len(deferred))
                S.checkpoint('B_attn')
            S.checkpoint('B_gla')
            if debug:
                S.dma('sp', 'dbg', dbg_d, mixT[:].rearrange("p a b -> p (a b)"), ['mix%d' % k for k in range(8)], ['dbg_d'])

            S.barrier()
            AR.reset()
            kxT = AR.bf16(8 * 256).rearrange("p (a b) -> p a b", b=256)
            vx = AR.bf16(2 * 1024).rearrange("p (a b) -> p a b", b=1024)
            qxT = AR.bf16(2 * GC).rearrange("p (a b) -> p a b", b=GC)
            big = AR.bf16(8 * GC).rearrange("p (a b) -> p a b", b=GC)
            PTx = AR.bf16(2 * 512).rearrange("p (a b) -> p a b", b=512)
            rtmp = AR.bf16(2 * 512).rearrange("p (a b) -> p a b", b=512)
            x1 = AR.f32(TC * D).rearrange("p (a b) -> p a b", b=D)
            rlx = AR.f32(512)
            rscr = AR.f32(512)
            gfin = AR.f32(D)
            S.dma('sp', 'gfin', gfin, gfin_d, (), ['gfin'])

            mkeys = ['hT0', 'hT1']
            for mt in range(2):
                xs = mt
                S.dma('sp', 'xt%d' % xs, xt[:, xs, :], mem_d[s, mt * 128:(mt + 1) * 128, :], (), ['xt%d' % xs])
                norm_to_hT(xt[:, xs, :], 'xt%d' % xs, gmem, mt * 128, mkeys[mt])
            for c in range(4):
                w, wk = wget()
                if c < 2:
                    for cb in range(4):
                        bi = 1 + cb % 2
                        for k in range(8):
                            S.matmul(pb[bi][:, 0:256], w[:, k, cb * 128:(cb + 1) * 128], hT[:, k, 0:256], k == 0, k == 7,
                                     mkeys + [wk], ['b%d' % bi])
                        evac_copy(kxT[:, c * 4 + cb, :], pb[bi][:, 0:256], ['b%d' % bi], ['kxT'])
                else:
                    for mt in range(2):
                        bi = 1 + mt % 2
                        for k in range(8):
                            S.matmul(pb[bi][:, :], hT[:, k, mt * 128:(mt + 1) * 128], w[:, k, :], k == 0, k == 7,
                                     [mkeys[mt], wk], ['b%d' % bi])
                        evac_copy(vx[:, mt, (c - 2) * 512:(c - 1) * 512], pb[bi][:, :], ['b%d' % bi], ['vx'])

            S.checkpoint('C_mem')
            for cg in range(NCG):
                tok0 = cg * GC
                hk = ['hT%d' % t for t in range(TC)]
                mixk = ['mix%d' % k for k in range(8)]
                for t in range(TC):
                    S.dma('sp', 'x1_%d' % t, x1[:, t, :], x_d[s, tok0 + t * 128:tok0 + (t + 1) * 128, :], (), ['x1_%d' % t])
                wA = wget()
                wB = wget(ahead=1)
                pend = None
                for t in range(TC):
                    for c, (w, wk) in enumerate((wA, wB)):
                        bi = 3 + (2 * t + c) % 4
                        for k in range(8):
                            S.matmul(pb[bi][:, :], mixT[:, k, tok0 + t * 128:tok0 + (t + 1) * 128], w[:, k, :],
                                     k == 0, k == 7, mixk + [wk], ['b%d' % bi])
                        S.tt('dve', x1[:, t, c * 512:(c + 1) * 512], pb[bi][:, :], x1[:, t, c * 512:(c + 1) * 512],
                             ALU.add, ['b%d' % bi, 'x1_%d' % t], ['x1_%d' % t])
                    r_ = norm_part1(x1[:, t, :], 'x1_%d' % t)
                    if pend is not None:
                        norm_part2(*pend)
                    pend = (r_, gxat, t * 128, hk[t])
                norm_part2(*pend)
                S.checkpoint('C_out')
                def xq_proj(xh, w, wk):
                    for c2 in range(2):
                        cb = (xh % 2) * 2 + c2
                        for hf in range(GC // 512):
                            bi = 1 + hf % 2
                            for k in range(8):
                                S.matmul(pb[bi][:, :], w[:, k, cb * 128:(cb + 1) * 128], hT[:, k, hf * 512:(hf + 1) * 512],
                                         k == 0, k == 7, hk + [wk], ['b%d' % bi])
                            S.act(qxT[:, c2, hf * 512:(hf + 1) * 512], pb[bi][:, :], AF.Copy, ['b%d' % bi], ['qxT'],
                                  scale=1.0 / 16)

                NHF = GC // 512
                STB = [(5, 6), (3, 4)]
                w, wk = wget()
                xq_proj(0, w, wk)
                for xh in range(4):
                    for hf in range(NHF):
                        hs = slice(hf * 512, (hf + 1) * 512)
                        for mt in range(2):
                            bi = STB[hf % 2][mt]
                            for c2 in range(2):
                                S.matmul(pb[bi][:, :], kxT[:, 2 * xh + c2, mt * 128:(mt + 1) * 128], qxT[:, c2, hs],
                                         c2 == 0, c2 == 1, ['kxT', 'qxT'], ['b%d' % bi])
                    for hf in range(NHF):
                        hs = slice(hf * 512, (hf + 1) * 512)
                        for mt in range(2):
                            bi = STB[hf % 2][mt]
                            S.act(PTx[:, mt, :], pb[bi][:, :], AF.Exp, ['b%d' % bi], ['PTx%d' % mt])
                        if hf == 0 and xh + 1 < 4:
                            if (xh + 1) % 2 == 0:
                                w, wk = wget()
                            xq_proj(xh + 1, w, wk)
                        for mt in range(2):
                            S.matmul(pb[7][:, :], ones_b[:], PTx[:, mt, :], mt == 0, mt == 1, ['ones_b', 'PTx%d' % mt], ['b7'])
                        recip2(rlx, pb[7][:, :], rscr, ['b7'], 'rlx', 'rscr')
                        for c2 in range(2):
                            bi = 1 + c2
                            for mt in range(2):
                                S.matmul(pb[bi][:, :], vx[:, mt, (2 * xh + c2) * 128:(2 * xh + c2 + 1) * 128], PTx[:, mt, :],
                                         mt == 0, mt == 1, ['vx', 'PTx%d' % mt], ['b%d' % bi])
                            S.tt('dve', big[:, 2 * xh + c2, hs], pb[bi][:, :], rlx, ALU.mult, ['b%d' % bi, 'rlx'], ['big'])
                wA = wget()
                wB = wget(ahead=1)
                pend = None
                for t in range(TC):
                    for c, (w, wk) in enumerate((wA, wB)):
                        bi = 3 + (2 * t + c) % 4
                        for k in range(8):
                            S.matmul(pb[bi][:, :], big[:, k, t * 128:(t + 1) * 128], w[:, k, :], k == 0, k == 7,
                                     ['big', wk], ['b%d' % bi])
                        S.tt('dve', x1[:, t, c * 512:(c + 1) * 512], pb[bi][:, :], x1[:, t, c * 512:(c + 1) * 512],
                             ALU.add, ['b%d' % bi, 'x1_%d' % t], ['x1_%d' % t])
                    r_ = norm_part1(x1[:, t, :], 'x1_%d' % t)
                    if pend is not None:
                        norm_part2(*pend)
                    pend = (r_, gmlp, t * 128, hk[t])
                norm_part2(*pend)
                S.checkpoint('C_xattn')
                for q in range(4):
                    for c in range(2):
                        w, wk = wget()
                        for cb in range(4):
                            for hf in range(GC // 512):
                                bi = 1 + rot('acc', 2)
                                for k in range(8):
                                    S.matmul(pb[bi][:, :], w[:, k, cb * 128:(cb + 1) * 128], hT[:, k, hf * 512:(hf + 1) * 512],
                                             k == 0, k == 7, hk + [wk], ['b%d' % bi])
                                rsl = rot('ts', 2)
                                S.act(rtmp[:, rsl, :], pb[bi][:, :], AF.Relu, ['b%d' % bi], ['rtmp%d' % rsl])
                                S.tt('dve', big[:, c * 4 + cb, hf * 512:(hf + 1) * 512], rtmp[:, rsl, :], rtmp[:, rsl, :],
                                     ALU.mult, ['rtmp%d' % rsl], ['big'])
                    for c in range(2):
                        w, wk = wget()
                        for t in range(TC):
                            bi = 3 + t % 4
                            for k in range(8):
                                S.matmul(pb[bi][:, :], big[:, k, t * 128:(t + 1) * 128], w[:, k, :], k == 0, k == 7,
                                         ['big', wk], ['b%d' % bi])
                            S.tt('dve', x1[:, t, c * 512:(c + 1) * 512], pb[bi][:, :], x1[:, t, c * 512:(c + 1) * 512],
                                 ALU.add, ['b%d' % bi, 'x1_%d' % t], ['x1_%d' % t])
                S.checkpoint('C_mlp')
                for t in range(TC):
                    rs, rk = rstd_of(x1[:, t, :], 'x1_%d' % t, D)
                    xs = rot('xs', 2)
                    S.stt(xt[:, xs, :], x1[:, t, :], rs, gfin, ALU.mult, ALU.mult, ['x1_%d' % t, rk, 'gfin'], ['xt%d' % xs])
                    S.dma('sp', 'out%d' % xs, out_d[s, tok0 + t * 128:tok0 + (t + 1) * 128, :], xt[:, xs, :],
                          ['xt%d' % xs], ['out_d%d' % xs])
            S.barrier()

        final_keys = ['out_d0', 'out_d1'] + (['dbg_d'] if debug else [])
        S.wait_all('sp', final_keys)
        S.barrier()
        assert S.stopped or wstate['i'] == len(wplan), (wstate['i'], len(wplan))
        with nc.allow_non_contiguous_dma(reason="strided weight / constant loads"):
            S.emit()
    return nc


def _rel_bucket_np(d):
    d = np.maximum(d, 0)
    large = 16 + (np.log(np.maximum(d, 1).astype(np.float32) / np.float32(16)) / np.float32(math.log(128 / 16))
                  * np.float32(16)).astype(np.int32)
    large = np.minimum(large, 31)
    return np.where(d < 16, d, large)


def _consts(T=2048):
    cst = np.zeros((128, 1024), np.float32)
    i = np.arange(128)
    cst[:, 0:128] = np.eye(128, dtype=np.float32)
    cst[:, 128:256] = np.where(i[:, None] <= i[None, :], -1.0 / 16, 0.0)
    cst[:, 256:384] = np.where(i[:, None] > i[None, :], -1.0 / 16, 0.0)
    cst[:, 384:512] = np.where(i[:, None] <= i[None, :], 1.0, 0.0)
    for B in range(8):
        row = np.where(np.arange(8) < B, 0.0, np.where(np.arange(8) == B, 64.0, -64.0)).astype(np.float32)
        cst[:, 512 + B * 64:512 + (B + 1) * 64] = np.tile(row, 8)[None, :]
    blk = np.zeros((9, T), np.float32)
    kpos = np.arange(T)
    for b in range(8):
        blk[b, :] = (kpos // 256 == b)
    blk[8, :] = 1.0
    return cst, blk


def _host_inputs(rp_table, norm_mix, g_norm, norm_xattn, norm_mem, norm_mlp, norm_final, T=2048):
    cst, blk = _consts(T)
    vecs = np.zeros((128, 48), np.float32)
    vecs[:, 0:8] = np.asarray(norm_mix, np.float32).reshape(8, 128).T
    vecs[:, 8:16] = np.asarray(norm_xattn, np.float32).reshape(8, 128).T
    vecs[:, 16:24] = np.asarray(norm_mlp, np.float32).reshape(8, 128).T
    vecs[:, 24:32] = np.asarray(norm_mem, np.float32).reshape(8, 128).T
    vecs[:, 32] = np.asarray(g_norm, np.float32).reshape(128)
    rp = np.asarray(rp_table, np.float32)
    vecs[:, 33:41] = rp[31][None, :]
    i = np.arange(128)[:, None]
    jj = np.arange(256)[None, :]
    dist = jj - i
    idx = _rel_bucket_np(dist)
    t5 = rp[idx]
    t5 = np.where((dist >= 0)[:, :, None], t5, np.float32(-30000.0)).astype(np.float32)
    t5 = np.ascontiguousarray(t5.transpose(0, 2, 1)).reshape(128, 2048)
    gfin = np.ascontiguousarray(np.broadcast_to(np.asarray(norm_final, np.float32).reshape(1, D), (128, D)))
    return cst, blk, vecs, t5, gfin


_NC_CACHE = {}


def kernel(x, mem, rp_table, norm_mix, w_in, w_gate_up, b_gate, g_norm, w_out, norm_xattn, norm_mem, w_xq, w_xkv,
           w_xo, norm_mlp, w_up, w_down, norm_final):
    x = np.asarray(x, np.float32)
    mem = np.asarray(mem, np.float32)
    Bt, T, _ = x.shape
    nseq = Bt // NCORES
    cst, blk, vecs, t5, gfin = _host_inputs(rp_table, norm_mix, g_norm, norm_xattn, norm_mem, norm_mlp, norm_final, T)
    key = (nseq, T)
    if key not in _NC_CACHE:
        _NC_CACHE[key] = build(nseq, T)
    nc = _NC_CACHE[key]
    f = lambda a: np.ascontiguousarray(np.asarray(a, np.float32))
    shared = {
        "w_in": f(w_in[0]), "w_gate_up": f(w_gate_up[0]), "b_gate": f(b_gate[0]).reshape(1, 256),
        "w_out": f(w_out[0]), "w_xq": f(w_xq[0]), "w_xkv": f(w_xkv[0]), "w_xo": f(w_xo[0]),
        "w_up": f(w_up[0]), "w_down": f(w_down[0]),
        "cst": cst, "blkind": blk, "vecs": vecs, "t5b": t5, "gfin": gfin,
    }
    in_maps = []
    for c in range(NCORES):
        m = dict(shared)
        m["x"] = np.ascontiguousarray(x[c * nseq:(c + 1) * nseq])
        m["mem"] = np.ascontiguousarray(mem[c * nseq:(c + 1) * nseq])
        in_maps.append(m)
    res = run_bass_kernel_spmd(nc, in_maps, core_ids=list(range(NCORES)))
    out = np.concatenate([np.asarray(r["out"], np.float32) for r in res.results], axis=0)
    return out
```

```python
import math
from contextlib import ExitStack

import numpy as np
import concourse.bass as bass
import concourse.mybir as mybir
from concourse.bass_utils import run_bass_kernel_spmd

F32 = mybir.dt.float32
BF16 = mybir.dt.bfloat16
AF = mybir.ActivationFunctionType
ALU = mybir.AluOpType
AXL = mybir.AxisListType

D = 1024
MEM = 256
IN_COLS = 3088
DFF = 4096
EPS = 1e-6
NCORES = 8

ENGS = ['pe', 'act', 'dve', 'pool', 'sp']
PSUM_KEYS = set('b%d' % i for i in range(8))
SAME_ENG_SYNC = {'pe': False, 'act': True, 'dve': True, 'pool': True, 'sp': False}


class Sched:
    def __init__(self, nc, st):
        self.nc = nc
        self.st = st
        self.semobj = {}
        for e in ENGS:
            self.semobj[e] = st.enter_context(nc.semaphore('q_' + e))
        self.cnt = {e: 0 for e in ENGS}
        self.known = {e: {} for e in ENGS}
        self.prog = {e: [] for e in ENGS}
        self.lastw = {}
        self.readers = {}
        self.dmacnt = {}
        self.stopped = False
        self.stop_at = None

    def checkpoint(self, label):
        if self.stop_at is not None and label == self.stop_at:
            self.stopped = True

    def _sem(self, key):
        if key not in self.semobj:
            self.semobj[key] = self.st.enter_context(self.nc.semaphore('d_' + key))
            self.dmacnt[key] = 0
        return self.semobj[key]

    def _deps(self, reads, writes, eng=None):
        deps = {}

        def add(s, v):
            if deps.get(s, 0) < v:
                deps[s] = v
        for r in reads:
            w = self.lastw.get(r)
            if w is not None:
                add(*w)
            if r in PSUM_KEYS:
                for s_, v in self.readers.get(r, {}).items():
                    if s_ != eng:
                        add(s_, v)
        for w_ in writes:
            w = self.lastw.get(w_)
            if w is not None:
                add(*w)
            for s, v in self.readers.get(w_, {}).items():
                add(s, v)
        return deps

    def _waits(self, eng, deps):
        waits = []
        for s, v in deps.items():
            if s == eng and not SAME_ENG_SYNC[eng]:
                continue
            if self.known[eng].get(s, 0) < v:
                waits.append((s, v))
                self.known[eng][s] = v
        return waits

    def _mark(self, reads, writes, s, v):
        for r in reads:
            d = self.readers.setdefault(r, {})
            if d.get(s, 0) < v:
                d[s] = v
        for w in writes:
            self.lastw[w] = (s, v)
            self.readers[w] = {}

    def op(self, eng, fn, reads=(), writes=()):
        if self.stopped:
            return
        deps = self._deps(reads, writes, eng)
        waits = self._waits(eng, deps)
        self.cnt[eng] += 1
        v = self.cnt[eng]
        self.prog[eng].append((waits, fn, eng, 1))
        self._mark(reads, writes, eng, v)

    def dma(self, q, semkey, out, in_, reads=(), writes=()):
        if self.stopped:
            return
        self._sem(semkey)
        deps = self._deps(reads, writes)
        waits = self._waits(q, deps)
        self.dmacnt[semkey] += 16
        v = self.dmacnt[semkey]
        fn = (lambda e, o=out, i=in_: e.dma_start(out=o, in_=i))
        self.prog[q].append((waits, fn, semkey, 16))
        self._mark(reads, writes, semkey, v)

    def wait_all(self, eng, keys):
        deps = self._deps(keys, ())
        waits = self._waits(eng, deps)
        if waits:
            self.prog[eng].append((waits, None, None, 0))

    def barrier(self):
        for e in ENGS:
            deps = {}
            for e2 in ENGS:
                if e2 != e and e2 != 'sp' and self.cnt[e2] > 0:
                    deps[e2] = self.cnt[e2]
            for k, v in self.dmacnt.items():
                if v > 0:
                    deps[k] = v
            waits = self._waits(e, deps)
            if waits:
                self.prog[e].append((waits, None, None, 0))

    def replay(self, name, e):
        for waits, fn, semname, inc in self.prog[name]:
            for s, v in waits:
                e.wait_ge(self.semobj[s], v)
            if fn is None:
                continue
            inst = fn(e)
            inst.then_inc(self.semobj[semname], inc)

    def emit(self):
        nc = self.nc
        with nc.Block() as block:
            @block.tensor
            def _(e):
                self.replay('pe', e)

            @block.scalar
            def _(e):
                self.replay('act', e)

            @block.vector
            def _(e):
                self.replay('dve', e)

            @block.gpsimd
            def _(e):
                self.replay('pool', e)

            @block.sync
            def _(e):
                self.replay('sp', e)

    def matmul(self, out, lhsT, rhs, start, stop, reads, writes):
        self.op('pe', lambda e, a=(out, lhsT, rhs, start, stop): e.matmul(a[0], a[1], a[2], start=a[3], stop=a[4]),
                reads, writes)

    def transpose(self, out, in_, ident, reads, writes):
        self.op('pe', lambda e, a=(out, in_, ident): e.transpose(a[0], a[1], a[2]), reads, writes)

    def act(self, out, in_, func, reads, writes, bias=None, scale=None, accum_out=None):
        kw = {}
        if bias is not None:
            kw['bias'] = bias
        if scale is not None:
            kw['scale'] = scale
        if accum_out is not None:
            kw['accum_out'] = accum_out
        self.op('act', lambda e, a=(out, in_, func), kw=kw: e.activation(a[0], a[1], a[2], **kw), reads, writes)

    def tt(self, eng, out, in0, in1, op, reads, writes):
        self.op(eng, lambda e, a=(out, in0, in1, op): e.tensor_tensor(a[0], a[1], a[2], a[3]), reads, writes)

    def ts(self, eng, out, in0, s1, op0, reads, writes, s2=None, op1=None):
        kw = {}
        if op1 is not None:
            kw['op1'] = op1
        self.op(eng, lambda e, a=(out, in0, s1, s2, op0), kw=kw: e.tensor_scalar(a[0], a[1], a[2], a[3], a[4], **kw),
                reads, writes)

    def stt(self, out, in0, scalar, in1, op0, op1, reads, writes):
        self.op('dve', lambda e, a=(out, in0, scalar, in1, op0, op1):
                e.scalar_tensor_tensor(a[0], a[1], a[2], a[3], a[4], a[5]), reads, writes)

    def copy(self, eng, out, in_, reads, writes):
        if eng == 'act':
            self.op('act', lambda e, a=(out, in_): e.copy(a[0], a[1]), reads, writes)
        else:
            self.op(eng, lambda e, a=(out, in_): e.tensor_copy(a[0], a[1]), reads, writes)

    def memset(self, eng, ap, val, writes):
        self.op(eng, lambda e, a=(ap, val): e.memset(a[0], a[1]), (), writes)


class Arena:
    def __init__(self, tile, ncols):
        self.t = tile
        self.n = ncols
        self.off = 0

    def reset(self):
        self.off = 0

    def f32(self, ncols):
        assert self.off + ncols <= self.n, (self.off, ncols, self.n)
        ap = self.t[:, self.off:self.off + ncols]
        self.off += ncols
        return ap

    def bf16(self, ncols):
        n32 = (ncols + 1) // 2
        assert self.off + n32 <= self.n, (self.off, n32, self.n)
        ap = self.t[:, self.off:self.off + n32].bitcast(BF16)
        self.off += n32
        return ap


def build(nseq=2, T=2048, GC=1024, debug=False, stop_at=None):
    NT = T // 128
    NG = T // 512
    GC = min(GC, T)
    NCG = T // GC
    TC = GC // 128
    nc = bass.Bass("TRN2", target_bir_lowering=False)

    def dt(name, shape, dtype=F32, kind="ExternalInput"):
        return nc.dram_tensor(name, shape, dtype, kind=kind).ap()

    x_d = dt("x", [nseq, T, D])
    mem_d = dt("mem", [nseq, MEM, D])
    w_in_d = dt("w_in", [D, IN_COLS])
    w_gu_d = dt("w_gate_up", [16, 256])
    b_gate_d = dt("b_gate", [1, 256])
    w_out_d = dt("w_out", [D, D])
    w_xq_d = dt("w_xq", [D, D])
    w_xkv_d = dt("w_xkv", [D, 2 * D])
    w_xo_d = dt("w_xo", [D, D])
    w_up_d = dt("w_up", [D, DFF])
    w_down_d = dt("w_down", [DFF, D])
    cst_d = dt("cst", [128, 1024])
    blk_d = dt("blkind", [9, T])
    vecs_d = dt("vecs", [128, 48])
    t5_d = dt("t5b", [128, 2048])
    gfin_d = dt("gfin", [128, D])
    out_d = dt("out", [nseq, T, D], kind="ExternalOutput")
    if debug:
        dbg_d = dt("dbg", [128, 8 * T], BF16, kind="ExternalOutput")

    def wv(w, c0, n):
        return w.rearrange("(k p) c -> p k c", p=128)[:, :, c0:c0 + n]

    with ExitStack() as st:
        S = Sched(nc, st)
        S.stop_at = stop_at

        def sb(name, shape, dtype):
            return st.enter_context(nc.sbuf_tensor("sb_" + name, shape, dtype))

        def psb(name, shape, dtype):
            return st.enter_context(nc.psum_tensor(name, shape, dtype))

        cst = sb("cst", [128, 1024], F32)
        ident_f = cst[:, 0:128]
        triU = cst[:, 128:256]
        strictL = cst[:, 256:384]
        pbias = cst[:, 512:1024]
        ident_b = sb("ident_b", [128, 128], BF16)
        causal_b = sb("causal_b", [128, 128], BF16)
        ones_b = sb("ones_b", [128, 128], BF16)
        ones_f = sb("ones_f", [128, 64], F32)
        KTx = sb("KTx", [128, 8, T], BF16)
        t5b = sb("t5b", [128, 8, 256], F32)
        vecs = sb("vecs", [128, 48], F32)
        gmix, gxat, gmlp, gmem = vecs[:, 0:8], vecs[:, 8:16], vecs[:, 16:24], vecs[:, 24:32]
        gnorm = vecs[:, 32:33]
        c31 = vecs[:, 33:41]
        w_aug = sb("w_aug", [32, 256], BF16)
        wglr = sb("wglr", [128, 8, 16], BF16)
        stat = sb("stat", [128, 48], F32)
        xt = sb("xt", [128, 2, D], F32)
        junk = sb("junk", [128, D], BF16)
        xn = sb("xn", [128, 2, D], BF16)
        hT = sb("hT", [128, 8, 1024], BF16)
        wsl = sb("wsl", [128, 3, 8, 512], BF16)
        mixT = sb("mixT", [128, 8, T], BF16)
        ARENA_COLS = 18496
        arena_t = sb("arena", [128, ARENA_COLS], F32)
        AR = Arena(arena_t, ARENA_COLS)

        pb = [psb("pbank0", [128, 1024], BF16)] + [psb("pbank%d" % i, [128, 512], F32) for i in range(1, 8)]

        S.dma('sp', 'cst', cst[:], cst_d, (), ['cst'])
        S.dma('sp', 'cst', vecs[:], vecs_d, (), ['cst'])
        S.dma('sp', 'cst', t5b[:].rearrange("p a b -> p (a b)"), t5_d, (), ['cst'])
        S.dma('pool', 'cstb', ident_b[:], cst_d[:, 0:128], (), ['cstb'])
        S.dma('pool', 'cstb', causal_b[:], cst_d[:, 384:512], (), ['cstb'])
        S.memset('pool', KTx[64:128, :, :], 0.0, ['KTx'])
        for h in range(8):
            S.dma('pool', 'cstb', KTx[64:73, h, :], blk_d, ['KTx'], ['cstb', 'KTx'])
        S.dma('pool', 'cstb', w_aug[0:16, :], w_gu_d, (), ['cstb'])
        S.dma('pool', 'cstb', w_aug[16:17, :], b_gate_d, (), ['cstb'])
        S.dma('pool', 'cstb', wglr[:], wv(w_in_d, 2560, 16), (), ['cstb'])
        S.memset('dve', ones_b[:], 1.0, ['ones_b'])
        S.memset('dve', ones_f[:], 1.0, ['ones_f'])
        for h in range(8):
            S.ts('dve', t5b[:, h, :], t5b[:, h, :], c31[:, h:h + 1], ALU.subtract, ['cst'], ['cst2'])
        S.checkpoint('const')

        wplan = []
        for s in range(nseq):
            for g in range(NG):
                for c0 in (0, 512, 1024, 1536, 2048, 2576):
                    wplan.append(wv(w_in_d, c0, 512))
            for c in range(4):
                wplan.append(wv(w_xkv_d, c * 512, 512))
            for cg in range(NCG):
                for c in range(2):
                    wplan.append(wv(w_out_d, c * 512, 512))
                for c in range(2):
                    wplan.append(wv(w_xq_d, c * 512, 512))
                for c in range(2):
                    wplan.append(wv(w_xo_d, c * 512, 512))
                for q in range(4):
                    for c in range(2):
                        wplan.append(wv(w_up_d, q * 1024 + c * 512, 512))
                    for c in range(2):
                        wplan.append(w_down_d.rearrange("(q k p) n -> p q k n", k=8, p=128)[:, q, :, c * 512:(c + 1) * 512])
        wstate = {'i': 0, 'issued': 0}

        def wget(ahead=2):
            i = wstate['i']
            while wstate['issued'] <= min(i + ahead, len(wplan) - 1):
                n = wstate['issued']
                S.dma('pool', 'w%d' % (n % 3), wsl[:, n % 3], wplan[n], (), ['w%d' % (n % 3)])
                wstate['issued'] += 1
            wstate['i'] += 1
            return wsl[:, i % 3], 'w%d' % (i % 3)

        MG = [0, 0, 0, 1, 1, 2, 1, 3]
        MA = [0, 64, 32, 64, 0, 64, 32, 64]
        cnt = {'gst': 0, 'stat': 0, 'xs': 0, 'acc': 0, 'sc': 0, 'pt': 0, 'ts': 0, 'ev': 0}

        def rot(name, n):
            v = cnt[name] % n
            cnt[name] += 1
            return v

        def evac_copy(out, in_, reads, writes):
            if rot('ev', 2) == 0:
                S.copy('act', out, in_, reads, writes)
            else:
                S.copy('dve', out, in_, reads, writes)

        def recip2(out, in_, scratch, reads, okey, skey):
            S.act(scratch, in_, AF.Ln, reads, [skey])
            S.act(out, scratch, AF.Exp, [skey], [okey], scale=-1.0)

        def rstd_of(src, skey, nfeat):
            sl = rot('stat', 16)
            k = 'st%d' % sl
            ssq, lnv, rs = stat[:, 3 * sl:3 * sl + 1], stat[:, 3 * sl + 1:3 * sl + 2], stat[:, 3 * sl + 2:3 * sl + 3]
            S.act(junk[:, 0:nfeat], src, AF.Square, [skey], [k], accum_out=ssq)
            S.act(lnv, ssq, AF.Ln, [k], [k], bias=EPS, scale=1.0 / nfeat)
            S.act(rs, lnv, AF.Exp, [k], [k], scale=-0.5)
            return rs, k

        def norm_part1(src, skey):
            rs, k = rstd_of(src, skey, D)
            xs = rot('xs', 2)
            xk = 'xn%d' % xs
            S.act(xn[:, xs, :], src, AF.Copy, [skey, k], [xk], scale=rs)
            return xs, xk

        def norm_part2(r, gcols, col0, hkey):
            xs, xk = r
            for kk in range(8):
                S.transpose(pb[0][:, kk * 128:(kk + 1) * 128], xn[:, xs, kk * 128:(kk + 1) * 128], ident_b[:],
                            [xk, 'cstb'], ['b0'])
            S.tt('dve', hT[:, :, col0:col0 + 128], pb[0][:, :].rearrange("p (a b) -> p a b", b=128),
                 gcols.unsqueeze(2).to_broadcast([128, 8, 128]), ALU.mult, ['b0', 'cst'], [hkey])

        def norm_to_hT(src, skey, gcols, col0, hkey):
            norm_part2(norm_part1(src, skey), gcols, col0, hkey)

        for s in range(nseq):
            AR.reset()
            VA = AR.bf16(NT * 4 * 192).rearrange("p (a h d) -> p a h d", h=4, d=192)
            QTx = AR.bf16(8 * 512).rearrange("p (a b) -> p a b", b=512)
            QGT = AR.bf16(2 * 512).rearrange("p (a b) -> p a b", b=512)
            KGT = AR.bf16(2 * 512).rearrange("p (a b) -> p a b", b=512)
            KGtok = AR.bf16(4 * 256).rearrange("p (a b) -> p a b", b=256)
            VGtok = AR.bf16(4 * 512).rearrange("p (a b) -> p a b", b=512)
            silu_rg = AR.bf16(4 * 512).rearrange("p (a b) -> p a b", b=512)
            glrT = AR.bf16(512)
            PT = AR.bf16(4 * 512).rearrange("p (a b) -> p a b", b=512)
            allowed = AR.bf16(8 * 80).rearrange("p (a b) -> p a b", b=80)
            qtl2 = AR.bf16(4 * 128).rearrange("p (j h b) -> p j h b", h=2, b=128)
            ktl2 = AR.bf16(4 * 128).rearrange("p (j h b) -> p j h b", h=2, b=128)
            khat = AR.bf16(2 * 128).rearrange("p (a b) -> p a b", b=128)
            ATm = AR.bf16(4 * 128).rearrange("p (a b) -> p a b", b=128)
            S_b = AR.bf16(2 * 128).rearrange("p (a b) -> p a b", b=128)
            ksum_xf = AR.bf16(64)
            ksum_x3 = ksum_xf.rearrange("p (h b) -> p h b", b=8)
            ksum_x4 = ksum_xf.rearrange("p (j e b) -> p j e b", e=2, b=8)
            sp_tok = AR.f32(4 * 256).rearrange("p (a b) -> p a b", b=256)
            etmp = AR.f32(1 * 256).rearrange("p (a b) -> p a b", b=256)
            tmpS = AR.f32(2 * 256).rearrange("p (a b) -> p a b", b=256)
            gm = AR.f32(64)
            top8 = AR.f32(64)
            rl_sb = AR.f32(1 * 512).rearrange("p (a b) -> p a b", b=512)
            eb = AR.f32(2 * 128).rearrange("p (a b) -> p a b", b=128)
            enb = AR.f32(2 * 128).rearrange("p (a b) -> p a b", b=128)
            erev = AR.f32(2 * 128).rearrange("p (a b) -> p a b", b=128)
            o_r = AR.f32(4 * 128).rearrange("p (a b) -> p a b", b=128)
            S_f = AR.f32(2 * 128).rearrange("p (a b) -> p a b", b=128)
            ksum_f = AR.f32(32).rearrange("p (a b) -> p a b", b=8)
            gstat = AR.f32(48).rearrange("p (a b) -> p a b", b=6)

            def norm_stages(g):
                out_ = []
                hs0 = (g % 2) * 512
                for t in range(4):
                    tt = g * 4 + t
                    xs = tt % 2
                    box = {}

                    def n1(tt=tt, xs=xs, box=box):
                        S.dma('sp', 'xt%d' % xs, xt[:, xs, :], x_d[s, tt * 128:(tt + 1) * 128, :], (), ['xt%d' % xs])
                        box['r'] = norm_part1(xt[:, xs, :], 'xt%d' % xs)

                    def n2(t=t, box=box, g=g, hs0=hs0):
                        norm_part2(box['r'], gmix, hs0 + t * 128, 'hTa%d_%d' % (g % 2, t))
                    out_ += [n1, n2]
                return out_

            def mask_stages(g):
                out_ = []
                for t in range(4):
                    B = (4 * g + t) // 2
                    tsl_ = slice(t * 128, (t + 1) * 128)

                    def m1(B=B, tsl_=tsl_):
                        for h in range(8):
                            S.matmul(pb[6][:, h * 8:(h + 1) * 8], QTx[:, h, tsl_], ksum_x3[:, h, :], True, True,
                                     ['QTq', 'QTm', 'ksum_x'], ['b6'])
                        S.tt('dve', gm, pb[6][:, 0:64], pbias[:, B * 64:(B + 1) * 64], ALU.add, ['b6', 'cst'], ['gm'])
                        for h in range(8):
                            S.op('dve', lambda e, h=h: e.max(top8[:, h * 8:(h + 1) * 8], gm[:, h * 8:(h + 1) * 8]),
                                 ['gm'], ['top8'])
                        for h in range(8):
                            S.ts('dve', allowed[:, h, 64:72], gm[:, h * 8:(h + 1) * 8], top8[:, h * 8 + 3:h * 8 + 4],
                                 ALU.is_ge, ['gm', 'top8'], ['allowed'], s2=64.0, op1=ALU.mult)

                    def m2(tsl_=tsl_):
                        for hg in range(2):
                            for h4 in range(4):
                                S.matmul(pb[7][0:73, h4 * 128:(h4 + 1) * 128], allowed[:, hg * 4 + h4, 0:73], ident_b[:],
                                         True, True, ['allowed', 'cstb'], ['b7'])
                            S.copy('act', QTx[64:73, hg * 4:(hg + 1) * 4, tsl_],
                                   pb[7][64:73, :].rearrange("p (a b) -> p a b", b=128), ['b7'], ['QTm'])
                    out_ += [m1, m2]
                return out_

            def gla_stages(g):
                lists = []
                for t in range(4):
                    n = 4 * g + t
                    tsl_ = slice(t * 128, (t + 1) * 128)
                    pair_lists = []
                    for j in range(2):
                        Bk = pb[1 + j]
                        bk = 'b%d' % (1 + j)
                        js = slice(j * 128, (j + 1) * 128)
                        box = {}

                        def g1(t=t, j=j, Bk=Bk, bk=bk, js=js):
                            S.matmul(Bk[:, 0:128], sp_tok[:, t, js], triU, True, True, ['sp_tok', 'cst'], [bk])
                            S.matmul(Bk[:, 128:256], strictL, sp_tok[:, t, js], True, True, ['sp_tok', 'cst'], [bk])

                        def g2(j=j, Bk=Bk, bk=bk):
                            S.act(eb[:, j, :], Bk[:, 0:128], AF.Exp, [bk], ['eb%d' % j])
                            S.act(enb[:, j, :], Bk[:, 0:128], AF.Exp, [bk], ['enb%d' % j], scale=-1.0)
                            S.act(erev[:, j, :], Bk[:, 128:256], AF.Exp, [bk], ['erev%d' % j])

                        def g3(t=t, j=j, js=js, tsl_=tsl_):
                            for hh in range(2):
                                rr = slice(hh * 64, (hh + 1) * 64)
                                S.tt('dve', qtl2[rr, j, hh, :], QGT[rr, j, tsl_], eb[rr, j, :], ALU.mult,
                                     ['QGT', 'eb%d' % j], ['qtl%d' % j])
                                S.tt('dve', ktl2[rr, j, hh, :], KGT[rr, j, tsl_], enb[rr, j, :], ALU.mult,
                                     ['KGT', 'enb%d' % j], ['ktl%d' % j])
                            S.tt('dve', khat[:, j, :], KGtok[:, t, js], erev[:, j, :], ALU.mult,
                                 ['KGtok', 'erev%d' % j], ['khat%d' % j])

                        def g4(j=j, Bk=Bk, bk=bk):
                            for hh in range(2):
                                S.matmul(Bk[:, 256 + hh * 128:384 + hh * 128], ktl2[:, j, hh, :], qtl2[:, j, hh, :],
                                         True, True, ['ktl%d' % j, 'qtl%d' % j], [bk])

                        def g5(j=j, Bk=Bk, bk=bk):
                            for hh in range(2):
                                S.tt('dve', ATm[:, 2 * j + hh, :], Bk[:, 256 + hh * 128:384 + hh * 128], causal_b[:],
                                     ALU.mult, [bk, 'cstb'], ['ATm%d' % (2 * j + hh)])

                        def g6(t=t, j=j, Bk=Bk, bk=bk):
                            for hh in range(2):
                                head = 2 * j + hh
                                S.matmul(Bk[:, hh * 128:(hh + 1) * 128], qtl2[:, j, hh, :], S_b[:, j, :], True, False,
                                         ['qtl%d' % j, 'S_b%d' % j], [bk])
                                S.matmul(Bk[:, hh * 128:(hh + 1) * 128], ATm[:, head, :],
                                         VGtok[:, t, head * 128:(head + 1) * 128], False, True,
                                         ['ATm%d' % head, 'VGtok'], [bk])
                            for hh in range(2):
                                head = 2 * j + hh
                                S.matmul(Bk[:, 256 + hh * 128:384 + hh * 128], khat[:, j, :],
                                         VGtok[:, t, head * 128:(head + 1) * 128], True, True,
                                         ['khat%d' % j, 'VGtok'], [bk])

                        def g7(j=j, Bk=Bk, bk=bk, box=box):
                            sl = rot('gst', 8)
                            k = 'gst%d' % sl
                            box['sl'] = sl
                            for hh in range(2):
                                S.act(junk[:, 0:128], Bk[:, hh * 128:(hh + 1) * 128], AF.Square, [bk], [k],
                                      accum_out=gstat[:, sl, hh:hh + 1])
                            S.act(gstat[:, sl, 2:4], gstat[:, sl, 0:2], AF.Ln, [k], [k], bias=EPS, scale=1.0 / 128)
                            S.act(gstat[:, sl, 4:6], gstat[:, sl, 2:4], AF.Exp, [k], [k], scale=-0.5)
                            for hh in range(2):
                                rr = slice(hh * 64, (hh + 1) * 64)
                                S.stt(S_f[rr, j, :], S_f[rr, j, :], eb[rr, j, 127:128],
                                      Bk[rr, 256 + hh * 128:384 + hh * 128], ALU.mult, ALU.add,
                                      ['S_f%d' % j, 'eb%d' % j, bk], ['S_f%d' % j])
                            S.copy('dve', S_b[:, j, :], S_f[:, j, :], ['S_f%d' % j], ['S_b%d' % j])

                        def g8(j=j, Bk=Bk, bk=bk, box=box):
                            sl = box['sl']
                            for hh in range(2):
                                S.ts('dve', o_r[:, 2 * j + hh, :], Bk[:, hh * 128:(hh + 1) * 128],
                                     gstat[:, sl, 4 + hh:5 + hh], ALU.mult, [bk, 'gst%d' % sl], ['o_r%d' % (2 * j + hh)])

                        def g9(j=j, Bk=Bk, bk=bk):
                            for hh in range(2):
                                S.transpose(Bk[:, 256 + hh * 128:384 + hh * 128], o_r[:, 2 * j + hh, :], ident_f,
                                            ['o_r%d' % (2 * j + hh), 'cst'], [bk])

                        def g10(j=j, Bk=Bk, bk=bk, n=n, tsl_=tsl_):
                            for hh in range(2):
                                head = 2 * j + hh
                                S.stt(mixT[:, 4 + head, n * 128:(n + 1) * 128], Bk[:, 256 + hh * 128:384 + hh * 128],
                                      gnorm, silu_rg[:, head, tsl_], ALU.mult, ALU.mult, [bk, 'cst', 'silu_rg'],
                                      ['mix%d' % (4 + head)])
                        pair_lists.append([g1, g2, g3, g4, g5, g6, g7, g8, g9, g10])
                    for a_, b_ in zip(pair_lists[0], pair_lists[1]):
                        lists.append(lambda a_=a_, b_=b_: (a_(), b_()))
                return lists

            def drain(lst, n):
                for _ in range(min(n, len(lst))):
                    lst.pop(0)()

            def proj_fm(w, wk, hkeys, hs, cb, bi):
                for k in range(8):
                    S.matmul(pb[bi][:, :], w[:, k, cb * 128:(cb + 1) * 128], hT[:, k, hs], k == 0, k == 7,
                             hkeys + [wk], ['b%d' % bi])

            def proj_tm(w, wk, hkey, hcol, bi, c0, c1):
                for k in range(8):
                    S.matmul(pb[bi][:, 0:c1 - c0], hT[:, k, hcol:hcol + 128], w[:, k, c0:c1], k == 0, k == 7,
                             [hkey, wk], ['b%d' % bi])

            first_norm = norm_stages(0)
            first_norm[0]()
            first_norm[2]()
            for j in range(4):
                S.memset('dve', VA[:, :, j, 64:128], 1.0, ['VA'])
            S.memset('dve', QTx[:, :, :], 0.0, ['QTq', 'QTm'])
            S.memset('dve', glrT[0:32, :], 1.0, ['glrT'])
            S.memset('dve', allowed[:, :, :], 0.0, ['allowed'])
            S.ts('dve', allowed[:, :, 72:73], c31.unsqueeze(2), -64.0, ALU.add, ['cst', 'allowed'], ['allowed'])
            S.memset('dve', qtl2[:, :, :, :], 0.0, ['qtl0', 'qtl1'])
            S.memset('dve', ktl2[:, :, :, :], 0.0, ['ktl0', 'ktl1'])
            S.memset('dve', S_f[:, :, :], 0.0, ['S_f0', 'S_f1'])
            S.memset('dve', S_b[:, :, :], 0.0, ['S_b0', 'S_b1'])
            S.memset('dve', ksum_f[:, :, :], 0.0, ['ksum_f'])
            S.memset('dve', ksum_xf, 0.0, ['ksum_x'])

            for i_ in (1, 4, 3, 6, 5, 7):
                first_norm[i_]()
            for g in range(NG):
                hs0 = (g % 2) * 512
                hs = slice(hs0, hs0 + 512)
                hkeys = ['hTa%d_%d' % (g % 2, t) for t in range(4)]
                gs = slice(g * 512, (g + 1) * 512)
                S.checkpoint('A_norm')
                ns = norm_stages(g + 1) if g + 1 < NG else []
                for k in range(8):
                    S.matmul(pb[5][0:16, :], wglr[:, k, :], hT[:, k, hs], k == 0, k == 7, hkeys + ['cstb'], ['b5'])
                S.copy('act', glrT[0:16, :], pb[5][0:16, :], ['b5'], ['glrT'])
                w, wk = wget()
                for cb in range(4):
                    bi = 1 + cb % 2
                    proj_fm(w, wk, hkeys, hs, cb, bi)
                    S.act(QTx[0:64, 2 * cb, :], pb[bi][0:64, :], AF.Copy, ['b%d' % bi], ['QTq'], scale=0.125)
                    S.ts('dve', QTx[0:64, 2 * cb + 1, :], pb[bi][64:128, :], 0.125, ALU.mult, ['b%d' % bi], ['QTq'])
                for t in range(4):
                    bi = 5 + t % 2
                    S.matmul(pb[bi][:, 0:256], glrT[0:17, t * 128:(t + 1) * 128], w_aug[0:17, :], True, True,
                             ['glrT', 'cstb'], ['b%d' % bi])
                    S.act(etmp[:, 0, :], pb[bi][:, 0:256], AF.Exp, ['b%d' % bi], ['etmp0'], scale=-1.0)
                    S.act(sp_tok[:, t, :], etmp[:, 0, :], AF.Ln, ['etmp0'], ['sp_tok'], bias=1.0)
                w, wk = wget()
                for cb in range(4):
                    bi = 1 + cb % 2
                    proj_fm(w, wk, hkeys, hs, cb, bi)
                    S.op('dve', lambda e, a=(ksum_f[:, cb, 2 * g:2 * g + 2],
                                             pb[bi][:, :].rearrange("p (b k) -> p b k", k=256)):
                         e.reduce_sum(a[0], a[1], AXL.X), ['b%d' % bi], ['ksum_f'])
                    S.copy('act', KTx[0:64, 2 * cb, gs], pb[bi][0:64, :], ['b%d' % bi], ['KTx'])
                    S.copy('dve', KTx[0:64, 2 * cb + 1, gs], pb[bi][64:128, :], ['b%d' % bi], ['KTx'])
                S.ts('dve', ksum_x4[0:64, :, 0, :], ksum_f[0:64, :, :], 1.0 / 256, ALU.mult, ['ksum_f'], ['ksum_x'])
                S.ts('dve', ksum_x4[0:64, :, 1, :], ksum_f[64:128, :, :], 1.0 / 256, ALU.mult, ['ksum_f'], ['ksum_x'])
                ms = mask_stages(g)
                drain(ms, 1)
                drain(ns, 1)
                w, wk = wget()
                for t in range(4):
                    bi = 1 + t % 2
                    proj_tm(w, wk, hkeys[t], hs0 + t * 128, bi, 0, 512)
                    pv4 = pb[bi][:, :].rearrange("p (j e d) -> p j e d", e=2, d=64)
                    S.copy('act', VA[:, g * 4 + t, :, 0:64], pv4[:, :, 0, :], ['b%d' % bi], ['VA'])
                    S.copy('dve', VA[:, g * 4 + t, :, 128:192], pv4[:, :, 1, :], ['b%d' % bi], ['VA'])
                drain(ms, 2)
                drain(ns, 2)
                w, wk = wget()
                for cb in range(4):
                    bi = 1 + cb % 2
                    proj_fm(w, wk, hkeys, hs, cb, bi)
                    if cb < 2:
                        S.act(QGT[:, cb, :], pb[bi][:, :], AF.Copy, ['b%d' % bi], ['QGT'], scale=0.125)
                    else:
                        S.copy('dve', KGT[:, cb - 2, :], pb[bi][:, :], ['b%d' % bi], ['KGT'])
                for t in range(4):
                    bi = 1 + t % 2
                    proj_tm(w, wk, hkeys[t], hs0 + t * 128, bi, 256, 512)
                    evac_copy(KGtok[:, t, :], pb[bi][:, 0:256], ['b%d' % bi], ['KGtok'])
                drain(ms, 2)
                drain(ns, 2)
                w, wk = wget()
                for t in range(4):
                    bi = 1 + t % 2
                    proj_tm(w, wk, hkeys[t], hs0 + t * 128, bi, 0, 512)
                    evac_copy(VGtok[:, t, :], pb[bi][:, :], ['b%d' % bi], ['VGtok'])
                drain(ms, 2)
                drain(ns, 2)
                w, wk = wget()
                for cb in range(4):
                    bi = 1 + cb % 2
                    proj_fm(w, wk, hkeys, hs, cb, bi)
                    S.act(silu_rg[:, cb, :], pb[bi][:, :], AF.Silu, ['b%d' % bi], ['silu_rg'])
                drain(ms, len(ms))
                drain(ns, len(ns))
                S.checkpoint('A_proj')
                S.checkpoint('B_mask')
                deferred = gla_stages(g)
                nkt = 4 * g + 4
                iters = [(h, kt) for h in range(8) for kt in range(nkt)]
                SB = [3, 4, 7]
                OB = [5, 6]

                def emit_qk(idx):
                    h, kt = iters[idx]
                    qlo = max(0, kt - 4 * g) * 128
                    bi = SB[idx % 3]
                    S.matmul(pb[bi][:, qlo:512], KTx[:, h, kt * 128:(kt + 1) * 128], QTx[:, h, qlo:512],
                             True, True, ['KTx', 'QTq', 'QTm', 'cstb'], ['b%d' % bi])

                def emit_exp(idx):
                    h, kt = iters[idx]
                    i0 = kt - 4 * g
                    bi = SB[idx % 3]
                    bk = 'b%d' % bi
                    bank = pb[bi]
                    ps_ = idx % 4
                    ptk = 'PT%d' % ps_
                    qlo = max(0, i0) * 128
                    if i0 >= -1:
                        na = max(0, i0) * 128
                        nb_ = min(i0 + 2, 4) * 128
                        off = 128 if i0 == -1 else 0
                        n = nb_ - na
                        S.tt('dve', bank[:, na:nb_], bank[:, na:nb_], t5b[:, h, off:off + n], ALU.add,
                             [bk, 'cst2'], [bk])
                    S.act(PT[:, ps_, qlo:512], bank[:, qlo:512], AF.Exp, [bk], [ptk])

                def emit_pv(idx):
                    h, kt = iters[idx]
                    j, e_ = h // 2, h % 2
                    qlo = max(0, kt - 4 * g) * 128
                    ob = OB[h % 2]
                    obk = 'b%d' % ob
                    ps_ = idx % 4
                    S.matmul(pb[ob][:, qlo:512], VA[:, kt, j, e_ * 64:e_ * 64 + 128], PT[:, ps_, qlo:512],
                             kt == 0, kt == nkt - 1, ['VA', 'PT%d' % ps_], [obk])
                    if kt == nkt - 1:
                        sl = 0
                        lo, hi = (64, 128) if e_ == 0 else (0, 64)
                        oo, oh = (0, 64) if e_ == 0 else (64, 128)
                        tmpf = tmpS[:, :, :].rearrange("p a b -> p (a b)")
                        recip2(rl_sb[oo:oh, sl, :], pb[ob][lo:hi, :], tmpf[lo:hi, :], [obk], 'rl%d' % sl, 'rscrA')
                        S.tt('dve', mixT[oo:oh, j, gs], pb[ob][oo:oh, :], rl_sb[oo:oh, sl, :], ALU.mult,
                             [obk, 'rl%d' % sl], ['mix%d' % j])

                nit = len(iters)
                n_def = len(deferred)
                emit_qk(0)
                if nit > 1:
                    emit_qk(1)
                for idx in range(nit):
                    if idx + 2 < nit:
                        emit_qk(idx + 2)
                    emit_exp(idx)
                    emit_pv(idx)
                    want = min(n_def, -(-((idx + 1) * n_def) // max(1, (3 * nit) // 5)))
                    drain(deferred, want - (n_def - len(deferred)))
                drain(deferred, len(deferred))
                S.checkpoint('B_attn')
            S.checkpoint('B_gla')
            if debug:
                S.dma('sp', 'dbg', dbg_d, mixT[:].rearrange("p a b -> p (a b)"), ['mix%d' % k for k in range(8)], ['dbg_d'])

            S.barrier()
            AR.reset()
            kxT = AR.bf16(8 * 256).rearrange("p (a b) -> p a b", b=256)
            vx = AR.bf16(2 * 1024).rearrange("p (a b) -> p a b", b=1024)
            qxT = AR.bf16(2 * GC).rearrange("p (a b) -> p a b", b=GC)
            big = AR.bf16(8 * GC).rearrange("p (a b) -> p a b", b=GC)
            PTx = AR.bf16(2 * 512).rearrange("p (a b) -> p a b", b=512)
            rtmp = AR.bf16(2 * 512).rearrange("p (a b) -> p a b", b=512)
            x1 = AR.f32(TC * D).rearrange("p (a b) -> p a b", b=D)
            rlx = AR.f32(512)
            rscr = AR.f32(512)
            gfin = AR.f32(D)
            S.dma('sp', 'gfin', gfin, gfin_d, (), ['gfin'])

            mkeys = ['hT0', 'hT1']
            for mt in range(2):
                xs = mt
                S.dma('sp', 'xt%d' % xs, xt[:, xs, :], mem_d[s, mt * 128:(mt + 1) * 128, :], (), ['xt%d' % xs])
                norm_to_hT(xt[:, xs, :], 'xt%d' % xs, gmem, mt * 128, mkeys[mt])
            for c in range(4):
                w, wk = wget()
                if c < 2:
                    for cb in range(4):
                        bi = 1 + cb % 2
                        for k in range(8):
                            S.matmul(pb[bi][:, 0:256], w[:, k, cb * 128:(cb + 1) * 128], hT[:, k, 0:256], k == 0, k == 7,
                                     mkeys + [wk], ['b%d' % bi])
                        evac_copy(kxT[:, c * 4 + cb, :], pb[bi][:, 0:256], ['b%d' % bi], ['kxT'])
                else:
                    for mt in range(2):
                        bi = 1 + mt % 2
                        for k in range(8):
                            S.matmul(pb[bi][:, :], hT[:, k, mt * 128:(mt + 1) * 128], w[:, k, :], k == 0, k == 7,
                                     [mkeys[mt], wk], ['b%d' % bi])
                        evac_copy(vx[:, mt, (c - 2) * 512:(c - 1) * 512], pb[bi][:, :], ['b%d' % bi], ['vx'])

            S.checkpoint('C_mem')
            for cg in range(NCG):
                tok0 = cg * GC
                hk = ['hT%d' % t for t in range(TC)]
                mixk = ['mix%d' % k for k in range(8)]
                for t in range(TC):
                    S.dma('sp', 'x1_%d' % t, x1[:, t, :], x_d[s, tok0 + t * 128:tok0 + (t + 1) * 128, :], (), ['x1_%d' % t])
                wA = wget()
                wB = wget(ahead=1)
                pend = None
                for t in range(TC):
                    for c, (w, wk) in enumerate((wA, wB)):
                        bi = 3 + (2 * t + c) % 4
                        for k in range(8):
                            S.matmul(pb[bi][:, :], mixT[:, k, tok0 + t * 128:tok0 + (t + 1) * 128], w[:, k, :],
                                     k == 0, k == 7, mixk + [wk], ['b%d' % bi])
                        S.tt('dve', x1[:, t, c * 512:(c + 1) * 512], pb[bi][:, :], x1[:, t, c * 512:(c + 1) * 512],
                             ALU.add, ['b%d' % bi, 'x1_%d' % t], ['x1_%d' % t])
                    r_ = norm_part1(x1[:, t, :], 'x1_%d' % t)
                    if pend is not None:
                        norm_part2(*pend)
                    pend = (r_, gxat, t * 128, hk[t])
                norm_part2(*pend)
                S.checkpoint('C_out')
                def xq_proj(xh, w, wk):
                    for c2 in range(2):
                        cb = (xh % 2) * 2 + c2
                        for hf in range(GC // 512):
                            bi = 1 + hf % 2
                            for k in range(8):
                                S.matmul(pb[bi][:, :], w[:, k, cb * 128:(cb + 1) * 128], hT[:, k, hf * 512:(hf + 1) * 512],
                                         k == 0, k == 7, hk + [wk], ['b%d' % bi])
                            S.act(qxT[:, c2, hf * 512:(hf + 1) * 512], pb[bi][:, :], AF.Copy, ['b%d' % bi], ['qxT'],
                                  scale=1.0 / 16)

                NHF = GC // 512
                STB = [(5, 6), (3, 4)]
                w, wk = wget()
                xq_proj(0, w, wk)
                for xh in range(4):
                    for hf in range(NHF):
                        hs = slice(hf * 512, (hf + 1) * 512)
                        for mt in range(2):
                            bi = STB[hf % 2][mt]
                            for c2 in range(2):
                                S.matmul(pb[bi][:, :], kxT[:, 2 * xh + c2, mt * 128:(mt + 1) * 128], qxT[:, c2, hs],
                                         c2 == 0, c2 == 1, ['kxT', 'qxT'], ['b%d' % bi])
                    for hf in range(NHF):
                        hs = slice(hf * 512, (hf + 1) * 512)
                        for mt in range(2):
                            bi = STB[hf % 2][mt]
                            S.act(PTx[:, mt, :], pb[bi][:, :], AF.Exp, ['b%d' % bi], ['PTx%d' % mt])
                        if hf == 0 and xh + 1 < 4:
                            if (xh + 1) % 2 == 0:
                                w, wk = wget()
                            xq_proj(xh + 1, w, wk)
                        for mt in range(2):
                            S.matmul(pb[7][:, :], ones_b[:], PTx[:, mt, :], mt == 0, mt == 1, ['ones_b', 'PTx%d' % mt], ['b7'])
                        recip2(rlx, pb[7][:, :], rscr, ['b7'], 'rlx', 'rscr')
                        for c2 in range(2):
                            bi = 1 + c2
                            for mt in range(2):
                                S.matmul(pb[bi][:, :], vx[:, mt, (2 * xh + c2) * 128:(2 * xh + c2 + 1) * 128], PTx[:, mt, :],
                                         mt == 0, mt == 1, ['vx', 'PTx%d' % mt], ['b%d' % bi])
                            S.tt('dve', big[:, 2 * xh + c2, hs], pb[bi][:, :], rlx, ALU.mult, ['b%d' % bi, 'rlx'], ['big'])
                wA = wget()
                wB = wget(ahead=1)
                pend = None
                for t in range(TC):
                    for c, (w, wk) in enumerate((wA, wB)):
                        bi = 3 + (2 * t + c) % 4
                        for k in range(8):
                            S.matmul(pb[bi][:, :], big[:, k, t * 128:(t + 1) * 128], w[:, k, :], k == 0, k == 7,
                                     ['big', wk], ['b%d' % bi])
                        S.tt('dve', x1[:, t, c * 512:(c + 1) * 512], pb[bi][:, :], x1[:, t, c * 512:(c + 1) * 512],
                             ALU.add, ['b%d' % bi, 'x1_%d' % t], ['x1_%d' % t])
                    r_ = norm_part1(x1[:, t, :], 'x1_%d' % t)
                    if pend is not None:
                        norm_part2(*pend)
                    pend = (r_, gmlp, t * 128, hk[t])
                norm_part2(*pend)
                S.checkpoint('C_xattn')
                for q in range(4):
                    for c in range(2):
                        w, wk = wget()
                        for cb in range(4):
                            for hf in range(GC // 512):
                                bi = 1 + rot('acc', 2)
                                for k in range(8):
                                    S.matmul(pb[bi][:, :], w[:, k, cb * 128:(cb + 1) * 128], hT[:, k, hf * 512:(hf + 1) * 512],
                                             k == 0, k == 7, hk + [wk], ['b%d' % bi])
                                rsl = rot('ts', 2)
                                S.act(rtmp[:, rsl, :], pb[bi][:, :], AF.Relu, ['b%d' % bi], ['rtmp%d' % rsl])
                                S.tt('dve', big[:, c * 4 + cb, hf * 512:(hf + 1) * 512], rtmp[:, rsl, :], rtmp[:, rsl, :],
                                     ALU.mult, ['rtmp%d' % rsl], ['big'])
                    for c in range(2):
                        w, wk = wget()
                        for t in range(TC):
                            bi = 3 + t % 4
                            for k in range(8):
                                S.matmul(pb[bi][:, :], big[:, k, t * 128:(t + 1) * 128], w[:, k, :], k == 0, k == 7,
                                         ['big', wk], ['b%d' % bi])
                            S.tt('dve', x1[:, t, c * 512:(c + 1) * 512], pb[bi][:, :], x1[:, t, c * 512:(c + 1) * 512],
                                 ALU.add, ['b%d' % bi, 'x1_%d' % t], ['x1_%d' % t])
                S.checkpoint('C_mlp')
                for t in range(TC):
                    rs, rk = rstd_of(x1[:, t, :], 'x1_%d' % t, D)
                    xs = rot('xs', 2)
                    S.stt(xt[:, xs, :], x1[:, t, :], rs, gfin, ALU.mult, ALU.mult, ['x1_%d' % t, rk, 'gfin'], ['xt%d' % xs])
                    S.dma('sp', 'out%d' % xs, out_d[s, tok0 + t * 128:tok0 + (t + 1) * 128, :], xt[:, xs, :],
                          ['xt%d' % xs], ['out_d%d' % xs])
            S.barrier()

        final_keys = ['out_d0', 'out_d1'] + (['dbg_d'] if debug else [])
        S.wait_all('sp', final_keys)
        S.barrier()
        assert S.stopped or wstate['i'] == len(wplan), (wstate['i'], len(wplan))
        with nc.allow_non_contiguous_dma(reason="strided weight / constant loads"):
            S.emit()
    return nc


def _rel_bucket_np(d):
    d = np.maximum(d, 0)
    large = 16 + (np.log(np.maximum(d, 1).astype(np.float32) / np.float32(16)) / np.float32(math.log(128 / 16))
                  * np.float32(16)).astype(np.int32)
    large = np.minimum(large, 31)
    return np.where(d < 16, d, large)


def _consts(T=2048):
    cst = np.zeros((128, 1024), np.float32)
    i = np.arange(128)
    cst[:, 0:128] = np.eye(128, dtype=np.float32)
    cst[:, 128:256] = np.where(i[:, None] <= i[None, :], -1.0 / 16, 0.0)
    cst[:, 256:384] = np.where(i[:, None] > i[None, :], -1.0 / 16, 0.0)
    cst[:, 384:512] = np.where(i[:, None] <= i[None, :], 1.0, 0.0)
    for B in range(8):
        row = np.where(np.arange(8) < B, 0.0, np.where(np.arange(8) == B, 64.0, -64.0)).astype(np.float32)
        cst[:, 512 + B * 64:512 + (B + 1) * 64] = np.tile(row, 8)[None, :]
    blk = np.zeros((9, T), np.float32)
    kpos = np.arange(T)
    for b in range(8):
        blk[b, :] = (kpos // 256 == b)
    blk[8, :] = 1.0
    return cst, blk


def _host_inputs(rp_table, norm_mix, g_norm, norm_xattn, norm_mem, norm_mlp, norm_final, T=2048):
    cst, blk = _consts(T)
    vecs = np.zeros((128, 48), np.float32)
    vecs[:, 0:8] = np.asarray(norm_mix, np.float32).reshape(8, 128).T
    vecs[:, 8:16] = np.asarray(norm_xattn, np.float32).reshape(8, 128).T
    vecs[:, 16:24] = np.asarray(norm_mlp, np.float32).reshape(8, 128).T
    vecs[:, 24:32] = np.asarray(norm_mem, np.float32).reshape(8, 128).T
    vecs[:, 32] = np.asarray(g_norm, np.float32).reshape(128)
    rp = np.asarray(rp_table, np.float32)
    vecs[:, 33:41] = rp[31][None, :]
    i = np.arange(128)[:, None]
    jj = np.arange(256)[None, :]
    dist = jj - i
    idx = _rel_bucket_np(dist)
    t5 = rp[idx]
    t5 = np.where((dist >= 0)[:, :, None], t5, np.float32(-30000.0)).astype(np.float32)
    t5 = np.ascontiguousarray(t5.transpose(0, 2, 1)).reshape(128, 2048)
    gfin = np.ascontiguousarray(np.broadcast_to(np.asarray(norm_final, np.float32).reshape(1, D), (128, D)))
    return cst, blk, vecs, t5, gfin


_NC_CACHE = {}


def kernel(x, mem, rp_table, norm_mix, w_in, w_gate_up, b_gate, g_norm, w_out, norm_xattn, norm_mem, w_xq, w_xkv,
           w_xo, norm_mlp, w_up, w_down, norm_final):
    x = np.asarray(x, np.float32)
    mem = np.asarray(mem, np.float32)
    Bt, T, _ = x.shape
    nseq = Bt // NCORES
    cst, blk, vecs, t5, gfin = _host_inputs(rp_table, norm_mix, g_norm, norm_xattn, norm_mem, norm_mlp, norm_final, T)
    key = (nseq, T)
    if key not in _NC_CACHE:
        _NC_CACHE[key] = build(nseq, T)
    nc = _NC_CACHE[key]
    f = lambda a: np.ascontiguousarray(np.asarray(a, np.float32))
    shared = {
        "w_in": f(w_in[0]), "w_gate_up": f(w_gate_up[0]), "b_gate": f(b_gate[0]).reshape(1, 256),
        "w_out": f(w_out[0]), "w_xq": f(w_xq[0]), "w_xkv": f(w_xkv[0]), "w_xo": f(w_xo[0]),
        "w_up": f(w_up[0]), "w_down": f(w_down[0]),
        "cst": cst, "blkind": blk, "vecs": vecs, "t5b": t5, "gfin": gfin,
    }
    in_maps = []
    for c in range(NCORES):
        m = dict(shared)
        m["x"] = np.ascontiguousarray(x[c * nseq:(c + 1) * nseq])
        m["mem"] = np.ascontiguousarray(mem[c * nseq:(c + 1) * nseq])
        in_maps.append(m)
    res = run_bass_kernel_spmd(nc, in_maps, core_ids=list(range(NCORES)))
    out = np.concatenate([np.asarray(r["out"], np.float32) for r in res.results], axis=0)
    return out
```
